# Optimizing a Trainium2 kernel written in Bass

```python
import jax, jax.numpy as jnp
from jax import lax
import numpy as np

D_MODEL = 1024
BATCH = 4
SEQ = 4096
DEPTH = 1

CHUNK = 64
PLE_DIM = 256
MIX_WIDTH = D_MODEL
S5_WIDTH = MIX_WIDTH // 2
RWKV_WIDTH = MIX_WIDTH - S5_WIDTH
S5_GROUP = 16
S5_GROUPS = S5_WIDTH // S5_GROUP
S5_STATE = 64
RWKV_HEAD = 64
RWKV_HEADS = RWKV_WIDTH // RWKV_HEAD
DECAY_LORA = 64
AAA_LORA = 64
GATE_LORA = 128
SHIFT_COLS = 3 * RWKV_WIDTH + DECAY_LORA + AAA_LORA + GATE_LORA
IN_COLS = S5_WIDTH + SHIFT_COLS
FFN_HIDDEN = ((8 * D_MODEL + 3 * 256 - 1) // (3 * 256)) * 256
RMS_EPS = 1e-6
GN_EPS = 64e-5
L2_EPS = 1e-12

kernel_name = "hymba_s5_rwkv7_streaming_block"


def rms_norm(x, g):
    xf = x.astype(jnp.float32)
    y = xf * lax.rsqrt(jnp.mean(xf * xf, axis=-1, keepdims=True) + RMS_EPS)
    return (y * g.astype(jnp.float32)).astype(x.dtype)


def _linear_binop(e1, e2):
    a1, b1 = e1
    a2, b2 = e2
    return a2 * a1, a2 * b1 + b2


def s5_mixer(u, lam_re, lam_im, log_step, b_re, b_im, c_re, c_im, d_skip, glu_w, glu_b):
    f32 = jnp.float32
    bsz, seq, _ = u.shape
    n_chunks = seq // CHUNK
    lam = lax.complex(lam_re.astype(f32), lam_im.astype(f32))
    lam_dt = lam * jnp.exp(log_step.astype(f32))[:, None]
    lam_bar = jnp.exp(lam_dt)
    b_mat = lax.complex(b_re.astype(f32), b_im.astype(f32))
    b_bar = ((lam_bar - 1.0) / lam)[..., None] * b_mat
    c_mat = lax.complex(c_re.astype(f32), c_im.astype(f32))
    d_g = d_skip.astype(f32).reshape(S5_GROUPS, S5_GROUP)
    lam_pow = jnp.exp(lam_dt[None] * jnp.arange(1, CHUNK + 1, dtype=f32)[:, None, None])
    ug = u.astype(f32).reshape(bsz, n_chunks, CHUNK, S5_GROUPS, S5_GROUP)
    ug = jnp.moveaxis(ug, 1, 0)

    def chunk_step(carry, u_c):
        bu = jnp.einsum('gpc,btgc->btgp', b_bar, u_c.astype(jnp.complex64))
        a = jnp.broadcast_to(lam_bar, bu.shape)
        _, local = lax.associative_scan(_linear_binop, (a, bu), axis=1)
        states = local + lam_pow[None] * carry[:, None]
        y = jnp.einsum('gcp,btgp->btgc', c_mat, states).real + d_g * u_c
        return states[:, -1], y

    carry0 = jnp.zeros((bsz, S5_GROUPS, S5_STATE), jnp.complex64)
    _, ys = lax.scan(chunk_step, carry0, ug)
    y = jnp.moveaxis(ys, 0, 1).reshape(bsz, seq, S5_WIDTH)
    z = jax.nn.gelu(y)
    return z * jax.nn.sigmoid(z @ glu_w.astype(f32) + glu_b.astype(f32))


def wkv7_scan(r, w, k, v, a, b):
    bsz, seq, nh, n = r.shape
    n_chunks = seq // CHUNK

    def to_chunks(t):
        return t.reshape(bsz, n_chunks, CHUNK, nh, n).transpose(1, 2, 0, 3, 4)

    def step(s, inp):
        r_t, w_t, k_t, v_t, a_t, b_t = inp
        sa = jnp.einsum('bhvk,bhk->bhv', s, a_t)
        s = s * w_t[:, :, None, :] + sa[..., None] * b_t[:, :, None, :] + v_t[..., None] * k_t[:, :, None, :]
        return s, jnp.einsum('bhvk,bhk->bhv', s, r_t)

    def chunk_step(s, inp_c):
        return lax.scan(step, s, inp_c)

    s0 = jnp.zeros((bsz, nh, n, n), jnp.float32)
    _, ys = lax.scan(chunk_step, s0, tuple(to_chunks(t) for t in (r, w, k, v, a, b)))
    return ys.transpose(2, 0, 1, 3, 4).reshape(bsz, seq, nh, n)


def rwkv7_mixer(z, shift_mu, w0, w2, a0, a2, g2, k_k, k_a, r_k, ln_w, ln_b):
    f32 = jnp.float32
    bsz, seq, _ = z.shape
    z = z.astype(f32)
    prev = jnp.pad(z[:, :-1], ((0, 0), (1, 0), (0, 0)))
    zs = z + (prev - z) * shift_mu.astype(f32)
    rw = RWKV_WIDTH
    splits = [rw, 2 * rw, 3 * rw, 3 * rw + DECAY_LORA, 3 * rw + DECAY_LORA + AAA_LORA]
    r, k, v, wl, al, gl = jnp.split(zs, splits, axis=-1)
    w = -jax.nn.softplus(-(w0.astype(f32) + jnp.tanh(wl) @ w2.astype(f32))) - 0.5
    decay = jnp.exp(-jnp.exp(w))
    a = jax.nn.sigmoid(a0.astype(f32) + al @ a2.astype(f32))
    g = jax.nn.sigmoid(gl) @ g2.astype(f32)

    def heads(t):
        return t.reshape(bsz, seq, RWKV_HEADS, RWKV_HEAD)

    kk = heads(k * k_k.astype(f32))
    kk = kk / jnp.maximum(jnp.sqrt(jnp.sum(kk * kk, axis=-1, keepdims=True)), L2_EPS)
    k = k * (1.0 + (a - 1.0) * k_a.astype(f32))
    r_h, k_h, v_h, a_h = heads(r), heads(k), heads(v), heads(a)
    y = wkv7_scan(r_h, heads(decay), k_h, v_h, -kk, kk * a_h)
    mu = jnp.mean(y, axis=-1, keepdims=True)
    yc = y - mu
    yn = yc * lax.rsqrt(jnp.mean(yc * yc, axis=-1, keepdims=True) + GN_EPS)
    yn = yn.reshape(bsz, seq, rw) * ln_w.astype(f32) + ln_b.astype(f32)
    bonus = jnp.sum(r_h * k_h * r_k.astype(f32), axis=-1, keepdims=True) * v_h
    return (yn + bonus.reshape(bsz, seq, rw)) * g


def setup_inputs(seed: int = 0) -> dict:
    key = jax.random.key(seed)
    ks = jax.random.split(key, 40)
    f32 = jnp.float32
    nrm = lambda k, s, sc: jax.random.normal(k, s, f32) * sc
    D, L = DEPTH, D_MODEL
    n_idx = jnp.arange(RWKV_WIDTH, dtype=f32) / (RWKV_WIDTH - 1)
    w0_base = -7.0 + 5.0 * n_idx ** 0.85 + 0.5
    lam_im_base = jnp.pi * jnp.arange(S5_STATE, dtype=f32)
    return {
        "x": nrm(ks[0], (BATCH, SEQ, D_MODEL), 1.0),
        "p": nrm(ks[1], (DEPTH, BATCH, SEQ, PLE_DIM), 1.0),
        "norm_mix": 1.0 + nrm(ks[2], (D, L), 0.02),
        "w_in": nrm(ks[3], (D, L, IN_COLS), L ** -0.5),
        "s5_lam_re": -0.5 + nrm(ks[4], (D, S5_GROUPS, S5_STATE), 0.01),
        "s5_lam_im": lam_im_base + nrm(ks[5], (D, S5_GROUPS, S5_STATE), 0.01),
        "s5_log_step": jax.random.uniform(ks[6], (D, S5_GROUPS), f32, np.log(1e-3), np.log(1e-1)),
        "s5_b_re": nrm(ks[7], (D, S5_GROUPS, S5_STATE, S5_GROUP), (2 * S5_GROUP) ** -0.5),
        "s5_b_im": nrm(ks[8], (D, S5_GROUPS, S5_STATE, S5_GROUP), (2 * S5_GROUP) ** -0.5),
        "s5_c_re": nrm(ks[9], (D, S5_GROUPS, S5_GROUP, S5_STATE), S5_STATE ** -0.5),
        "s5_c_im": nrm(ks[10], (D, S5_GROUPS, S5_GROUP, S5_STATE), S5_STATE ** -0.5),
        "s5_d": nrm(ks[11], (D, S5_WIDTH), 1.0),
        "s5_glu_w": nrm(ks[12], (D, S5_WIDTH, S5_WIDTH), S5_WIDTH ** -0.5),
        "s5_glu_b": nrm(ks[13], (D, S5_WIDTH), 0.01),
        "rw_shift_mu": jax.random.uniform(ks[14], (D, SHIFT_COLS), f32, 0.1, 0.9),
        "rw_w0": w0_base + nrm(ks[15], (D, RWKV_WIDTH), 0.1),
        "rw_w2": nrm(ks[16], (D, DECAY_LORA, RWKV_WIDTH), 0.1),
        "rw_a0": nrm(ks[17], (D, RWKV_WIDTH), 0.1),
        "rw_a2": nrm(ks[18], (D, AAA_LORA, RWKV_WIDTH), 0.1),
        "rw_g2": nrm(ks[19], (D, GATE_LORA, RWKV_WIDTH), GATE_LORA ** -0.5),
        "rw_k_k": 0.85 + nrm(ks[20], (D, RWKV_WIDTH), 0.02),
        "rw_k_a": 1.0 + nrm(ks[21], (D, RWKV_WIDTH), 0.02),
        "rw_r_k": -0.04 + nrm(ks[22], (D, RWKV_HEADS, RWKV_HEAD), 0.02),
        "rw_ln_w": 1.0 + nrm(ks[23], (D, RWKV_WIDTH), 0.02),
        "rw_ln_b": nrm(ks[24], (D, RWKV_WIDTH), 0.01),
        "w_out": nrm(ks[25], (D, MIX_WIDTH, L), MIX_WIDTH ** -0.5),
        "norm_ffn": 1.0 + nrm(ks[26], (D, L), 0.02),
        "ffn_w1": nrm(ks[27], (D, L, FFN_HIDDEN), L ** -0.5),
        "ffn_w3": nrm(ks[28], (D, L, FFN_HIDDEN), L ** -0.5),
        "ffn_w2": nrm(ks[29], (D, FFN_HIDDEN, L), FFN_HIDDEN ** -0.5),
        "norm_ple": 1.0 + nrm(ks[30], (D, L), 0.02),
        "ple_gate_w": nrm(ks[31], (D, L, L), L ** -0.5),
        "ple_up_w": nrm(ks[32], (D, PLE_DIM, L), PLE_DIM ** -0.5),
        "final_norm": 1.0 + nrm(ks[33], (L,), 0.02),
    }


def reference(x, p, norm_mix, w_in, s5_lam_re, s5_lam_im, s5_log_step, s5_b_re, s5_b_im,
              s5_c_re, s5_c_im, s5_d, s5_glu_w, s5_glu_b, rw_shift_mu, rw_w0, rw_w2, rw_a0,
              rw_a2, rw_g2, rw_k_k, rw_k_a, rw_r_k, rw_ln_w, rw_ln_b, w_out, norm_ffn,
              ffn_w1, ffn_w3, ffn_w2, norm_ple, ple_gate_w, ple_up_w, final_norm):
    h = x
    for i in range(DEPTH):
        xn = rms_norm(h, norm_mix[i])
        proj = xn @ w_in[i]
        s5_out = s5_mixer(proj[..., :S5_WIDTH], s5_lam_re[i], s5_lam_im[i], s5_log_step[i],
                          s5_b_re[i], s5_b_im[i], s5_c_re[i], s5_c_im[i], s5_d[i],
                          s5_glu_w[i], s5_glu_b[i])
        rw_out = rwkv7_mixer(proj[..., S5_WIDTH:], rw_shift_mu[i], rw_w0[i], rw_w2[i], rw_a0[i],
                             rw_a2[i], rw_g2[i], rw_k_k[i], rw_k_a[i], rw_r_k[i],
                             rw_ln_w[i], rw_ln_b[i])
        mixed = jnp.concatenate([s5_out, rw_out], axis=-1).astype(h.dtype) @ w_out[i]
        h = h + mixed
        hn = rms_norm(h, norm_ffn[i])
        h = h + (jax.nn.silu(hn @ ffn_w1[i]) * (hn @ ffn_w3[i])) @ ffn_w2[i]
        gate = jax.nn.sigmoid(rms_norm(h, norm_ple[i]) @ ple_gate_w[i])
        h = h + gate * (p[i] @ ple_up_w[i])
    return rms_norm(h, final_norm)
```

```python
import contextlib
import numpy as np
import concourse.bass as bass
import concourse.mybir as mybir
from concourse.bass_utils import run_bass_kernel_spmd

F32 = mybir.dt.float32
BF16 = mybir.dt.bfloat16
I32 = mybir.dt.int32
ALU = mybir.AluOpType
AF = mybir.ActivationFunctionType

D = 1024
NT = 4096
OWN = 2048
FF = 2816
NFC = 22
EPS = 1e-6
TWO_PI = 6.283185307179586
PI = 3.141592653589793


class Sched:
    ENG = ("pe", "dve", "act", "pool", "sp")

    def __init__(self, nc, es):
        self.nc = nc
        self.es = es
        self.ops = []
        self.lw = {}
        self.rd = {}
        self.count = {e: 0 for e in self.ENG}
        self.sem = {e: es.enter_context(nc.semaphore("s_" + e)) for e in ("pe", "dve", "act", "pool")}
        self.NSLOT = 8
        self.dsem = [es.enter_context(nc.semaphore("s_dma%d" % i)) for i in range(self.NSLOT)]
        self.ndma = 0

    region_fn = staticmethod(lambda col: col // 512)

    def key(self, ap):
        n = ap.name
        n = n() if callable(n) else n
        if n == "pALL":
            return ("psum", self.region_fn(ap.offset % 4096))
        return n

    def op(self, eng, fn, outs, ins, keys_out=None, keys_in=None):
        ko = [self.key(a) for a in outs] if keys_out is None else keys_out
        ki = [self.key(a) for a in ins] if keys_in is None else keys_in
        ko = list(ko) + [k for k in ki if isinstance(k, tuple) and k[0] == "psum" and k not in ko]
        ki = [k for k in ki if not (isinstance(k, tuple) and k[0] == "psum")]
        deps = set()
        for k in ki:
            if k in self.lw:
                deps.add(self.lw[k])
        for k in ko:
            if k in self.lw:
                deps.add(self.lw[k])
            deps.update(self.rd.get(k, ()))
        idx = len(self.ops)
        if eng == "sp":
            slot = self.ndma % self.NSLOT
            val = 16 * (self.ndma // self.NSLOT + 1)
            self.ndma += 1
            sig = (self.dsem[slot], val, True)
        else:
            self.count[eng] += 1
            sig = (self.sem[eng], self.count[eng], False)
        self.ops.append((eng, fn, deps, sig))
        for k in ki:
            self.rd.setdefault(k, []).append(idx)
        for k in ko:
            self.lw[k] = idx
            self.rd[k] = []
        return idx

    def emit(self):
        nc = self.nc
        start = getattr(self, "emitted", 0)
        prev_tot = getattr(self, "prev_tot", {})
        self.emitted = len(self.ops)
        tot_now = {}
        for (eng, fn, deps, sig) in self.ops:
            tot_now[id(sig[0])] = (sig[0], max(tot_now.get(id(sig[0]), (None, 0))[1], sig[1]))
        self.prev_tot = tot_now
        with nc.Block() as block:
            def run(ename):
                def body(e):
                    waited = {}
                    last = None
                    for kk, (s, v) in prev_tot.items():
                        e.wait_ge(s, v)
                        waited[kk] = v
                    for (eng, fn, deps, sig) in self.ops[start:]:
                        if eng != ename:
                            continue
                        need = {}
                        for d in deps:
                            deng, _, _, dsig = self.ops[d]
                            if deng == "pe" and ename == "pe":
                                continue
                            s, v, isd = dsig
                            kk = id(s)
                            if waited.get(kk, 0) >= v:
                                continue
                            if kk not in need or need[kk][1] < v:
                                need[kk] = (s, v)
                        if sig[2]:
                            s, v, _ = sig
                            if v > 16 and waited.get(id(s), 0) < v - 16:
                                if id(s) not in need or need[id(s)][1] < v - 16:
                                    need[id(s)] = (s, v - 16)
                        for kk, (s, v) in need.items():
                            e.wait_ge(s, v)
                            waited[kk] = v
                        ins = fn(e)
                        ins.then_inc(sig[0], 16 if sig[2] else 1)
                        last = sig
                    if ename == "sp":
                        tot = {}
                        for (eng, fn, deps, sig) in self.ops:
                            if eng == "sp":
                                tot[id(sig[0])] = (sig[0], max(tot.get(id(sig[0]), (None, 0))[1], sig[1]))
                        for kk, (s, v) in tot.items():
                            e.wait_ge(s, v)
                return body
            block.tensor(run("pe"))
            block.vector(run("dve"))
            block.scalar(run("act"))
            block.gpsimd(run("pool"))
            block.sync(run("sp"))


def build():
    nc = bass.Bass("TRN2", target_bir_lowering=False)
    es = contextlib.ExitStack()
    with es:
        S = Sched(nc, es)

        def din(name, shape, dt=F32):
            return nc.dram_tensor(name, list(shape), dt, kind="ExternalInput").ap()

        xT = din("xT", [D, NT])
        pT = din("pT", [256, OWN])
        outT = nc.dram_tensor("outT", [D, OWN], F32, kind="ExternalOutput").ap()
        w_in = din("w_in", [D, 2304])
        w_out = din("w_out", [D, D])
        ffn_w1 = din("ffn_w1", [D, FF])
        ffn_w3 = din("ffn_w3", [D, FF])
        ffn_w2 = din("ffn_w2", [FF, D])
        gate_w = din("gate_w", [D, D])
        up_w = din("up_w", [256, D])
        glu_w = din("glu_w", [512, 512])
        cols = din("cols", [128, 80])
        ident_d = din("ident", [128, 128])
        consts_d = din("consts", [128, 1024])
        w2aug_d = din("w2aug", [65, 512])
        a2_d = din("a2", [128, 512])
        g2_d = din("g2", [128, 512])
        c2_d = din("c2", [128, 1536])
        s5B_d = din("s5B", [128, 48 + 4 * 256])
        s5c_d = din("s5c", [128, 17 + 32 + 128])

        _n = [0]
        PES = [es]

        def sb(shape, dt=F32, name=None):
            _n[0] += 1
            return PES[0].enter_context(nc.sbuf_tensor(("sb_" + name) if name else ("anon%d" % _n[0]), list(shape), dt))

        def psum(shape, name=None):
            _n[0] += 1
            return es.enter_context(nc.psum_tensor(name or ("p%d" % _n[0]), list(shape), F32))

        def dma(out, in_):
            S.op("sp", lambda e: e.dma_start(out=out, in_=in_), [out], [in_])

        def mm(out, lhsT, rhs, start=True, stop=True):
            S.op("pe", lambda e: e.matmul(out, lhsT, rhs, start=start, stop=stop), [out], [lhsT, rhs])

        def tr(out, in_, ident):
            S.op("pe", lambda e: e.matmul(out, in_, ident, start=True, stop=True), [out], [in_, ident])

        def trT(out, in_, ident_):
            S.op("pe", lambda e: e.transpose(out, in_, ident_), [out], [in_, ident_])

        def act(out, in_, func, bias=None, scale=1.0, eng="act"):
            ins = [in_] + ([bias] if (bias is not None and not isinstance(bias, float)) else []) + \
                ([scale] if not isinstance(scale, float) else [])
            kw = {}
            if bias is not None:
                kw["bias"] = bias
            S.op("act", lambda e: e.activation(out, in_, func, scale=scale, **kw), [out], ins)

        def tt(out, a, b, op, eng="dve"):
            S.op(eng, lambda e: e.tensor_tensor(out, a, b, op), [out], [a, b])

        def ts(out, a, s1, s2, op0, op1=None, eng="dve"):
            ins = [a] + [s for s in (s1, s2) if s is not None and not isinstance(s, (float, int))]
            if op1 is None:
                S.op(eng, lambda e: e.tensor_scalar(out, a, s1, None, op0), [out], ins)
            else:
                S.op(eng, lambda e: e.tensor_scalar(out, a, s1, s2, op0, op1), [out], ins)

        def stt(out, a, s, b, op0, op1, eng="dve"):
            ins = [a, b] + ([s] if not isinstance(s, (float, int)) else [])
            S.op(eng, lambda e: e.scalar_tensor_tensor(out, a, s, b, op0, op1), [out], ins)

        def cp(out, in_, eng="dve"):
            S.op(eng, lambda e: e.tensor_copy(out, in_), [out], [in_])

        def ecp(i, out, in_):
            if i % 2:
                act(out, in_, AF.Copy)
            else:
                cp(out, in_)

        def acopy(out, in_):
            act(out, in_, AF.Copy)

        def memset(ap, v, eng="dve"):
            S.op(eng, lambda e: e.memset(ap, v), [ap], [])

        def recip(out, in_):
            S.op("dve", lambda e: e.reciprocal(out, in_), [out], [in_])

        ident = sb([128, 128])
        dma(ident[:], ident_d)
        cst = sb([128, 1024])
        dma(cst[:], consts_d)
        ones = cst[:, 0:128]
        blk1 = cst[:, 128:256]
        tri_i = cst[:, 256:384]
        tri_e = cst[:, 384:512]
        mask_g = cst[:, 512:832]
        col = sb([128, 80])
        dma(col[:], cols)
        C_NMIX, C_NFFN, C_NPLE, C_NFIN = 0, 8, 16, 24
        C_MU = 32
        C_KK, C_KA, C_RK, C_LNW, C_LNB, C_A0, C_GLUB = 46, 50, 54, 58, 62, 66, 70


        pALL = es.enter_context(nc.psum_tensor("pALL", [128, 4096], F32))
        pA, pB, pC, pLP, pG, pDT, pW, pX = [pALL[:, i * 512:(i + 1) * 512] for i in range(8)]

        concB = sb([128, 4, OWN], BF16, "concB")
        xt = sb([128, 8, 256], F32, "xt")
        sqb = [sb([128, 256], F32, "sq%d" % i) for i in range(2)]
        rstd = sb([128, 256], F32, "rstd")
        xT_v = xT.rearrange("(c p) t -> p c t", p=128)

        def rmsnorm_tile(src_tile, t0, n, gcol, dst, pcol=2560):
            pss = pALL[:, pcol:pcol + n]
            for c in range(8):
                act(sqb[c % 2][:, 0:n], src_tile[:, c, 0:n], AF.Square)
                mm(pss, ones, sqb[c % 2][:, 0:n], start=(c == 0), stop=(c == 7))
            ts(rstd[:, 0:n], pss, 1.0 / D, EPS, ALU.mult, ALU.add)
            act(rstd[:, 0:n], rstd[:, 0:n], AF.Sqrt)
            recip(rstd[:, 0:n], rstd[:, 0:n])
            for c in range(8):
                stt(dst[:, c, 0:n], src_tile[:, c, 0:n], col[:, gcol + c:gcol + c + 1], rstd[:, 0:n], ALU.mult, ALU.mult)

        stage_ref = [None]

        _ld = [0]

        def load_cast(dst, src_ap, ncols):
            st = stage_ref[0]
            step = (st[0] if isinstance(st, list) else st).shape[1]
            for n0 in range(0, ncols, step):
                n1 = min(ncols, n0 + step)
                k = _ld[0]; _ld[0] += 1
                stage = st[k % len(st)] if isinstance(st, list) else st
                dma(stage[:, 0:n1 - n0], src_ap[:, n0:n1])
                if isinstance(st, list) and k % 2 == 1:
                    act(dst[:, n0:n1], stage[:, 0:n1 - n0], AF.Copy)
                else:
                    cp(dst[:, n0:n1], stage[:, 0:n1 - n0], eng="pool")

        def load_w_bf16(dst, src_ap, ncols):
            kc = src_ap.shape[0] // 128
            v = src_ap.rearrange("(c p) n -> p c n", p=128)
            for c in range(kc):
                load_cast(dst[:, c, :], v[:, c, :], ncols)

        if PES[0] is not es:
            S.emit()
            PES[0].close()
        PES[0] = contextlib.ExitStack()
        CW = 0.6065306597126334
        win_rw = sb([128, 8, 1792], BF16, "win_rw")
        _phB = PES[0]
        PES[0] = contextlib.ExitStack()
        stage_ref[0] = [sb([128, 2048], F32, "stageBa"), sb([128, 2048], F32, "stageBb")]
        load_w_bf16(win_rw, w_in[:, 512:2304], 1792)
        S.emit()
        PES[0].close()
        PES[0] = _phB
        w2aug = sb([65, 512], F32, "w2aug"); dma(w2aug[:], w2aug_d)
        a2sb = sb([128, 512], F32, "a2sb"); dma(a2sb[:], a2_d)
        g2sb = sb([128, 512], F32, "g2sb"); dma(g2sb[:], g2_d)
        c2 = sb([128, 1024], F32, "c2masks"); dma(c2[:], c2_d[:, 0:1024])
        maskLE4 = c2[:, 0:512].rearrange("p (c n) -> p c n", n=128)
        maskN4 = c2[:, 512:768].rearrange("p (c n) -> p c n", n=64)
        istk4 = c2[:, 768:1024].rearrange("p (c n) -> p c n", n=64)
        xn = sb([128, 8, 256], BF16, "xn")
        z = [sb([128, 257], F32, "z%d" % m) for m in range(14)]
        tmpb = [sb([128, 256], F32, "tmpb%d" % i) for i in range(2)]
        tw = sb([65, 256], F32, "tw"); memset(tw[64:65, :], 1.0)
        sgt = [sb([128, 512], F32, "sgt%d" % i) for i in range(2)]
        sgl = sb([128, 256], F32, "sgl")
        t1s = sb([128, 256], F32, "t1s")
        R4 = range(4)
        P_ = [sb([128, 256], F32, "P_%d" % i) for i in R4]
        g_ = [sb([128, 256], F32, "g_%d" % i) for i in R4]
        ar = [sb([128, 4, 128], F32, "ar%d" % i) for i in R4]
        bT = [sb([128, 4, 64], F32, "bT%d" % i) for i in R4]
        kT = [sb([128, 4, 64], F32, "kT%d" % i) for i in R4]
        bonus = [sb([128, 256], F32, "bonus%d" % i) for i in R4]
        tok = [sb([128, 4, 192], F32, "tok%d" % i) for i in R4]
        gmA = [sb([128, 4, 128], F32, "gmA%d" % i) for i in R4]
        gmN = [sb([128, 4, 64], F32, "gmN%d" % i) for i in R4]
        gmB = [sb([128, 4, 128], F32, "gmB%d" % i) for i in R4]
        nzb = [[sb([128, 4, 128], F32, "nzb%d_%d" % (i, j)) for j in range(2)] for i in R4]
        inj = [sb([128, 4, 64], F32, "inj%d" % i) for i in R4]
        xtb = [[sb([128, 4, 64], F32, "xtb%d_%d" % (i, j)) for j in range(2)] for i in R4]
        Wsb = [sb([128, 64], F32, "Wsb%d" % i) for i in R4]
        Usb = [sb([128, 64], F32, "Usb%d" % i) for i in R4]
        yT = [sb([128, 256], F32, "yT%d" % i) for i in R4]
        ST = [sb([128, 64], F32, "ST%d" % h) for h in R4]
        for h in R4:
            memset(ST[h][:], 0.0)
        yc = sb([128, 256], F32, "yc")
        HS = (slice(0, 64), slice(64, 128))
        p_in = pALL[:, 2048:2304]
        p_sgt = pALL[:, 2560:3072]
        p_lp = pALL[:, 3072:3584]
        p_x0 = pALL[:, 3584:3840]; p_x1 = pALL[:, 3840:4096]

        def pset(hp):
            return pALL[:, hp * 512:(hp + 1) * 512].rearrange("p (c n) -> p c n", n=128)

        def acopy(out, in_):
            act(out, in_, AF.Copy)

        zc = sb([128, 14], F32, "zc"); memset(zc[:], 0.0)

        def rms_in(tj_):
            dma(xt[:], xT_v[:, :, tj_ * 256:tj_ * 256 + 256])
            rmsnorm_tile(xt, 0, 256, C_NMIX, xn)

        def inproj_chunk(m):
            cp(z[m][:, 0:1], zc[:, m:m + 1], eng="pool")
            p_in = pALL[:, 2048 + (m % 2) * 512:2048 + (m % 2) * 512 + 256]
            for c in range(8):
                mm(p_in, win_rw[:, c, m * 128:(m + 1) * 128], xn[:, c, :], start=(c == 0), stop=(c == 7))
            acopy(z[m][:, 1:257], p_in)
            tb_ = tmpb[m % 2]
            tt(tb_[:], z[m][:, 0:256], z[m][:, 1:257], ALU.subtract, eng="pool")
            cp(zc[:, m:m + 1], z[m][:, 256:257], eng="pool")
            stt(z[m][:, 1:257], tb_[:], col[:, C_MU + m:C_MU + m + 1], z[m][:, 1:257], ALU.mult, ALU.add)

        NTILE = NT // 256
        for tj in range(NTILE):
            t0 = tj * 256
            own = t0 >= NT - OWN
            o0 = t0 - (NT - OWN)
            if tj == 0:
                rms_in(0)
                for m in range(14):
                    inproj_chunk(m)
            zs = [zz[:, 1:257] for zz in z]
            act(tw[0:64, :], zs[12][0:64, :], AF.Tanh)
            for tb in range(2):
                mm(p_sgt, tw[:, tb * 128:(tb + 1) * 128], w2aug[:], True, True)
                act(sgt[tb][:], p_sgt, AF.Sigmoid)
            act(sgl[:], zs[13], AF.Sigmoid)
            fl = lambda t_: t_[:].rearrange("p c n -> p (c n)")
            Pe = [fl(gmA[hp])[:, 0:256] for hp in R4]; Pi = [fl(gmA[hp])[:, 256:512] for hp in R4]
            a_ = [fl(gmB[hp])[:, 0:256] for hp in R4]; kk = [fl(gmB[hp])[:, 256:512] for hp in R4]
            rn = [fl(nzb[hp][0])[:, 0:256] for hp in R4]; kp = [fl(nzb[hp][0])[:, 256:512] for hp in R4]
            t1 = [fl(tok[hp])[:, 0:256] for hp in R4]
            HC = [slice(hp * 128, (hp + 1) * 128) for hp in R4]
            plp = [pALL[:, hp * 512:(hp + 1) * 512] for hp in R4]
            px0 = [pALL[:, 2048 + hp * 512:2048 + hp * 512 + 256] for hp in R4]
            px1 = [pALL[:, 2048 + hp * 512 + 256:2048 + hp * 512 + 512] for hp in R4]
            v4 = lambda a: a.rearrange("p (c t) -> p c t", t=64)
            for hp in R4:
                for tb in range(2):
                    mm(plp[hp][:, tb * 128:(tb + 1) * 128], sgt[tb][:, HC[hp]], tri_i, True, True)
                    mm(plp[hp][:, 256 + tb * 128:256 + (tb + 1) * 128], sgt[tb][:, HC[hp]], tri_e, True, True)
            for hp in R4:
                mm(px0[hp], a2sb[64:128, HC[hp]], zs[12][64:128, :], True, True)
                mm(px1[hp], g2sb[:, HC[hp]], sgl[:], True, True)
            for hp in R4:
                ts(kk[hp], zs[4 + hp], col[:, C_KK + hp:C_KK + hp + 1], None, ALU.mult)
                tt(rn[hp], kk[hp], kk[hp], ALU.mult, eng="pool")
            for hp in R4:
                act(P_[hp][:], plp[hp][:, 0:256], AF.Exp, scale=-CW)
                act(Pi[hp], plp[hp][:, 0:256], AF.Exp, scale=CW)
                act(Pe[hp], plp[hp][:, 256:512], AF.Exp, scale=-CW)
            for hp in R4:
                act(a_[hp], px0[hp], AF.Sigmoid, bias=col[:, C_A0 + hp:C_A0 + hp + 1])
            for hp in R4:
                acopy(g_[hp][:], px1[hp])
            for hp in R4:
                mm(px0[hp], blk1, rn[hp], True, True)
            for hp in R4:
                ts(rn[hp], px0[hp], 1e-24, None, ALU.max)
            for hp in R4:
                act(rn[hp], rn[hp], AF.Sqrt)
            for hp in R4:
                recip(rn[hp], rn[hp])
                tt(kk[hp], kk[hp], rn[hp], ALU.mult)
                ts(t1[hp], a_[hp], -1.0, col[:, C_KA + hp:C_KA + hp + 1], ALU.add, ALU.mult)
                stt(kp[hp], t1[hp], 1.0, zs[4 + hp], ALU.add, ALU.mult)
            for hp in R4:
                tt(ar[hp][:, :, 64:128], v4(zs[hp]), v4(P_[hp][:]), ALU.mult, eng="pool")
            for hp in R4:
                stt(ar[hp][:, :, 0:64], v4(kk[hp]), -1.0, v4(Pe[hp]), ALU.mult, ALU.mult)
                tt(t1[hp], kk[hp], a_[hp], ALU.mult)
                tt(bT[hp][:].rearrange("p c t -> p (c t)"), t1[hp], Pi[hp], ALU.mult)
                tt(kT[hp][:].rearrange("p c t -> p (c t)"), kp[hp], Pi[hp], ALU.mult)
            if own:
                for hp in R4:
                    stt(t1[hp], zs[hp], col[:, C_RK + hp:C_RK + hp + 1], kp[hp], ALU.mult, ALU.mult)
                for hp in R4:
                    mm(px0[hp], blk1, t1[hp], True, True)
                for hp in R4:
                    tt(bonus[hp][:], px0[hp], zs[8 + hp], ALU.mult)

            def st_T1(hp):
                ps = pset(hp); v_ = zs[8 + hp]
                for c in range(4):
                    cs = slice(c * 64, (c + 1) * 64)
                    for h in range(2):
                        hs = HS[h]
                        tr(ps[hs, c, 0:64], v_[hs, cs], ident[hs, hs])
                        tr(ps[hs, c, 64:128], bT[hp][hs, c, :], ident[hs, hs])
                acopy(tok[hp][:, :, 0:128], ps)

            def st_T2(hp):
                ps = pset(hp)
                for c in range(4):
                    for h in range(2):
                        hs = HS[h]
                        tr(ps[hs, c, 0:64], kT[hp][hs, c, :], ident[hs, hs])
                        mm(ps[hs, c, 64:128], ar[hp][hs, c, 0:64], bT[hp][hs, c, :], True, True)
                acopy(tok[hp][:, :, 128:192], ps[:, :, 0:64])
                tt(gmN[hp][:], ps[:, :, 64:128], maskN4, ALU.mult)

            def st_G1(hp):
                ps = pset(hp)
                for c in range(4):
                    for h in range(2):
                        hs = HS[h]
                        mm(ps[hs, c, 0:128], bT[hp][hs, c, :], ar[hp][hs, c, :], True, True)
                tt(gmA[hp][:], ps, maskLE4, ALU.mult)
                tt(xtb[hp][0][:], gmA[hp][:, :, 0:64], istk4, ALU.add, eng="pool")

            def st_G2(hp):
                ps = pset(hp)
                for c in range(4):
                    for h in range(2):
                        hs = HS[h]
                        mm(ps[hs, c, 0:128], kT[hp][hs, c, :], ar[hp][hs, c, :], True, True)
                tt(gmB[hp][:], ps, maskLE4, ALU.mult)

            def mk_level(j):
                def sa(hp):
                    ps = pset(hp)
                    if j == 1:
                        Zc, Nc = gmA[hp][:, :, 0:64], gmN[hp][:]
                    else:
                        Nc, Zc = nzb[hp][j % 2][:, :, 0:64], nzb[hp][j % 2][:, :, 64:128]
                    for c in range(4):
                        for h in range(2):
                            hs = HS[h]
                            mm(ps[hs, c, 0:64], Zc[hs, c, :], Nc[hs, c, :], True, True)
                            if j < 5:
                                mm(ps[hs, c, 64:128], Nc[hs, c, :], Zc[hs, c, :], True, True)
                    tt(inj[hp][:], ps[:, :, 0:64], istk4, ALU.add)
                    if j < 5:
                        acopy(nzb[hp][(j + 1) % 2][:], ps)

                def sb_(hp):
                    ps = pset(hp)
                    XTc = xtb[hp][(j - 1) % 2]
                    for c in range(4):
                        for h in range(2):
                            hs = HS[h]
                            mm(ps[hs, c, 0:64], inj[hp][hs, c, :], XTc[hs, c, :], True, True)
                    acopy(xtb[hp][j % 2][:], ps[:, :, 0:64])
                return sa, sb_

            stages = [st_T1, st_T2, st_G1, st_G2]
            for j in range(1, 6):
                stages.extend(mk_level(j))
            for si, stg in enumerate(stages):
                for hp in R4:
                    stg(hp)
                if tj + 1 < NTILE:
                    if si == 0:
                        rms_in(tj + 1)
                    else:
                        inproj_chunk(si - 1)
            if tj + 1 < NTILE:
                inproj_chunk(13)
            for c in range(4):
                cs = slice(c * 64, (c + 1) * 64)
                for hp in R4:
                    pw = pALL[:, 2048 + hp * 512:2048 + hp * 512 + 64]
                    for h in range(2):
                        hs = HS[h]
                        mm(pw[hs], ar[hp][hs, c, 0:64], ST[hp][hs, :], True, False)
                        mm(pw[hs], gmB[hp][hs, c, 0:64], tok[hp][hs, c, 0:64], False, True)
                    cp(Wsb[hp][:], pw)
                for hp in R4:
                    pu = pALL[:, 2048 + hp * 512 + 64:2048 + hp * 512 + 128]
                    for h in range(2):
                        hs = HS[h]
                        mm(pu[hs], xtb[hp][1][hs, c, :], Wsb[hp][hs, :], True, True)
                    acopy(Usb[hp][:], pu)
                for hp in R4:
                    py = pALL[:, 2048 + hp * 512 + 128:2048 + hp * 512 + 192]
                    pst = pALL[:, 2048 + hp * 512 + 192:2048 + hp * 512 + 256]
                    Vt, Bt, Kt = tok[hp][:, c, 0:64], tok[hp][:, c, 64:128], tok[hp][:, c, 128:192]
                    for h in range(2):
                        hs = HS[h]
                        mm(pst[hs], ident[hs, hs], ST[hp][hs, :], True, False)
                        mm(pst[hs], Bt[hs], Usb[hp][hs, :], False, False)
                        mm(pst[hs], Kt[hs], Vt[hs], False, True)
                    if own:
                        for h in range(2):
                            hs = HS[h]
                            mm(py[hs], ST[hp][hs, :], ar[hp][hs, c, 64:128], True, False)
                            mm(py[hs], Usb[hp][hs, :], gmA[hp][hs, c, 64:128], False, False)
                            mm(py[hs], Vt[hs], gmB[hp][hs, c, 64:128], False, True)
                        acopy(yT[hp][:, cs], py)
                    ts(ST[hp][:], pst, P_[hp][:, c * 64 + 63:c * 64 + 64], None, ALU.mult)
            if own:
                for hp in R4:
                    mm(p_x0, blk1, yT[hp][:], True, True)
                    stt(yc[:], p_x0, -1.0 / 64, yT[hp][:], ALU.mult, ALU.add)
                    tt(t1s[:], yc[:], yc[:], ALU.mult, eng="pool")
                    mm(p_x1, blk1, t1s[:], True, True)
                    ts(t1s[:], p_x1, 1.0 / 64, 64e-5, ALU.mult, ALU.add)
                    act(t1s[:], t1s[:], AF.Sqrt)
                    recip(t1s[:], t1s[:])
                    tt(yc[:], yc[:], t1s[:], ALU.mult)
                    ts(yc[:], yc[:], col[:, C_LNW + hp:C_LNW + hp + 1], col[:, C_LNB + hp:C_LNB + hp + 1], ALU.mult, ALU.add)
                    tt(yc[:], yc[:], bonus[hp][:], ALU.add, eng="pool")
                    tt(concB[:, hp, o0:o0 + 256], yc[:], g_[hp][:], ALU.mult, eng="pool")

        if PES[0] is not es:
            S.emit()
            PES[0].close()
        PES[0] = contextlib.ExitStack()
        concA = es.enter_context(nc.sbuf_tensor("sb_concA", [128, 4, OWN], BF16))
        NPW = 17
        s5c = sb([128, 177], F32, "s5c"); dma(s5c[:], s5c_d)
        pr = sb([128, 16, NPW], F32, "pr"); pi_ = sb([128, 16, NPW], F32, "pi_")
        WCre = sb([128, 16, 128], BF16, "WCre"); WCim = sb([128, 16, 128], BF16, "WCim")
        WEre = sb([128, 32, 64], BF16, "WEre"); WEim = sb([128, 32, 64], BF16, "WEim"); Toep = sb([128, 32, 128], BF16, "Toep")
        identb = sb([128, 128], BF16, "identb"); cp(identb[:], ident[:])
        win_s5 = sb([128, 8, 512], BF16, "win_s5")
        glub = sb([128, 4, 512], BF16, "glub")
        _phaseA = PES[0]
        PES[0] = contextlib.ExitStack()
        stage_ref[0] = [sb([128, 1024], F32, "stageAa"), sb([128, 1024], F32, "stageAb")]
        load_w_bf16(win_s5, w_in[:, 0:512], 512)
        load_w_bf16(glub, glu_w, 512)
        s5B = sb([128, 1072], F32, "s5B"); dma(s5B[:], s5B_d)
        lre, lim, lst = s5B[:, 0:16], s5B[:, 16:32], s5B[:, 32:48]
        cre = s5B[:, 48:304].rearrange("p (g c) -> p g c", c=16)
        cim = s5B[:, 304:560].rearrange("p (g c) -> p g c", c=16)
        bre = s5B[:, 560:816].rearrange("p (g c) -> p g c", c=16)
        bim = s5B[:, 816:1072].rearrange("p (g c) -> p g c", c=16)
        mtab = s5c[:, 0:17]; dcol = s5c[:, 17:49]; maskT = s5c[:, 49:177]
        dtt = sb([128, 16], F32, "dtt"); are = sb([128, 16], F32, "are"); aim = sb([128, 16], F32, "aim")
        act(dtt[:], lst, AF.Exp)
        tt(are[:], lre, dtt[:], ALU.mult)
        tt(aim[:], lim, dtt[:], ALU.mult)
        IDX0 = lambda m: (m + 7) if m != 64 else 16
        m1 = sb([128, 16], F32, "m1"); rr = sb([128, 16], F32, "rr"); acc_ = sb([128, 16], F32, "acc_")
        sn = sb([128, 16], F32, "sn"); cs_ = sb([128, 16], F32, "cs_"); mg = sb([128, 16], F32, "mg")
        act(mg[:], are[:], AF.Exp)

        def sin_small(dst, shift):
            ts(rr[:], aim[:], shift, None, ALU.add)
            memset(acc_[:], 0.0)
            for k in range(1, 7):
                ts(m1[:], rr[:], (2 * k - 1) * PI, -TWO_PI, ALU.is_gt, ALU.mult)
                tt(acc_[:], acc_[:], m1[:], ALU.add)
            tt(rr[:], rr[:], acc_[:], ALU.add)
            act(dst, rr[:], AF.Sin)

        sin_small(sn[:], 0.0)
        sin_small(cs_[:], PI / 2)
        P1r, P1i = pr[:, :, IDX0(1)], pi_[:, :, IDX0(1)]
        tt(P1r, cs_[:], mg[:], ALU.mult)
        tt(P1i, sn[:], mg[:], ALU.mult)
        memset(pr[:, :, IDX0(0)], 1.0); memset(pi_[:, :, IDX0(0)], 0.0)
        cm1 = sb([128, 16], F32, "cm1"); cm2 = sb([128, 16], F32, "cm2")

        def cmul_s(dr, di, ar_, ai_, br_, bi_):
            tt(cm1[:], ar_, br_, ALU.mult); tt(cm2[:], ai_, bi_, ALU.mult); tt(dr, cm1[:], cm2[:], ALU.subtract)
            tt(cm1[:], ar_, bi_, ALU.mult); tt(cm2[:], ai_, br_, ALU.mult); tt(di, cm1[:], cm2[:], ALU.add)

        for m in range(2, 9):
            cmul_s(pr[:, :, IDX0(m)], pi_[:, :, IDX0(m)], pr[:, :, IDX0(m - 1)], pi_[:, :, IDX0(m - 1)], P1r, P1i)
        ivr = sb([128, 16], F32, "ivr"); ivi = sb([128, 16], F32, "ivi")
        tt(cm1[:], mg[:], mg[:], ALU.mult)
        recip(cm1[:], cm1[:])
        tt(ivr[:], P1r, cm1[:], ALU.mult)
        stt(ivi[:], P1i, -1.0, cm1[:], ALU.mult, ALU.mult)
        cp(pr[:, :, IDX0(-1)], ivr[:]); cp(pi_[:, :, IDX0(-1)], ivi[:])
        for m in range(2, 8):
            cmul_s(pr[:, :, IDX0(-m)], pi_[:, :, IDX0(-m)], pr[:, :, IDX0(-(m - 1))], pi_[:, :, IDX0(-(m - 1))], ivr[:], ivi[:])
        s16r = sb([128, 16], F32, "s16r"); s16i = sb([128, 16], F32, "s16i")
        s32r = sb([128, 16], F32, "s32r"); s32i = sb([128, 16], F32, "s32i")
        cmul_s(s16r[:], s16i[:], pr[:, :, IDX0(8)], pi_[:, :, IDX0(8)], pr[:, :, IDX0(8)], pi_[:, :, IDX0(8)])
        cmul_s(s32r[:], s32i[:], s16r[:], s16i[:], s16r[:], s16i[:])
        cmul_s(pr[:, :, IDX0(64)], pi_[:, :, IDX0(64)], s32r[:], s32i[:], s32r[:], s32i[:])
        IDX = lambda m: (m + 7) if m != 64 else 16
        qn = sb([128, 16], F32, "qn"); qre = sb([128, 16], F32, "qre"); qim = sb([128, 16], F32, "qim")
        den = sb([128, 16], F32, "den"); lb1 = sb([128, 16], F32, "lb1"); q2 = sb([128, 16], F32, "q2")
        ts(lb1[:], pr[:, :, IDX(1)], -1.0, None, ALU.add)
        tt(den[:], lre, lre, ALU.mult); tt(q2[:], lim, lim, ALU.mult); tt(den[:], den[:], q2[:], ALU.add)
        recip(den[:], den[:])
        tt(qre[:], lb1[:], lre, ALU.mult); tt(q2[:], pi_[:, :, IDX(1)], lim, ALU.mult); tt(qre[:], qre[:], q2[:], ALU.add)
        tt(qre[:], qre[:], den[:], ALU.mult)
        tt(qim[:], pi_[:, :, IDX(1)], lre, ALU.mult); tt(q2[:], lb1[:], lim, ALU.mult); tt(qim[:], qim[:], q2[:], ALU.subtract)
        tt(qim[:], qim[:], den[:], ALU.mult)
        bbr = sb([128, 16, 16], F32, "bbr"); bbi = sb([128, 16, 16], F32, "bbi"); tq = sb([128, 16, 16], F32, "tq")
        bq = lambda a: a.unsqueeze(2).to_broadcast([128, 16, 16])
        tt(bbr[:], bre, bq(qre[:]), ALU.mult); tt(tq[:], bim, bq(qim[:]), ALU.mult); tt(bbr[:], bbr[:], tq[:], ALU.subtract)
        tt(bbi[:], bim, bq(qre[:]), ALU.mult); tt(tq[:], bre, bq(qim[:]), ALU.mult); tt(bbi[:], bbi[:], tq[:], ALU.add)
        WCre_f = sb([128, 16, 128], F32, "WCre_f"); WCim_f = sb([128, 16, 128], F32, "WCim_f")
        QCre = sb([128, 16, 128], F32, "QCre"); QCim = sb([128, 16, 128], F32, "QCim")
        PBre = sb([128, 16, 128], F32, "PBre"); PBimn = sb([128, 16, 128], F32, "PBimn")
        P7re = sb([128, 16, 128], F32, "P7re"); P7im = sb([128, 16, 128], F32, "P7im")

        prN = sb([128, 16, 8], F32, "prN"); piN = sb([128, 16, 8], F32, "piN")
        pr7 = sb([128, 16, 8], F32, "pr7"); pi7 = sb([128, 16, 8], F32, "pi7")
        for j in range(8):
            cp(prN[:, :, j], pr[:, :, IDX(-j)], eng="pool"); cp(piN[:, :, j], pi_[:, :, IDX(-j)], eng="pool")
            cp(pr7[:, :, j], pr[:, :, IDX(7 - j)], eng="pool"); cp(pi7[:, :, j], pi_[:, :, IDX(7 - j)], eng="pool")
        ctmp = sb([128, 16, 128], F32, "ctmp")

        def cmulv(Tre, Tim, sre, sim, lr8, li8, neg_im=False):
            v = lambda a: a.rearrange("p g (j c) -> p g j c", c=16)
            bs = lambda a: a.unsqueeze(2).to_broadcast([128, 16, 8, 16])
            bl = lambda a: a.unsqueeze(3).to_broadcast([128, 16, 8, 16])
            tt(v(Tre[:]), bs(sre), bl(lr8), ALU.mult)
            tt(v(ctmp[:]), bs(sim), bl(li8), ALU.mult)
            tt(Tre[:], Tre[:], ctmp[:], ALU.subtract)
            tt(v(Tim[:]), bs(sim), bl(lr8), ALU.mult)
            tt(v(ctmp[:]), bs(sre), bl(li8), ALU.mult)
            tt(Tim[:], Tim[:], ctmp[:], ALU.add)
            if neg_im:
                ts(Tim[:], Tim[:], -1.0, None, ALU.mult)

        cmulv(WCre_f, WCim_f, cre, cim, pr[:, :, IDX(1):IDX(8) + 1], pi_[:, :, IDX(1):IDX(8) + 1], neg_im=True)
        cp(WCre[:], WCre_f[:], eng="pool"); cp(WCim[:], WCim_f[:], eng="pool")
        cmulv(QCre, QCim, cre, cim, pr[:, :, IDX(0):IDX(7) + 1], pi_[:, :, IDX(0):IDX(7) + 1])
        cmulv(PBre, PBimn, bbr[:], bbi[:], prN[:], piN[:], neg_im=True)
        cmulv(P7re, P7im, bbr[:], bbi[:], pr7[:], pi7[:])
        BKp = [pA, pB, pC, pLP, pG, pDT, pW, pX]
        for gb in range(8):
            bkE = BKp[gb % 4]; bkT = BKp[4 + gb % 4]
            for gi in range(4):
                g = gb * 4 + gi
                hs = HS[g % 2]; pg = g // 2
                tr(bkE[:, gi * 64:(gi + 1) * 64], P7re[hs, pg, :], ident[hs, hs])
                tr(bkE[:, 256 + gi * 64:256 + (gi + 1) * 64], P7im[hs, pg, :], ident[hs, hs])
                mm(bkT[:, gi * 128:(gi + 1) * 128], PBre[hs, pg, :], QCre[hs, pg, :], True, False)
                mm(bkT[:, gi * 128:(gi + 1) * 128], PBimn[hs, pg, :], QCim[hs, pg, :], False, True)
            gs4 = slice(gb * 4, gb * 4 + 4)
            acopy(WEre[:, gs4, :], bkE[:, 0:256].rearrange("p (g n) -> p g n", n=64))
            acopy(WEim[:, gs4, :], bkE[:, 256:512].rearrange("p (g n) -> p g n", n=64))
            tt(Toep[:, gs4, :], bkT.rearrange("p (g n) -> p g n", n=128), maskT.unsqueeze(1).to_broadcast([128, 4, 128]), ALU.mult)
        S.emit()
        PES[0].close()
        PES[0] = _phaseA
        L8r, L8i = pr[:, :, IDX(8)], pi_[:, :, IDX(8)]
        L64r, L64i = pr[:, :, IDX(64)], pi_[:, :, IDX(64)]
        L16r = sb([128, 16], F32, "L16r"); L16i = sb([128, 16], F32, "L16i")
        L32r = sb([128, 16], F32, "L32r"); L32i = sb([128, 16], F32, "L32i")
        lsq = sb([128, 16], F32, "lsq"); lsq2 = sb([128, 16], F32, "lsq2")
        for (sr_, si_, dr_, di_) in ((L8r, L8i, L16r[:], L16i[:]), (L16r[:], L16i[:], L32r[:], L32i[:])):
            tt(lsq[:], sr_, sr_, ALU.mult); tt(lsq2[:], si_, si_, ALU.mult); tt(dr_, lsq[:], lsq2[:], ALU.subtract)
            tt(lsq[:], sr_, si_, ALU.mult); ts(di_, lsq[:], 2.0, None, ALU.mult)
        Lpr = sb([128, 16, 8], F32, "Lpr"); Lpi = sb([128, 16, 8], F32, "Lpi")
        lq1 = sb([128, 16], F32, "lq1"); lq2 = sb([128, 16], F32, "lq2")
        cp(Lpr[:, :, 0], L64r); cp(Lpi[:, :, 0], L64i)
        for c in range(1, 8):
            tt(lq1[:], Lpr[:, :, c - 1], L64r, ALU.mult); tt(lq2[:], Lpi[:, :, c - 1], L64i, ALU.mult)
            tt(Lpr[:, :, c], lq1[:], lq2[:], ALU.subtract)
            tt(lq1[:], Lpr[:, :, c - 1], L64i, ALU.mult); tt(lq2[:], Lpi[:, :, c - 1], L64r, ALU.mult)
            tt(Lpi[:, :, c], lq1[:], lq2[:], ALU.add)

        xn5 = sb([128, 8, 512], BF16, "xn5")
        X8 = sb([128, 32, 8, 16], BF16, "X8")
        Ytok8 = sb([128, 8, 512], F32, "Ytok8")[:]
        U8 = sb([128, 32, 64], BF16, "U8")
        stbr = sb([128, 16, 64], BF16, "stbr"); stbi = sb([128, 16, 64], BF16, "stbi")
        du = sb([128, 8, 64], F32, "du")
        E8r = sb([128, 16, 64], F32, "E8r"); E8i = sb([128, 16, 64], F32, "E8i")
        str_ = sb([128, 16, 64], F32, "str_"); sti = sb([128, 16, 64], F32, "sti")
        accr = sb([128, 16, 8], F32, "accr"); acci = sb([128, 16, 8], F32, "acci")
        ca = sb([128, 16, 8], F32, "ca"); cb = sb([128, 16, 8], F32, "cb")
        cc = sb([128, 16, 8], F32, "cc"); cd = sb([128, 16, 8], F32, "cd")
        hsr = sb([128, 16, 8], F32, "hsr"); hsi = sb([128, 16, 8], F32, "hsi")
        cstr = sb([128, 16, 8], F32, "cstr"); csti = sb([128, 16, 8], F32, "csti")
        car = sb([128, 16], F32, "car"); cai = sb([128, 16], F32, "cai"); memset(car[:], 0.0); memset(cai[:], 0.0)
        c1 = sb([128, 16], F32, "c1"); c2 = sb([128, 16], F32, "c2")
        Yim = sb([128, 32, 64], F32, "Yim")
        y5 = Yim[:].rearrange("p g b -> p (g b)").rearrange("p (q t) -> p q t", t=512)
        g1 = sb([128, 512], F32, "g1"); g2t = sb([128, 512], F32, "g2t")
        zb = sb([128, 4, 512], BF16, "zb")
        zf = y5

        def bc(ap16, n):
            return ap16.unsqueeze(2).to_broadcast([128, 16, n])

        def cstep(dr, di, sr, si, lr, li, er, ei, n, scr=None):
            ca, cb, cc, cd = scr
            tt(ca[:, :, 0:n], sr, bc(lr, n), ALU.mult)
            tt(cb[:, :, 0:n], si, bc(li, n), ALU.mult)
            tt(cc[:, :, 0:n], sr, bc(li, n), ALU.mult)
            tt(cd[:, :, 0:n], si, bc(lr, n), ALU.mult)
            tt(ca[:, :, 0:n], ca[:, :, 0:n], cb[:, :, 0:n], ALU.subtract)
            tt(cc[:, :, 0:n], cc[:, :, 0:n], cd[:, :, 0:n], ALU.add)
            tt(dr, ca[:, :, 0:n], er, ALU.add)
            tt(di, cc[:, :, 0:n], ei, ALU.add)

        BK = [pA, pB, pC, pLP, pG, pDT, pW, pX]
        U8s = [U8, sb([128, 32, 64], BF16, "U8b")]
        E8rs = [E8r, sb([128, 16, 64], F32, "E8rb")]; E8is = [E8i, sb([128, 16, 64], F32, "E8ib")]
        cstrs = [cstr, sb([128, 16, 8], F32, "cstrb")]; cstis = [csti, sb([128, 16, 8], F32, "cstib")]
        scrF = (ca, cb, cc, cd)
        scrB = tuple(sb([128, 16, 8], F32, "scrB%d" % i) for i in range(4))
        v8 = lambda t_: t_[:].rearrange("p g (c j) -> p g c j", j=8)
        vh = lambda t_: t_[:].rearrange("p g (ch j) -> p g ch j", j=4)
        a16r = sb([128, 16, 16], F32, "a16r"); a16i = sb([128, 16, 16], F32, "a16i")
        scr16 = tuple(sb([128, 16, 16], F32, "scr16_%d" % i) for i in range(4))
        scr16b = tuple(sb([128, 16, 16], F32, "scr16b_%d" % i) for i in range(4))
        aLor = [sb([128, 16, 8], F32, "aLor%d" % i) for i in range(2)]
        aLoi = [sb([128, 16, 8], F32, "aLoi%d" % i) for i in range(2)]

        def front(ti):
            t0 = ti * 512
            pp = ti % 2
            U8_, E8r_, E8i_ = U8s[pp], E8rs[pp], E8is[pp]
            E8rv, E8iv = v8(E8r_), v8(E8i_)
            st_ = []

            def f_rms(s):
                dma(xt[:], xT_v[:, :, t0 + s * 256:t0 + (s + 1) * 256])
                rmsnorm_tile(xt, 0, 256, C_NMIX, xn5[:, :, s * 256:(s + 1) * 256])

            def f_in(i0):
                bk = BK[i0 % 4]
                for c in range(8):
                    mm(bk[0:64, :], xn5[:, c, i0::8], win_s5[:, c, :], start=(c == 0), stop=(c == 7))
                ecp(i0, X8[0:64, :, i0, :], bk[0:64, :].rearrange("p (g c) -> p g c", c=16))

            def f_tr(gb):
                bk = BK[4 + gb]
                for gi in range(8):
                    g = gb * 8 + gi
                    tr(bk[:, gi * 64:(gi + 1) * 64], X8[0:64, g, :, :].rearrange("p a b -> p (a b)"), identb[0:64, 0:64])
                ecp(gb, U8_[:, gb * 8:(gb + 1) * 8, :], bk.rearrange("p (g b) -> p g b", b=64))

            def f_e8(hf):
                for g in range(hf * 16, hf * 16 + 16):
                    hs = HS[g % 2]; pg = g // 2
                    mm(pC[hs, (pg % 8) * 64:(pg % 8 + 1) * 64], WEre[:, g, :], U8_[:, g, :], True, True)
                    mm(pLP[hs, (pg % 8) * 64:(pg % 8 + 1) * 64], WEim[:, g, :], U8_[:, g, :], True, True)
                o = hf * 8
                cp(E8r_[:, o:o + 8, :], pC[:, :].rearrange("p (g b) -> p g b", b=64))
                acopy(E8i_[:, o:o + 8, :], pLP[:, :].rearrange("p (g b) -> p g b", b=64))

            E8rh, E8ih = vh(E8r_), vh(E8i_)

            def f_p1i():
                cp(a16r[:], E8rh[:, :, :, 0]); cp(a16i[:], E8ih[:, :, :, 0], eng="pool")

            def f_p1(j):
                if j < 4:
                    cstep(a16r[:], a16i[:], a16r[:], a16i[:], L8r, L8i, E8rh[:, :, :, j], E8ih[:, :, :, j], 16, scr=scr16)
                else:
                    lo_r = a16r[:].rearrange("p g (c h) -> p g c h", h=2)[:, :, :, 0]
                    lo_i = a16i[:].rearrange("p g (c h) -> p g c h", h=2)[:, :, :, 0]
                    hi_r = a16r[:].rearrange("p g (c h) -> p g c h", h=2)[:, :, :, 1]
                    hi_i = a16i[:].rearrange("p g (c h) -> p g c h", h=2)[:, :, :, 1]
                    cp(aLor[pp][:], lo_r); cp(aLoi[pp][:], lo_i, eng="pool")
                    cstep(accr[:], acci[:], lo_r, lo_i, L32r[:], L32i[:], hi_r, hi_i, 8, scr=scrF)

            hsb = [(accr, acci), (hsr, hsi)]

            def f_hs(k):
                d, pw_ = ((1, 0), (2, 1), (4, 3))[k]
                (Ar, Ai), (Br, Bi) = hsb[k % 2], hsb[(k + 1) % 2]
                cp(Br[:, :, 0:d], Ar[:, :, 0:d]); cp(Bi[:, :, 0:d], Ai[:, :, 0:d], eng="pool")
                cstep(Br[:, :, d:8], Bi[:, :, d:8], Ar[:, :, 0:8 - d], Ai[:, :, 0:8 - d], Lpr[:, :, pw_], Lpi[:, :, pw_],
                      Ar[:, :, d:8], Ai[:, :, d:8], 8 - d, scr=scrF)

            def f_carry():
                (Ar, Ai), (Br, Bi) = hsb[1], hsb[0]
                ca_, cb_, cc_, cd_ = scrF
                cbr = car[:].unsqueeze(2).to_broadcast([128, 16, 8]); cbi = cai[:].unsqueeze(2).to_broadcast([128, 16, 8])
                tt(ca_[:], cbr, Lpr[:], ALU.mult); tt(cb_[:], cbi, Lpi[:], ALU.mult)
                tt(cc_[:], cbr, Lpi[:], ALU.mult, eng="pool"); tt(cd_[:], cbi, Lpr[:], ALU.mult, eng="pool")
                tt(ca_[:], ca_[:], cb_[:], ALU.subtract); tt(cc_[:], cc_[:], cd_[:], ALU.add, eng="pool")
                tt(Br[:], ca_[:], Ar[:], ALU.add); tt(Bi[:], cc_[:], Ai[:], ALU.add, eng="pool")
                cr_, ci_ = cstrs[pp], cstis[pp]
                cp(cr_[:, :, 0], car[:]); cp(ci_[:, :, 0], cai[:], eng="pool")
                cp(cr_[:, :, 1:8], Br[:, :, 0:7]); cp(ci_[:, :, 1:8], Bi[:, :, 0:7], eng="pool")
                cp(car[:], Br[:, :, 7]); cp(cai[:], Bi[:, :, 7], eng="pool")

            st_ += [lambda s=s: f_rms(s) for s in range(2)]
            st_ += [lambda i0=i0: f_in(i0) for i0 in range(8)]
            st_ += [lambda gb=gb: f_tr(gb) for gb in range(4)]
            st_ += [lambda hf=hf: f_e8(hf) for hf in range(2)]
            st_ += [f_p1i]
            st_ += [lambda j=j: f_p1(j) for j in range(1, 5)]
            st_ += [lambda k=k: f_hs(k) for k in range(3)]
            st_ += [f_carry]
            return st_

        def back(ti):
            t0 = ti * 512
            o0 = t0 - (NT - OWN)
            pp = ti % 2
            U8_, E8r_, E8i_ = U8s[pp], E8rs[pp], E8is[pp]
            E8rv, E8iv = v8(E8r_), v8(E8i_)
            strv, stiv = v8(str_), v8(sti)
            st_ = []

            E8rh, E8ih = vh(E8r_), vh(E8i_)
            strh, stih = vh(str_), vh(sti)
            s5r = str_[:].rearrange("p g (c h j) -> p g c h j", h=2, j=4)
            s5i = sti[:].rearrange("p g (c h j) -> p g c h j", h=2, j=4)

            def b_init():
                cp(s5r[:, :, :, 0, 0], cstrs[pp][:]); cp(s5i[:, :, :, 0, 0], cstis[pp][:], eng="pool")
                cstep(s5r[:, :, :, 1, 0], s5i[:, :, :, 1, 0], cstrs[pp][:], cstis[pp][:], L32r[:], L32i[:],
                      aLor[pp][:], aLoi[pp][:], 8, scr=scrB)

            def b_p2(j):
                cstep(strh[:, :, :, j + 1], stih[:, :, :, j + 1], strh[:, :, :, j], stih[:, :, :, j], L8r, L8i,
                      E8rh[:, :, :, j], E8ih[:, :, :, j], 16, scr=scr16b)

            def b_stb():
                cp(stbr[:], str_[:]); cp(stbi[:], sti[:], eng="pool")

            def b_out(gb):
                bk = BK[gb]
                for gi in range(8):
                    g = gb * 8 + gi
                    hs = HS[g % 2]; pg = g // 2
                    mm(bk[:, gi * 64:(gi + 1) * 64], WCre[hs, pg, :], stbr[hs, pg, :], True, False)
                    mm(bk[:, gi * 64:(gi + 1) * 64], WCim[hs, pg, :], stbi[hs, pg, :], False, False)
                    mm(bk[:, gi * 64:(gi + 1) * 64], Toep[:, g, :], U8_[:, g, :], False, True)
                gs8 = slice(gb * 8, (gb + 1) * 8)
                tt(du[:], U8_[:, gs8, :], dcol[:, gs8].unsqueeze(2).to_broadcast([128, 8, 64]), ALU.mult, eng="pool")
                tt(Yim[:, gs8, :], du[:], bk.rearrange("p (g b) -> p g b", b=64), ALU.add)

            def b_a9(gb):
                bk = BK[gb % 4 + 4]
                for gi in range(4):
                    g = gb * 4 + gi
                    trT(bk[0:64, gi * 128:(gi + 1) * 128], Yim[:, g, :], ident[:])
                ecp(gb, Ytok8[0:64, :, gb * 64:(gb + 1) * 64].rearrange("p j (g c) -> p j g c", c=16),
                    bk[0:64, :].rearrange("p (g j c) -> p j g c", j=8, c=16))

            def b_a10(j0):
                bk = BK[j0 % 4]
                for q in range(4):
                    trT(bk[:, q * 64:(q + 1) * 64], Ytok8[0:64, j0, q * 128:(q + 1) * 128], ident[0:64, 0:64])
                ecp(j0, y5[:, :, j0::8], bk[:, 0:256].rearrange("p (q b) -> p q b", b=64))

            def b_gelu(q):
                tt(g1[:], y5[:, q, :], y5[:, q, :], ALU.mult)
                ts(g1[:], g1[:], 0.044715, 1.0, ALU.mult, ALU.add)
                tt(g1[:], g1[:], y5[:, q, :], ALU.mult)
                act(g1[:], g1[:], AF.Sigmoid, scale=1.5957691216057308)
                tt(y5[:, q, :], y5[:, q, :], g1[:], ALU.mult)
                cp(zb[:, q, :], zf[:, q, :], eng="pool")

            def b_glu(q):
                for kq in range(4):
                    mm(pX[:, :], glub[:, kq, q * 128:(q + 1) * 128], zb[:, kq, :], start=(kq == 0), stop=(kq == 3))
                act(g2t[:], pX[:, :], AF.Sigmoid, bias=col[:, C_GLUB + q:C_GLUB + q + 1])
                tt(concA[:, q, o0:o0 + 512], zf[:, q, :], g2t[:], ALU.mult)

            st_ += [b_init]
            st_ += [lambda j=j: b_p2(j) for j in range(3)]
            st_ += [b_stb]
            st_ += [lambda gb=gb: b_out(gb) for gb in range(4)]
            st_ += [lambda gb=gb: b_a9(gb) for gb in range(8)]
            st_ += [lambda j0=j0: b_a10(j0) for j0 in range(8)]
            st_ += [lambda q=q: b_gelu(q) for q in range(4)]
            st_ += [lambda q=q: b_glu(q) for q in range(4)]
            return st_

        NT5 = NT // 512
        NA = 14

        def zipped(l1, l2):
            for i_ in range(max(len(l1), len(l2))):
                if i_ < len(l1):
                    l1[i_]()
                if i_ < len(l2):
                    l2[i_]()

        NPRE = (NT - OWN) // 512
        fr = {0: front(0)}
        zipped(fr[0][:NA], [])
        for ti in range(NPRE):
            fr[ti + 1] = front(ti + 1)
            zipped(fr[ti][NA:], fr[ti + 1][:NA])
        zipped(fr[NPRE][NA:], [])
        for ti in range(NPRE, NT5):
            fs_ = front(ti + 1) if ti + 1 < NT5 else []
            zipped(back(ti), fs_)

        if PES[0] is not es:
            S.emit()
            PES[0].close()
        PES[0] = contextlib.ExitStack()
        h = es.enter_context(nc.sbuf_tensor("sb_h", [128, 8, OWN], F32))
        hn = es.enter_context(nc.sbuf_tensor("sb_hn", [128, 8, OWN], BF16))
        stage_ref[0] = [sb([128, 2048], F32, "stageC1a"), sb([128, 2048], F32, "stageC1b")]
        wo = sb([128, 8, 1024], BF16, "wo")
        load_w_bf16(wo, w_out, 1024)
        BK = [pA, pB, pC, pLP, pG, pDT, pW, pX]
        for tq_ in range(OWN // 512):
            dma(h[:, :, tq_ * 512:(tq_ + 1) * 512], xT_v[:, :, NT - OWN + tq_ * 512:NT - OWN + (tq_ + 1) * 512])
        def norm_sub(gcol, s, pcol=2560):
            sl = slice(s * 256, (s + 1) * 256)
            rmsnorm_tile(h[:, :, sl], 0, 256, gcol, hn[:, :, sl], pcol=pcol)

        for tq_ in range(OWN // 512):
            ts0 = slice(tq_ * 512, (tq_ + 1) * 512)
            for dc in range(8):
                bk = BK[dc % 4]
                for c in range(8):
                    mm(bk, wo[:, c, dc * 128:(dc + 1) * 128], (concA if c < 4 else concB)[:, c % 4, ts0], start=(c == 0), stop=(c == 7))
                tt(h[:, dc, ts0], h[:, dc, ts0], bk, ALU.add)
                if tq_ >= 1 and dc in (3, 7):
                    norm_sub(C_NFFN, 2 * (tq_ - 1) + (0 if dc == 3 else 1))
        for k_ in range(2):
            norm_sub(C_NFFN, OWN // 256 - 2 + k_)

        def norm_own(gcol):
            for s in range(OWN // 256):
                norm_sub(gcol, s)

        S.emit()
        PES[0].close()
        PES[0] = contextlib.ExitStack()
        GS = 2
        NG = NFC // GS
        stage_ref[0] = [sb([128, 2048], F32, "stageC2a"), sb([128, 2048], F32, "stageC2b")]
        w1g = [sb([128, 8, GS * 128], BF16, "w1g%d" % i) for i in range(2)]
        w3g = [sb([128, 8, GS * 128], BF16, "w3g%d" % i) for i in range(2)]
        w2g = [sb([128, GS, 1024], BF16, "w2g%d" % i) for i in range(2)]
        gT = [sb([128, GS, 512], BF16, "gT%d" % i) for i in range(2)]
        s1 = [sb([128, 512], F32, "s1_%d" % i) for i in range(2)]

        stg3 = stage_ref[0] + [sb([128, 2048], F32, "stageC2c")]
        _lg = [0]

        def load_group(gi):
            f0 = gi * GS
            fs = slice(f0 * 128, (f0 + GS) * 128)
            v1 = ffn_w1.rearrange("(c p) n -> p c n", p=128)[:, :, fs]
            v3 = ffn_w3.rearrange("(c p) n -> p c n", p=128)[:, :, fs]
            v2 = ffn_w2[fs, :].rearrange("(f p) n -> p f n", p=128)
            for (dst, srcv, shp) in ((w1g[gi % 2], v1, (8, GS * 128)), (w3g[gi % 2], v3, (8, GS * 128)), (w2g[gi % 2], v2, (GS, 1024))):
                st = stg3[_lg[0] % 3]; _lg[0] += 1
                sv_ = st[:].rearrange("p (a b) -> p a b", a=shp[0])
                dma(sv_, srcv)
                act(dst[:], sv_, AF.Copy)

        wg = concB[:].rearrange("p a (b c) -> p (a b) c", c=1024)
        cAf = concA[:].rearrange("p a n -> p (a n)")
        wu = cAf[:, 0:2048].rearrange("p (c n) -> p c n", n=1024)
        pb = cAf[:, 2048:2048 + 2 * OWN].rearrange("p (c n) -> p c n", n=OWN)
        pT_v = pT.rearrange("(c p) t -> p c t", p=128)
        load_group(0)
        k_ = [0]

        def ffn_A(gi, tq_):
            W1, W3 = w1g[gi % 2], w3g[gi % 2]
            ts0 = slice(tq_ * 512, (tq_ + 1) * 512)
            G = gT[tq_ % 2]
            for f in range(GS):
                bA, bB = (pA, pB) if (k_[0] % 2 == 0) else (pG, pDT)
                for c in range(8):
                    mm(bA, W1[:, c, f * 128:(f + 1) * 128], hn[:, c, ts0], start=(c == 0), stop=(c == 7))
                for c in range(8):
                    mm(bB, W3[:, c, f * 128:(f + 1) * 128], hn[:, c, ts0], start=(c == 0), stop=(c == 7))
                act(s1[k_[0] % 2][:], bA, AF.Silu)
                tt(G[:, f, :], s1[k_[0] % 2][:], bB, ALU.mult)
                k_[0] += 1

        def ffn_B(gi, tq_):
            W2 = w2g[gi % 2]
            ts0 = slice(tq_ * 512, (tq_ + 1) * 512)
            G = gT[tq_ % 2]
            for dc in range(8):
                bk = [pC, pLP, pW, pX][dc % 4]
                for f in range(GS):
                    mm(bk, W2[:, f, dc * 128:(dc + 1) * 128], G[:, f, :], start=(f == 0), stop=(f == GS - 1))
                tt(h[:, dc, ts0], h[:, dc, ts0], bk, ALU.add)

        units = [(gi, tq_) for gi in range(NG) for tq_ in range(OWN // 512)]
        for ui, (gi, tq_) in enumerate(units):
            ffn_A(gi, tq_)
            if ui > 0:
                ffn_B(*units[ui - 1])
            if tq_ == 0 and gi + 1 < NG:
                load_group(gi + 1)
            if ui == 6:
                stage_ref[0] = stg3
                load_w_bf16(wg, gate_w, 1024)
                load_w_bf16(wu, up_w, 1024)
                for c in range(2):
                    load_cast(pb[:, c, :], pT_v[:, c, :], OWN)
        ffn_B(*units[-1])
        S.emit()
        PES[0].close()
        PES[0] = contextlib.ExitStack()
        norm_own(C_NPLE)
        s1 = [sb([128, 512], F32, "s1p_%d" % i) for i in range(2)]
        outv = outT.rearrange("(c p) t -> p c t", p=128)
        obs = [sb([128, 8, 256], F32, "ob%d" % i) for i in range(2)]

        def final_sub(s):
            sl = slice(s * 256, (s + 1) * 256)
            rmsnorm_tile(h[:, :, sl], 0, 256, C_NFIN, obs[s % 2], pcol=1024)
            dma(outv[:, :, sl], obs[s % 2][:])

        for tq_ in range(OWN // 512):
            ts0 = slice(tq_ * 512, (tq_ + 1) * 512)
            for dc in range(8):
                bA, bB = (pA, pB) if (dc % 2 == 0) else (pG, pDT)
                for c in range(8):
                    mm(bA, wg[:, c, dc * 128:(dc + 1) * 128], hn[:, c, ts0], start=(c == 0), stop=(c == 7))
                for c in range(2):
                    mm(bB, wu[:, c, dc * 128:(dc + 1) * 128], pb[:, c, ts0], start=(c == 0), stop=(c == 1))
                act(s1[dc % 2][:], bA, AF.Sigmoid)
                tt(s1[dc % 2][:], s1[dc % 2][:], bB, ALU.mult)
                tt(h[:, dc, ts0], h[:, dc, ts0], s1[dc % 2][:], ALU.add, eng="pool")
                if tq_ >= 1 and dc in (3, 7):
                    final_sub(2 * (tq_ - 1) + (0 if dc == 3 else 1))
        for k_ in range(2):
            final_sub(OWN // 256 - 2 + k_)
        S.emit()
        PES[0].close()
    return nc


def _cols(a, n):
    return np.ascontiguousarray(np.asarray(a, np.float32).reshape(n, 128).T)


def kernel(**inp):
    f = lambda k: np.asarray(inp[k], np.float32)
    x = f("x"); p = f("p")[0]
    col = np.zeros((128, 80), np.float32)
    col[:, 0:8] = _cols(f("norm_mix")[0], 8); col[:, 8:16] = _cols(f("norm_ffn")[0], 8)
    col[:, 16:24] = _cols(f("norm_ple")[0], 8); col[:, 24:32] = _cols(f("final_norm"), 8)
    col[:, 32:46] = _cols(f("rw_shift_mu")[0], 14)
    col[:, 46:50] = _cols(f("rw_k_k")[0], 4); col[:, 50:54] = _cols(f("rw_k_a")[0], 4)
    col[:, 54:58] = _cols(f("rw_r_k")[0].reshape(512), 4); col[:, 58:62] = _cols(f("rw_ln_w")[0], 4)
    col[:, 62:66] = _cols(f("rw_ln_b")[0], 4); col[:, 66:70] = _cols(f("rw_a0")[0], 4)
    col[:, 70:74] = _cols(f("s5_glu_b")[0], 4)
    ident = np.eye(128, dtype=np.float32)
    cst = np.zeros((128, 1024), np.float32)
    cst[:, 0:128] = 1.0
    pi = np.arange(128)
    same = (pi[:, None] // 64) == (pi[None, :] // 64)
    cst[:, 128:256] = same
    cst[:, 256:384] = same & (pi[:, None] <= pi[None, :])
    cst[:, 384:512] = same & (pi[:, None] < pi[None, :])
    s = (pi % 64)[:, None]; t = np.arange(64)[None, :]
    lt = (s < t).astype(np.float32); le = (s <= t).astype(np.float32)
    cst[:, 512:576] = lt; cst[:, 576:640] = le; cst[:, 640:704] = lt; cst[:, 704:768] = le
    cst[:, 768:832] = (t < s).astype(np.float32)
    cst[:, 832:896] = (s == t).astype(np.float32)
    c2 = np.zeros((128, 1536), np.float32)
    nmask = (t < s).astype(np.float32)
    for c in range(4):
        c2[:, c * 128:c * 128 + 64] = lt; c2[:, c * 128 + 64:c * 128 + 128] = le
        c2[:, 512 + c * 64:512 + (c + 1) * 64] = nmask
        c2[:, 768 + c * 64:768 + (c + 1) * 64] = (s == t)
    w2aug = np.concatenate([f("rw_w2")[0], f("rw_w0")[0][None, :]], 0)
    a2 = np.zeros((128, 512), np.float32); a2[64:128] = f("rw_a2")[0]
    g2 = f("rw_g2")[0]
    def lb(a):
        return np.ascontiguousarray(a.reshape(16, 2, 64).transpose(1, 2, 0).reshape(128, 16))
    def lb3(a):
        return np.ascontiguousarray(a.reshape(16, 2, 64, 16).transpose(1, 2, 0, 3).reshape(128, 256))
    s5B = np.concatenate([
        lb(f("s5_lam_re")[0]), lb(f("s5_lam_im")[0]), lb(np.repeat(f("s5_log_step")[0][:, None], 64, 1)),
        lb3(f("s5_c_re")[0].transpose(0, 2, 1)), lb3(f("s5_c_im")[0].transpose(0, 2, 1)),
        lb3(f("s5_b_re")[0]), lb3(f("s5_b_im")[0])], 1).astype(np.float32)
    s5c = np.zeros((128, 177), np.float32)
    s5c[:, 0:17] = np.array(list(range(-7, 9)) + [64], np.float32)[None, :]
    s5c[:, 17:49] = np.tile(f("s5_d")[0].reshape(32, 16).T, (8, 1))
    s5c[:, 49:177] = ((pi[None, :] // 16) >= (pi[:, None] // 16)).astype(np.float32)
    common = {
        "w_in": f("w_in")[0], "w_out": f("w_out")[0], "ffn_w1": f("ffn_w1")[0], "ffn_w3": f("ffn_w3")[0],
        "ffn_w2": f("ffn_w2")[0], "gate_w": f("ple_gate_w")[0], "up_w": f("ple_up_w")[0], "glu_w": f("s5_glu_w")[0],
        "cols": col, "ident": ident, "consts": cst, "w2aug": w2aug, "a2": a2, "g2": g2, "c2": c2, "s5B": s5B, "s5c": s5c,
    }
    in_maps = []
    for core in range(8):
        b, hf = core // 2, core % 2
        xw = np.zeros((D, NT), np.float32)
        if hf == 0:
            xw[:, OWN:] = x[b, 0:OWN].T
        else:
            xw[:, :] = x[b].T
        m = dict(common)
        m["xT"] = xw
        m["pT"] = np.ascontiguousarray(p[b, hf * OWN:(hf + 1) * OWN].T)
        in_maps.append(m)
    nc = build()
    res = run_bass_kernel_spmd(nc, in_maps, core_ids=list(range(8)))
    out = np.zeros((4, 4096, D), np.float32)
    for core in range(8):
        b, hf = core // 2, core % 2
        out[b, hf * OWN:(hf + 1) * OWN] = res.results[core]["outT"].T
    return out
```

```python
import contextlib
import numpy as np
import concourse.bass as bass
import concourse.mybir as mybir
from concourse.bass_utils import run_bass_kernel_spmd

F32 = mybir.dt.float32
BF16 = mybir.dt.bfloat16
I32 = mybir.dt.int32
ALU = mybir.AluOpType
AF = mybir.ActivationFunctionType

D = 1024
NT = 4096
OWN = 2048
FF = 2816
NFC = 22
EPS = 1e-6
TWO_PI = 6.283185307179586
PI = 3.141592653589793


class Sched:
    ENG = ("pe", "dve", "act", "pool", "sp")

    def __init__(self, nc, es):
        self.nc = nc
        self.es = es
        self.ops = []
        self.lw = {}
        self.rd = {}
        self.count = {e: 0 for e in self.ENG}
        self.sem = {e: es.enter_context(nc.semaphore("s_" + e)) for e in ("pe", "dve", "act", "pool")}
        self.NSLOT = 8
        self.dsem = [es.enter_context(nc.semaphore("s_dma%d" % i)) for i in range(self.NSLOT)]
        self.ndma = 0

    region_fn = staticmethod(lambda col: col // 512)

    def key(self, ap):
        n = ap.name
        n = n() if callable(n) else n
        if n == "pALL":
            return ("psum", self.region_fn(ap.offset % 4096))
        return n

    def op(self, eng, fn, outs, ins, keys_out=None, keys_in=None):
        ko = [self.key(a) for a in outs] if keys_out is None else keys_out
        ki = [self.key(a) for a in ins] if keys_in is None else keys_in
        ko = list(ko) + [k for k in ki if isinstance(k, tuple) and k[0] == "psum" and k not in ko]
        ki = [k for k in ki if not (isinstance(k, tuple) and k[0] == "psum")]
        deps = set()
        for k in ki:
            if k in self.lw:
                deps.add(self.lw[k])
        for k in ko:
            if k in self.lw:
                deps.add(self.lw[k])
            deps.update(self.rd.get(k, ()))
        idx = len(self.ops)
        if eng == "sp":
            slot = self.ndma % self.NSLOT
            val = 16 * (self.ndma // self.NSLOT + 1)
            self.ndma += 1
            sig = (self.dsem[slot], val, True)
        else:
            self.count[eng] += 1
            sig = (self.sem[eng], self.count[eng], False)
        self.ops.append((eng, fn, deps, sig))
        for k in ki:
            self.rd.setdefault(k, []).append(idx)
        for k in ko:
            self.lw[k] = idx
            self.rd[k] = []
        return idx

    def emit(self):
        nc = self.nc
        start = getattr(self, "emitted", 0)
        prev_tot = getattr(self, "prev_tot", {})
        self.emitted = len(self.ops)
        tot_now = {}
        for (eng, fn, deps, sig) in self.ops:
            tot_now[id(sig[0])] = (sig[0], max(tot_now.get(id(sig[0]), (None, 0))[1], sig[1]))
        self.prev_tot = tot_now
        with nc.Block() as block:
            def run(ename):
                def body(e):
                    waited = {}
                    last = None
                    for kk, (s, v) in prev_tot.items():
                        e.wait_ge(s, v)
                        waited[kk] = v
                    for (eng, fn, deps, sig) in self.ops[start:]:
                        if eng != ename:
                            continue
                        need = {}
                        for d in deps:
                            deng, _, _, dsig = self.ops[d]
                            if deng == "pe" and ename == "pe":
                                continue
                            s, v, isd = dsig
                            kk = id(s)
                            if waited.get(kk, 0) >= v:
                                continue
                            if kk not in need or need[kk][1] < v:
                                need[kk] = (s, v)
                        if sig[2]:
                            s, v, _ = sig
                            if v > 16 and waited.get(id(s), 0) < v - 16:
                                if id(s) not in need or need[id(s)][1] < v - 16:
                                    need[id(s)] = (s, v - 16)
                        for kk, (s, v) in need.items():
                            e.wait_ge(s, v)
                            waited[kk] = v
                        ins = fn(e)
                        ins.then_inc(sig[0], 16 if sig[2] else 1)
                        last = sig
                    if ename == "sp":
                        tot = {}
                        for (eng, fn, deps, sig) in self.ops:
                            if eng == "sp":
                                tot[id(sig[0])] = (sig[0], max(tot.get(id(sig[0]), (None, 0))[1], sig[1]))
                        for kk, (s, v) in tot.items():
                            e.wait_ge(s, v)
                return body
            block.tensor(run("pe"))
            block.vector(run("dve"))
            block.scalar(run("act"))
            block.gpsimd(run("pool"))
            block.sync(run("sp"))


def build():
    nc = bass.Bass("TRN2", target_bir_lowering=False)
    es = contextlib.ExitStack()
    with es:
        S = Sched(nc, es)

        def din(name, shape, dt=F32):
            return nc.dram_tensor(name, list(shape), dt, kind="ExternalInput").ap()

        xT = din("xT", [D, NT])
        pT = din("pT", [256, OWN])
        outT = nc.dram_tensor("outT", [D, OWN], F32, kind="ExternalOutput").ap()
        w_in = din("w_in", [D, 2304])
        w_out = din("w_out", [D, D])
        ffn_w1 = din("ffn_w1", [D, FF])
        ffn_w3 = din("ffn_w3", [D, FF])
        ffn_w2 = din("ffn_w2", [FF, D])
        gate_w = din("gate_w", [D, D])
        up_w = din("up_w", [256, D])
        glu_w = din("glu_w", [512, 512])
        cols = din("cols", [128, 80])
        ident_d = din("ident", [128, 128])
        consts_d = din("consts", [128, 1024])
        w2aug_d = din("w2aug", [65, 512])
        a2_d = din("a2", [128, 512])
        g2_d = din("g2", [128, 512])
        c2_d = din("c2", [128, 1536])
        s5B_d = din("s5B", [128, 48 + 4 * 256])
        s5c_d = din("s5c", [128, 17 + 32 + 128])

        _n = [0]
        PES = [es]

        def sb(shape, dt=F32, name=None):
            _n[0] += 1
            return PES[0].enter_context(nc.sbuf_tensor(("sb_" + name) if name else ("anon%d" % _n[0]), list(shape), dt))

        def psum(shape, name=None):
            _n[0] += 1
            return es.enter_context(nc.psum_tensor(name or ("p%d" % _n[0]), list(shape), F32))

        def dma(out, in_):
            S.op("sp", lambda e: e.dma_start(out=out, in_=in_), [out], [in_])

        def mm(out, lhsT, rhs, start=True, stop=True):
            S.op("pe", lambda e: e.matmul(out, lhsT, rhs, start=start, stop=stop), [out], [lhsT, rhs])

        def tr(out, in_, ident):
            S.op("pe", lambda e: e.matmul(out, in_, ident, start=True, stop=True), [out], [in_, ident])

        def trT(out, in_, ident_):
            S.op("pe", lambda e: e.transpose(out, in_, ident_), [out], [in_, ident_])

        def act(out, in_, func, bias=None, scale=1.0, eng="act"):
            ins = [in_] + ([bias] if (bias is not None and not isinstance(bias, float)) else []) + \
                ([scale] if not isinstance(scale, float) else [])
            kw = {}
            if bias is not None:
                kw["bias"] = bias
            S.op("act", lambda e: e.activation(out, in_, func, scale=scale, **kw), [out], ins)

        def tt(out, a, b, op, eng="dve"):
            S.op(eng, lambda e: e.tensor_tensor(out, a, b, op), [out], [a, b])

        def ts(out, a, s1, s2, op0, op1=None, eng="dve"):
            ins = [a] + [s for s in (s1, s2) if s is not None and not isinstance(s, (float, int))]
            if op1 is None:
                S.op(eng, lambda e: e.tensor_scalar(out, a, s1, None, op0), [out], ins)
            else:
                S.op(eng, lambda e: e.tensor_scalar(out, a, s1, s2, op0, op1), [out], ins)

        def stt(out, a, s, b, op0, op1, eng="dve"):
            ins = [a, b] + ([s] if not isinstance(s, (float, int)) else [])
            S.op(eng, lambda e: e.scalar_tensor_tensor(out, a, s, b, op0, op1), [out], ins)

        def cp(out, in_, eng="dve"):
            S.op(eng, lambda e: e.tensor_copy(out, in_), [out], [in_])

        def ecp(i, out, in_):
            if i % 2:
                act(out, in_, AF.Copy)
            else:
                cp(out, in_)

        def acopy(out, in_):
            act(out, in_, AF.Copy)

        def memset(ap, v, eng="dve"):
            S.op(eng, lambda e: e.memset(ap, v), [ap], [])

        def recip(out, in_):
            S.op("dve", lambda e: e.reciprocal(out, in_), [out], [in_])

        ident = sb([128, 128])
        dma(ident[:], ident_d)
        cst = sb([128, 1024])
        dma(cst[:], consts_d)
        ones = cst[:, 0:128]
        blk1 = cst[:, 128:256]
        tri_i = cst[:, 256:384]
        tri_e = cst[:, 384:512]
        mask_g = cst[:, 512:832]
        col = sb([128, 80])
        dma(col[:], cols)
        C_NMIX, C_NFFN, C_NPLE, C_NFIN = 0, 8, 16, 24
        C_MU = 32
        C_KK, C_KA, C_RK, C_LNW, C_LNB, C_A0, C_GLUB = 46, 50, 54, 58, 62, 66, 70


        pALL = es.enter_context(nc.psum_tensor("pALL", [128, 4096], F32))
        pA, pB, pC, pLP, pG, pDT, pW, pX = [pALL[:, i * 512:(i + 1) * 512] for i in range(8)]

        concB = sb([128, 4, OWN], BF16, "concB")
        xt = sb([128, 8, 256], F32, "xt")
        sqb = [sb([128, 256], F32, "sq%d" % i) for i in range(2)]
        rstd = sb([128, 256], F32, "rstd")
        xT_v = xT.rearrange("(c p) t -> p c t", p=128)

        def rmsnorm_tile(src_tile, t0, n, gcol, dst):
            pss = pALL[:, 2560:2560 + n]
            for c in range(8):
                act(sqb[c % 2][:, 0:n], src_tile[:, c, 0:n], AF.Square)
                mm(pss, ones, sqb[c % 2][:, 0:n], start=(c == 0), stop=(c == 7))
            ts(rstd[:, 0:n], pss, 1.0 / D, EPS, ALU.mult, ALU.add)
            act(rstd[:, 0:n], rstd[:, 0:n], AF.Sqrt)
            recip(rstd[:, 0:n], rstd[:, 0:n])
            for c in range(8):
                stt(dst[:, c, 0:n], src_tile[:, c, 0:n], col[:, gcol + c:gcol + c + 1], rstd[:, 0:n], ALU.mult, ALU.mult)

        stage_ref = [None]

        _ld = [0]

        def load_cast(dst, src_ap, ncols):
            st = stage_ref[0]
            step = (st[0] if isinstance(st, list) else st).shape[1]
            for n0 in range(0, ncols, step):
                n1 = min(ncols, n0 + step)
                k = _ld[0]; _ld[0] += 1
                stage = st[k % len(st)] if isinstance(st, list) else st
                dma(stage[:, 0:n1 - n0], src_ap[:, n0:n1])
                if isinstance(st, list) and k % 2 == 1:
                    act(dst[:, n0:n1], stage[:, 0:n1 - n0], AF.Copy)
                else:
                    cp(dst[:, n0:n1], stage[:, 0:n1 - n0], eng="pool")

        def load_w_bf16(dst, src_ap, ncols):
            kc = src_ap.shape[0] // 128
            v = src_ap.rearrange("(c p) n -> p c n", p=128)
            for c in range(kc):
                load_cast(dst[:, c, :], v[:, c, :], ncols)

        if PES[0] is not es:
            S.emit()
            PES[0].close()
        PES[0] = contextlib.ExitStack()
        CW = 0.6065306597126334
        win_rw = sb([128, 8, 1792], BF16, "win_rw")
        _phB = PES[0]
        PES[0] = contextlib.ExitStack()
        stage_ref[0] = [sb([128, 2048], F32, "stageBa"), sb([128, 2048], F32, "stageBb")]
        load_w_bf16(win_rw, w_in[:, 512:2304], 1792)
        S.emit()
        PES[0].close()
        PES[0] = _phB
        w2aug = sb([65, 512], F32, "w2aug"); dma(w2aug[:], w2aug_d)
        a2sb = sb([128, 512], F32, "a2sb"); dma(a2sb[:], a2_d)
        g2sb = sb([128, 512], F32, "g2sb"); dma(g2sb[:], g2_d)
        c2 = sb([128, 1024], F32, "c2masks"); dma(c2[:], c2_d[:, 0:1024])
        maskLE4 = c2[:, 0:512].rearrange("p (c n) -> p c n", n=128)
        maskN4 = c2[:, 512:768].rearrange("p (c n) -> p c n", n=64)
        istk4 = c2[:, 768:1024].rearrange("p (c n) -> p c n", n=64)
        xn = sb([128, 8, 256], BF16, "xn")
        z = [sb([128, 257], F32, "z%d" % m) for m in range(14)]
        tmpb = [sb([128, 256], F32, "tmpb%d" % i) for i in range(2)]
        tw = sb([65, 256], F32, "tw"); memset(tw[64:65, :], 1.0)
        sgt = [sb([128, 512], F32, "sgt%d" % i) for i in range(2)]
        sgl = sb([128, 256], F32, "sgl")
        t1s = sb([128, 256], F32, "t1s")
        R4 = range(4)
        P_ = [sb([128, 256], F32, "P_%d" % i) for i in R4]
        g_ = [sb([128, 256], F32, "g_%d" % i) for i in R4]
        ar = [sb([128, 4, 128], F32, "ar%d" % i) for i in R4]
        bT = [sb([128, 4, 64], F32, "bT%d" % i) for i in R4]
        kT = [sb([128, 4, 64], F32, "kT%d" % i) for i in R4]
        bonus = [sb([128, 256], F32, "bonus%d" % i) for i in R4]
        tok = [sb([128, 4, 192], F32, "tok%d" % i) for i in R4]
        gmA = [sb([128, 4, 128], F32, "gmA%d" % i) for i in R4]
        gmN = [sb([128, 4, 64], F32, "gmN%d" % i) for i in R4]
        gmB = [sb([128, 4, 128], F32, "gmB%d" % i) for i in R4]
        nzb = [[sb([128, 4, 128], F32, "nzb%d_%d" % (i, j)) for j in range(2)] for i in R4]
        inj = [sb([128, 4, 64], F32, "inj%d" % i) for i in R4]
        xtb = [[sb([128, 4, 64], F32, "xtb%d_%d" % (i, j)) for j in range(2)] for i in R4]
        Wsb = [sb([128, 64], F32, "Wsb%d" % i) for i in R4]
        Usb = [sb([128, 64], F32, "Usb%d" % i) for i in R4]
        yT = [sb([128, 256], F32, "yT%d" % i) for i in R4]
        ST = [sb([128, 64], F32, "ST%d" % h) for h in R4]
        for h in R4:
            memset(ST[h][:], 0.0)
        yc = sb([128, 256], F32, "yc")
        HS = (slice(0, 64), slice(64, 128))
        p_in = pALL[:, 2048:2304]
        p_sgt = pALL[:, 2560:3072]
        p_lp = pALL[:, 3072:3584]
        p_x0 = pALL[:, 3584:3840]; p_x1 = pALL[:, 3840:4096]

        def pset(hp):
            return pALL[:, hp * 512:(hp + 1) * 512].rearrange("p (c n) -> p c n", n=128)

        def acopy(out, in_):
            act(out, in_, AF.Copy)

        zc = sb([128, 14], F32, "zc"); memset(zc[:], 0.0)

        def rms_in(tj_):
            dma(xt[:], xT_v[:, :, tj_ * 256:tj_ * 256 + 256])
            rmsnorm_tile(xt, 0, 256, C_NMIX, xn)

        def inproj_chunk(m):
            cp(z[m][:, 0:1], zc[:, m:m + 1], eng="pool")
            p_in = pALL[:, 2048 + (m % 2) * 512:2048 + (m % 2) * 512 + 256]
            for c in range(8):
                mm(p_in, win_rw[:, c, m * 128:(m + 1) * 128], xn[:, c, :], start=(c == 0), stop=(c == 7))
            acopy(z[m][:, 1:257], p_in)
            tb_ = tmpb[m % 2]
            tt(tb_[:], z[m][:, 0:256], z[m][:, 1:257], ALU.subtract, eng="pool")
            cp(zc[:, m:m + 1], z[m][:, 256:257], eng="pool")
            stt(z[m][:, 1:257], tb_[:], col[:, C_MU + m:C_MU + m + 1], z[m][:, 1:257], ALU.mult, ALU.add)

        NTILE = NT // 256
        for tj in range(NTILE):
            t0 = tj * 256
            own = t0 >= NT - OWN
            o0 = t0 - (NT - OWN)
            if tj == 0:
                rms_in(0)
                for m in range(14):
                    inproj_chunk(m)
            zs = [zz[:, 1:257] for zz in z]
            act(tw[0:64, :], zs[12][0:64, :], AF.Tanh)
            for tb in range(2):
                mm(p_sgt, tw[:, tb * 128:(tb + 1) * 128], w2aug[:], True, True)
                act(sgt[tb][:], p_sgt, AF.Sigmoid)
            act(sgl[:], zs[13], AF.Sigmoid)
            fl = lambda t_: t_[:].rearrange("p c n -> p (c n)")
            Pe = [fl(gmA[hp])[:, 0:256] for hp in R4]; Pi = [fl(gmA[hp])[:, 256:512] for hp in R4]
            a_ = [fl(gmB[hp])[:, 0:256] for hp in R4]; kk = [fl(gmB[hp])[:, 256:512] for hp in R4]
            rn = [fl(nzb[hp][0])[:, 0:256] for hp in R4]; kp = [fl(nzb[hp][0])[:, 256:512] for hp in R4]
            t1 = [fl(tok[hp])[:, 0:256] for hp in R4]
            HC = [slice(hp * 128, (hp + 1) * 128) for hp in R4]
            plp = [pALL[:, hp * 512:(hp + 1) * 512] for hp in R4]
            px0 = [pALL[:, 2048 + hp * 512:2048 + hp * 512 + 256] for hp in R4]
            px1 = [pALL[:, 2048 + hp * 512 + 256:2048 + hp * 512 + 512] for hp in R4]
            v4 = lambda a: a.rearrange("p (c t) -> p c t", t=64)
            for hp in R4:
                for tb in range(2):
                    mm(plp[hp][:, tb * 128:(tb + 1) * 128], sgt[tb][:, HC[hp]], tri_i, True, True)
                    mm(plp[hp][:, 256 + tb * 128:256 + (tb + 1) * 128], sgt[tb][:, HC[hp]], tri_e, True, True)
            for hp in R4:
                mm(px0[hp], a2sb[64:128, HC[hp]], zs[12][64:128, :], True, True)
                mm(px1[hp], g2sb[:, HC[hp]], sgl[:], True, True)
            for hp in R4:
                ts(kk[hp], zs[4 + hp], col[:, C_KK + hp:C_KK + hp + 1], None, ALU.mult)
                tt(rn[hp], kk[hp], kk[hp], ALU.mult, eng="pool")
            for hp in R4:
                act(P_[hp][:], plp[hp][:, 0:256], AF.Exp, scale=-CW)
                act(Pi[hp], plp[hp][:, 0:256], AF.Exp, scale=CW)
                act(Pe[hp], plp[hp][:, 256:512], AF.Exp, scale=-CW)
            for hp in R4:
                act(a_[hp], px0[hp], AF.Sigmoid, bias=col[:, C_A0 + hp:C_A0 + hp + 1])
            for hp in R4:
                acopy(g_[hp][:], px1[hp])
            for hp in R4:
                mm(px0[hp], blk1, rn[hp], True, True)
            for hp in R4:
                ts(rn[hp], px0[hp], 1e-24, None, ALU.max)
            for hp in R4:
                act(rn[hp], rn[hp], AF.Sqrt)
            for hp in R4:
                recip(rn[hp], rn[hp])
                tt(kk[hp], kk[hp], rn[hp], ALU.mult)
                ts(t1[hp], a_[hp], -1.0, col[:, C_KA + hp:C_KA + hp + 1], ALU.add, ALU.mult)
                stt(kp[hp], t1[hp], 1.0, zs[4 + hp], ALU.add, ALU.mult)
            for hp in R4:
                tt(ar[hp][:, :, 64:128], v4(zs[hp]), v4(P_[hp][:]), ALU.mult, eng="pool")
            for hp in R4:
                stt(ar[hp][:, :, 0:64], v4(kk[hp]), -1.0, v4(Pe[hp]), ALU.mult, ALU.mult)
                tt(t1[hp], kk[hp], a_[hp], ALU.mult)
                tt(bT[hp][:].rearrange("p c t -> p (c t)"), t1[hp], Pi[hp], ALU.mult)
                tt(kT[hp][:].rearrange("p c t -> p (c t)"), kp[hp], Pi[hp], ALU.mult)
            if own:
                for hp in R4:
                    stt(t1[hp], zs[hp], col[:, C_RK + hp:C_RK + hp + 1], kp[hp], ALU.mult, ALU.mult)
                for hp in R4:
                    mm(px0[hp], blk1, t1[hp], True, True)
                for hp in R4:
                    tt(bonus[hp][:], px0[hp], zs[8 + hp], ALU.mult)

            def st_T1(hp):
                ps = pset(hp); v_ = zs[8 + hp]
                for c in range(4):
                    cs = slice(c * 64, (c + 1) * 64)
                    for h in range(2):
                        hs = HS[h]
                        tr(ps[hs, c, 0:64], v_[hs, cs], ident[hs, hs])
                        tr(ps[hs, c, 64:128], bT[hp][hs, c, :], ident[hs, hs])
                acopy(tok[hp][:, :, 0:128], ps)

            def st_T2(hp):
                ps = pset(hp)
                for c in range(4):
                    for h in range(2):
                        hs = HS[h]
                        tr(ps[hs, c, 0:64], kT[hp][hs, c, :], ident[hs, hs])
                        mm(ps[hs, c, 64:128], ar[hp][hs, c, 0:64], bT[hp][hs, c, :], True, True)
                acopy(tok[hp][:, :, 128:192], ps[:, :, 0:64])
                tt(gmN[hp][:], ps[:, :, 64:128], maskN4, ALU.mult)

            def st_G1(hp):
                ps = pset(hp)
                for c in range(4):
                    for h in range(2):
                        hs = HS[h]
                        mm(ps[hs, c, 0:128], bT[hp][hs, c, :], ar[hp][hs, c, :], True, True)
                tt(gmA[hp][:], ps, maskLE4, ALU.mult)
                tt(xtb[hp][0][:], gmA[hp][:, :, 0:64], istk4, ALU.add, eng="pool")

            def st_G2(hp):
                ps = pset(hp)
                for c in range(4):
                    for h in range(2):
                        hs = HS[h]
                        mm(ps[hs, c, 0:128], kT[hp][hs, c, :], ar[hp][hs, c, :], True, True)
                tt(gmB[hp][:], ps, maskLE4, ALU.mult)

            def mk_level(j):
                def sa(hp):
                    ps = pset(hp)
                    if j == 1:
                        Zc, Nc = gmA[hp][:, :, 0:64], gmN[hp][:]
                    else:
                        Nc, Zc = nzb[hp][j % 2][:, :, 0:64], nzb[hp][j % 2][:, :, 64:128]
                    for c in range(4):
                        for h in range(2):
                            hs = HS[h]
                            mm(ps[hs, c, 0:64], Zc[hs, c, :], Nc[hs, c, :], True, True)
                            if j < 5:
                                mm(ps[hs, c, 64:128], Nc[hs, c, :], Zc[hs, c, :], True, True)
                    tt(inj[hp][:], ps[:, :, 0:64], istk4, ALU.add)
                    if j < 5:
                        acopy(nzb[hp][(j + 1) % 2][:], ps)

                def sb_(hp):
                    ps = pset(hp)
                    XTc = xtb[hp][(j - 1) % 2]
                    for c in range(4):
                        for h in range(2):
                            hs = HS[h]
                            mm(ps[hs, c, 0:64], inj[hp][hs, c, :], XTc[hs, c, :], True, True)
                    acopy(xtb[hp][j % 2][:], ps[:, :, 0:64])
                return sa, sb_

            stages = [st_T1, st_T2, st_G1, st_G2]
            for j in range(1, 6):
                stages.extend(mk_level(j))
            for si, stg in enumerate(stages):
                for hp in R4:
                    stg(hp)
                if tj + 1 < NTILE:
                    if si == 0:
                        rms_in(tj + 1)
                    else:
                        inproj_chunk(si - 1)
            if tj + 1 < NTILE:
                inproj_chunk(13)
            for c in range(4):
                cs = slice(c * 64, (c + 1) * 64)
                for hp in R4:
                    pw = pALL[:, 2048 + hp * 512:2048 + hp * 512 + 64]
                    for h in range(2):
                        hs = HS[h]
                        mm(pw[hs], ar[hp][hs, c, 0:64], ST[hp][hs, :], True, False)
                        mm(pw[hs], gmB[hp][hs, c, 0:64], tok[hp][hs, c, 0:64], False, True)
                    cp(Wsb[hp][:], pw)
                for hp in R4:
                    pu = pALL[:, 2048 + hp * 512 + 64:2048 + hp * 512 + 128]
                    for h in range(2):
                        hs = HS[h]
                        mm(pu[hs], xtb[hp][1][hs, c, :], Wsb[hp][hs, :], True, True)
                    acopy(Usb[hp][:], pu)
                for hp in R4:
                    py = pALL[:, 2048 + hp * 512 + 128:2048 + hp * 512 + 192]
                    pst = pALL[:, 2048 + hp * 512 + 192:2048 + hp * 512 + 256]
                    Vt, Bt, Kt = tok[hp][:, c, 0:64], tok[hp][:, c, 64:128], tok[hp][:, c, 128:192]
                    for h in range(2):
                        hs = HS[h]
                        mm(pst[hs], ident[hs, hs], ST[hp][hs, :], True, False)
                        mm(pst[hs], Bt[hs], Usb[hp][hs, :], False, False)
                        mm(pst[hs], Kt[hs], Vt[hs], False, True)
                    if own:
                        for h in range(2):
                            hs = HS[h]
                            mm(py[hs], ST[hp][hs, :], ar[hp][hs, c, 64:128], True, False)
                            mm(py[hs], Usb[hp][hs, :], gmA[hp][hs, c, 64:128], False, False)
                            mm(py[hs], Vt[hs], gmB[hp][hs, c, 64:128], False, True)
                        acopy(yT[hp][:, cs], py)
                    ts(ST[hp][:], pst, P_[hp][:, c * 64 + 63:c * 64 + 64], None, ALU.mult)
            if own:
                for hp in R4:
                    mm(p_x0, blk1, yT[hp][:], True, True)
                    stt(yc[:], p_x0, -1.0 / 64, yT[hp][:], ALU.mult, ALU.add)
                    tt(t1s[:], yc[:], yc[:], ALU.mult, eng="pool")
                    mm(p_x1, blk1, t1s[:], True, True)
                    ts(t1s[:], p_x1, 1.0 / 64, 64e-5, ALU.mult, ALU.add)
                    act(t1s[:], t1s[:], AF.Sqrt)
                    recip(t1s[:], t1s[:])
                    tt(yc[:], yc[:], t1s[:], ALU.mult)
                    ts(yc[:], yc[:], col[:, C_LNW + hp:C_LNW + hp + 1], col[:, C_LNB + hp:C_LNB + hp + 1], ALU.mult, ALU.add)
                    tt(yc[:], yc[:], bonus[hp][:], ALU.add, eng="pool")
                    tt(concB[:, hp, o0:o0 + 256], yc[:], g_[hp][:], ALU.mult, eng="pool")

        if PES[0] is not es:
            S.emit()
            PES[0].close()
        PES[0] = contextlib.ExitStack()
        concA = es.enter_context(nc.sbuf_tensor("sb_concA", [128, 4, OWN], BF16))
        NPW = 17
        s5c = sb([128, 177], F32, "s5c"); dma(s5c[:], s5c_d)
        pr = sb([128, 16, NPW], F32, "pr"); pi_ = sb([128, 16, NPW], F32, "pi_")
        WCre = sb([128, 16, 128], BF16, "WCre"); WCim = sb([128, 16, 128], BF16, "WCim")
        WEre = sb([128, 32, 64], BF16, "WEre"); WEim = sb([128, 32, 64], BF16, "WEim"); Toep = sb([128, 32, 128], BF16, "Toep")
        identb = sb([128, 128], BF16, "identb"); cp(identb[:], ident[:])
        win_s5 = sb([128, 8, 512], BF16, "win_s5")
        glub = sb([128, 4, 512], BF16, "glub")
        xn5 = sb([128, 8, 512], BF16, "xn5")
        X8 = sb([128, 32, 8, 16], BF16, "X8")
        U8 = sb([128, 32, 64], BF16, "U8")
        BK0 = [pA, pB, pC, pLP, pG, pDT, pW, pX]

        def A0_steps():
            def a_rms(s):
                dma(xt[:], xT_v[:, :, s * 256:(s + 1) * 256])
                rmsnorm_tile(xt, 0, 256, C_NMIX, xn5[:, :, s * 256:(s + 1) * 256])

            def a_in(i0):
                bk = BK0[i0 % 4]
                for c in range(8):
                    mm(bk[0:64, :], xn5[:, c, i0::8], win_s5[:, c, :], start=(c == 0), stop=(c == 7))
                ecp(i0, X8[0:64, :, i0, :], bk[0:64, :].rearrange("p (g c) -> p g c", c=16))

            def a_tr(gb):
                bk = BK0[4 + gb]
                for gi in range(8):
                    g = gb * 8 + gi
                    tr(bk[:, gi * 64:(gi + 1) * 64], X8[0:64, g, :, :].rearrange("p a b -> p (a b)"), identb[0:64, 0:64])
                ecp(gb, U8[:, gb * 8:(gb + 1) * 8, :], bk.rearrange("p (g b) -> p g b", b=64))
            return ([lambda s=s: a_rms(s) for s in range(2)] + [lambda i0=i0: a_in(i0) for i0 in range(8)]
                    + [lambda gb=gb: a_tr(gb) for gb in range(4)])

        a0q = A0_steps()
        _phaseA = PES[0]
        PES[0] = contextlib.ExitStack()
        stage_ref[0] = [sb([128, 1024], F32, "stageAa"), sb([128, 1024], F32, "stageAb")]
        load_w_bf16(win_s5, w_in[:, 0:512], 512)
        load_w_bf16(glub, glu_w, 512)
        s5B = sb([128, 1072], F32, "s5B"); dma(s5B[:], s5B_d)
        lre, lim, lst = s5B[:, 0:16], s5B[:, 16:32], s5B[:, 32:48]
        cre = s5B[:, 48:304].rearrange("p (g c) -> p g c", c=16)
        cim = s5B[:, 304:560].rearrange("p (g c) -> p g c", c=16)
        bre = s5B[:, 560:816].rearrange("p (g c) -> p g c", c=16)
        bim = s5B[:, 816:1072].rearrange("p (g c) -> p g c", c=16)
        mtab = s5c[:, 0:17]; dcol = s5c[:, 17:49]; maskT = s5c[:, 49:177]
        dtt = sb([128, 16], F32, "dtt"); are = sb([128, 16], F32, "are"); aim = sb([128, 16], F32, "aim")
        act(dtt[:], lst, AF.Exp)
        tt(are[:], lre, dtt[:], ALU.mult)
        tt(aim[:], lim, dtt[:], ALU.mult)
        IDX0 = lambda m: (m + 7) if m != 64 else 16
        m1 = sb([128, 16], F32, "m1"); rr = sb([128, 16], F32, "rr"); acc_ = sb([128, 16], F32, "acc_")
        sn = sb([128, 16], F32, "sn"); cs_ = sb([128, 16], F32, "cs_"); mg = sb([128, 16], F32, "mg")
        act(mg[:], are[:], AF.Exp)

        def sin_small(dst, shift):
            ts(rr[:], aim[:], shift, None, ALU.add)
            memset(acc_[:], 0.0)
            for k in range(1, 7):
                ts(m1[:], rr[:], (2 * k - 1) * PI, -TWO_PI, ALU.is_gt, ALU.mult)
                tt(acc_[:], acc_[:], m1[:], ALU.add)
            tt(rr[:], rr[:], acc_[:], ALU.add)
            act(dst, rr[:], AF.Sin)

        sin_small(sn[:], 0.0)
        sin_small(cs_[:], PI / 2)
        P1r, P1i = pr[:, :, IDX0(1)], pi_[:, :, IDX0(1)]
        tt(P1r, cs_[:], mg[:], ALU.mult)
        tt(P1i, sn[:], mg[:], ALU.mult)
        memset(pr[:, :, IDX0(0)], 1.0); memset(pi_[:, :, IDX0(0)], 0.0)
        cm1 = sb([128, 16], F32, "cm1"); cm2 = sb([128, 16], F32, "cm2")

        def cmul_s(dr, di, ar_, ai_, br_, bi_):
            tt(cm1[:], ar_, br_, ALU.mult); tt(cm2[:], ai_, bi_, ALU.mult); tt(dr, cm1[:], cm2[:], ALU.subtract)
            tt(cm1[:], ar_, bi_, ALU.mult); tt(cm2[:], ai_, br_, ALU.mult); tt(di, cm1[:], cm2[:], ALU.add)

        for m in range(2, 9):
            cmul_s(pr[:, :, IDX0(m)], pi_[:, :, IDX0(m)], pr[:, :, IDX0(m - 1)], pi_[:, :, IDX0(m - 1)], P1r, P1i)
            for _ in range(2):
                if a0q:
                    a0q.pop(0)()
        ivr = sb([128, 16], F32, "ivr"); ivi = sb([128, 16], F32, "ivi")
        tt(cm1[:], mg[:], mg[:], ALU.mult)
        recip(cm1[:], cm1[:])
        tt(ivr[:], P1r, cm1[:], ALU.mult)
        stt(ivi[:], P1i, -1.0, cm1[:], ALU.mult, ALU.mult)
        cp(pr[:, :, IDX0(-1)], ivr[:]); cp(pi_[:, :, IDX0(-1)], ivi[:])
        for m in range(2, 8):
            cmul_s(pr[:, :, IDX0(-m)], pi_[:, :, IDX0(-m)], pr[:, :, IDX0(-(m - 1))], pi_[:, :, IDX0(-(m - 1))], ivr[:], ivi[:])
        s16r = sb([128, 16], F32, "s16r"); s16i = sb([128, 16], F32, "s16i")
        s32r = sb([128, 16], F32, "s32r"); s32i = sb([128, 16], F32, "s32i")
        cmul_s(s16r[:], s16i[:], pr[:, :, IDX0(8)], pi_[:, :, IDX0(8)], pr[:, :, IDX0(8)], pi_[:, :, IDX0(8)])
        cmul_s(s32r[:], s32i[:], s16r[:], s16i[:], s16r[:], s16i[:])
        cmul_s(pr[:, :, IDX0(64)], pi_[:, :, IDX0(64)], s32r[:], s32i[:], s32r[:], s32i[:])
        IDX = lambda m: (m + 7) if m != 64 else 16
        qn = sb([128, 16], F32, "qn"); qre = sb([128, 16], F32, "qre"); qim = sb([128, 16], F32, "qim")
        den = sb([128, 16], F32, "den"); lb1 = sb([128, 16], F32, "lb1"); q2 = sb([128, 16], F32, "q2")
        ts(lb1[:], pr[:, :, IDX(1)], -1.0, None, ALU.add)
        tt(den[:], lre, lre, ALU.mult); tt(q2[:], lim, lim, ALU.mult); tt(den[:], den[:], q2[:], ALU.add)
        recip(den[:], den[:])
        tt(qre[:], lb1[:], lre, ALU.mult); tt(q2[:], pi_[:, :, IDX(1)], lim, ALU.mult); tt(qre[:], qre[:], q2[:], ALU.add)
        tt(qre[:], qre[:], den[:], ALU.mult)
        tt(qim[:], pi_[:, :, IDX(1)], lre, ALU.mult); tt(q2[:], lb1[:], lim, ALU.mult); tt(qim[:], qim[:], q2[:], ALU.subtract)
        tt(qim[:], qim[:], den[:], ALU.mult)
        bbr = sb([128, 16, 16], F32, "bbr"); bbi = sb([128, 16, 16], F32, "bbi"); tq = sb([128, 16, 16], F32, "tq")
        bq = lambda a: a.unsqueeze(2).to_broadcast([128, 16, 16])
        tt(bbr[:], bre, bq(qre[:]), ALU.mult); tt(tq[:], bim, bq(qim[:]), ALU.mult); tt(bbr[:], bbr[:], tq[:], ALU.subtract)
        tt(bbi[:], bim, bq(qre[:]), ALU.mult); tt(tq[:], bre, bq(qim[:]), ALU.mult); tt(bbi[:], bbi[:], tq[:], ALU.add)
        WCre_f = sb([128, 16, 128], F32, "WCre_f"); WCim_f = sb([128, 16, 128], F32, "WCim_f")
        QCre = sb([128, 16, 128], F32, "QCre"); QCim = sb([128, 16, 128], F32, "QCim")
        PBre = sb([128, 16, 128], F32, "PBre"); PBimn = sb([128, 16, 128], F32, "PBimn")
        P7re = sb([128, 16, 128], F32, "P7re"); P7im = sb([128, 16, 128], F32, "P7im")

        prN = sb([128, 16, 8], F32, "prN"); piN = sb([128, 16, 8], F32, "piN")
        pr7 = sb([128, 16, 8], F32, "pr7"); pi7 = sb([128, 16, 8], F32, "pi7")
        for j in range(8):
            cp(prN[:, :, j], pr[:, :, IDX(-j)], eng="pool"); cp(piN[:, :, j], pi_[:, :, IDX(-j)], eng="pool")
            cp(pr7[:, :, j], pr[:, :, IDX(7 - j)], eng="pool"); cp(pi7[:, :, j], pi_[:, :, IDX(7 - j)], eng="pool")
        ctmp = sb([128, 16, 128], F32, "ctmp")

        def cmulv(Tre, Tim, sre, sim, lr8, li8, neg_im=False):
            v = lambda a: a.rearrange("p g (j c) -> p g j c", c=16)
            bs = lambda a: a.unsqueeze(2).to_broadcast([128, 16, 8, 16])
            bl = lambda a: a.unsqueeze(3).to_broadcast([128, 16, 8, 16])
            tt(v(Tre[:]), bs(sre), bl(lr8), ALU.mult)
            tt(v(ctmp[:]), bs(sim), bl(li8), ALU.mult)
            tt(Tre[:], Tre[:], ctmp[:], ALU.subtract)
            tt(v(Tim[:]), bs(sim), bl(lr8), ALU.mult)
            tt(v(ctmp[:]), bs(sre), bl(li8), ALU.mult)
            tt(Tim[:], Tim[:], ctmp[:], ALU.add)
            if neg_im:
                ts(Tim[:], Tim[:], -1.0, None, ALU.mult)

        cmulv(WCre_f, WCim_f, cre, cim, pr[:, :, IDX(1):IDX(8) + 1], pi_[:, :, IDX(1):IDX(8) + 1], neg_im=True)
        cp(WCre[:], WCre_f[:], eng="pool"); cp(WCim[:], WCim_f[:], eng="pool")
        cmulv(QCre, QCim, cre, cim, pr[:, :, IDX(0):IDX(7) + 1], pi_[:, :, IDX(0):IDX(7) + 1])
        cmulv(PBre, PBimn, bbr[:], bbi[:], prN[:], piN[:], neg_im=True)
        cmulv(P7re, P7im, bbr[:], bbi[:], pr7[:], pi7[:])
        BKp = [pA, pB, pC, pLP, pG, pDT, pW, pX]
        for gb in range(8):
            bkE = BKp[gb % 4]; bkT = BKp[4 + gb % 4]
            for gi in range(4):
                g = gb * 4 + gi
                hs = HS[g % 2]; pg = g // 2
                tr(bkE[:, gi * 64:(gi + 1) * 64], P7re[hs, pg, :], ident[hs, hs])
                tr(bkE[:, 256 + gi * 64:256 + (gi + 1) * 64], P7im[hs, pg, :], ident[hs, hs])
                mm(bkT[:, gi * 128:(gi + 1) * 128], PBre[hs, pg, :], QCre[hs, pg, :], True, False)
                mm(bkT[:, gi * 128:(gi + 1) * 128], PBimn[hs, pg, :], QCim[hs, pg, :], False, True)
            gs4 = slice(gb * 4, gb * 4 + 4)
            acopy(WEre[:, gs4, :], bkE[:, 0:256].rearrange("p (g n) -> p g n", n=64))
            acopy(WEim[:, gs4, :], bkE[:, 256:512].rearrange("p (g n) -> p g n", n=64))
            tt(Toep[:, gs4, :], bkT.rearrange("p (g n) -> p g n", n=128), maskT.unsqueeze(1).to_broadcast([128, 4, 128]), ALU.mult)
        S.emit()
        PES[0].close()
        PES[0] = _phaseA
        L8r, L8i = pr[:, :, IDX(8)], pi_[:, :, IDX(8)]
        L64r, L64i = pr[:, :, IDX(64)], pi_[:, :, IDX(64)]
        L16r = sb([128, 16], F32, "L16r"); L16i = sb([128, 16], F32, "L16i")
        L32r = sb([128, 16], F32, "L32r"); L32i = sb([128, 16], F32, "L32i")
        lsq = sb([128, 16], F32, "lsq"); lsq2 = sb([128, 16], F32, "lsq2")
        for (sr_, si_, dr_, di_) in ((L8r, L8i, L16r[:], L16i[:]), (L16r[:], L16i[:], L32r[:], L32i[:])):
            tt(lsq[:], sr_, sr_, ALU.mult); tt(lsq2[:], si_, si_, ALU.mult); tt(dr_, lsq[:], lsq2[:], ALU.subtract)
            tt(lsq[:], sr_, si_, ALU.mult); ts(di_, lsq[:], 2.0, None, ALU.mult)
        Lpr = sb([128, 16, 8], F32, "Lpr"); Lpi = sb([128, 16, 8], F32, "Lpi")
        lq1 = sb([128, 16], F32, "lq1"); lq2 = sb([128, 16], F32, "lq2")
        cp(Lpr[:, :, 0], L64r); cp(Lpi[:, :, 0], L64i)
        for c in range(1, 8):
            tt(lq1[:], Lpr[:, :, c - 1], L64r, ALU.mult); tt(lq2[:], Lpi[:, :, c - 1], L64i, ALU.mult)
            tt(Lpr[:, :, c], lq1[:], lq2[:], ALU.subtract)
            tt(lq1[:], Lpr[:, :, c - 1], L64i, ALU.mult); tt(lq2[:], Lpi[:, :, c - 1], L64r, ALU.mult)
            tt(Lpi[:, :, c], lq1[:], lq2[:], ALU.add)

        Ytok8 = sb([128, 8, 512], F32, "Ytok8")[:]
        stbr = sb([128, 16, 64], BF16, "stbr"); stbi = sb([128, 16, 64], BF16, "stbi")
        du = sb([128, 8, 64], F32, "du")
        E8r = sb([128, 16, 64], F32, "E8r"); E8i = sb([128, 16, 64], F32, "E8i")
        str_ = sb([128, 16, 64], F32, "str_"); sti = sb([128, 16, 64], F32, "sti")
        accr = sb([128, 16, 8], F32, "accr"); acci = sb([128, 16, 8], F32, "acci")
        ca = sb([128, 16, 8], F32, "ca"); cb = sb([128, 16, 8], F32, "cb")
        cc = sb([128, 16, 8], F32, "cc"); cd = sb([128, 16, 8], F32, "cd")
        hsr = sb([128, 16, 8], F32, "hsr"); hsi = sb([128, 16, 8], F32, "hsi")
        cstr = sb([128, 16, 8], F32, "cstr"); csti = sb([128, 16, 8], F32, "csti")
        car = sb([128, 16], F32, "car"); cai = sb([128, 16], F32, "cai"); memset(car[:], 0.0); memset(cai[:], 0.0)
        c1 = sb([128, 16], F32, "c1"); c2 = sb([128, 16], F32, "c2")
        Yim = sb([128, 32, 64], F32, "Yim")
        y5 = Yim[:].rearrange("p g b -> p (g b)").rearrange("p (q t) -> p q t", t=512)
        g1 = sb([128, 512], F32, "g1"); g2t = sb([128, 512], F32, "g2t")
        zb = sb([128, 4, 512], BF16, "zb")
        zf = y5

        def bc(ap16, n):
            return ap16.unsqueeze(2).to_broadcast([128, 16, n])

        def cstep(dr, di, sr, si, lr, li, er, ei, n, scr=None):
            ca, cb, cc, cd = scr
            tt(ca[:, :, 0:n], sr, bc(lr, n), ALU.mult)
            tt(cb[:, :, 0:n], si, bc(li, n), ALU.mult)
            tt(cc[:, :, 0:n], sr, bc(li, n), ALU.mult)
            tt(cd[:, :, 0:n], si, bc(lr, n), ALU.mult)
            tt(ca[:, :, 0:n], ca[:, :, 0:n], cb[:, :, 0:n], ALU.subtract)
            tt(cc[:, :, 0:n], cc[:, :, 0:n], cd[:, :, 0:n], ALU.add)
            tt(dr, ca[:, :, 0:n], er, ALU.add)
            tt(di, cc[:, :, 0:n], ei, ALU.add)

        BK = [pA, pB, pC, pLP, pG, pDT, pW, pX]
        U8s = [U8, sb([128, 32, 64], BF16, "U8b")]
        E8rs = [E8r, sb([128, 16, 64], F32, "E8rb")]; E8is = [E8i, sb([128, 16, 64], F32, "E8ib")]
        cstrs = [cstr, sb([128, 16, 8], F32, "cstrb")]; cstis = [csti, sb([128, 16, 8], F32, "cstib")]
        scrF = (ca, cb, cc, cd)
        scrB = tuple(sb([128, 16, 8], F32, "scrB%d" % i) for i in range(4))
        v8 = lambda t_: t_[:].rearrange("p g (c j) -> p g c j", j=8)
        vh = lambda t_: t_[:].rearrange("p g (ch j) -> p g ch j", j=4)
        a16r = sb([128, 16, 16], F32, "a16r"); a16i = sb([128, 16, 16], F32, "a16i")
        scr16 = tuple(sb([128, 16, 16], F32, "scr16_%d" % i) for i in range(4))
        scr16b = tuple(sb([128, 16, 16], F32, "scr16b_%d" % i) for i in range(4))
        aLor = [sb([128, 16, 8], F32, "aLor%d" % i) for i in range(2)]
        aLoi = [sb([128, 16, 8], F32, "aLoi%d" % i) for i in range(2)]

        def front(ti):
            t0 = ti * 512
            pp = ti % 2
            U8_, E8r_, E8i_ = U8s[pp], E8rs[pp], E8is[pp]
            E8rv, E8iv = v8(E8r_), v8(E8i_)
            st_ = []

            def f_rms(s):
                dma(xt[:], xT_v[:, :, t0 + s * 256:t0 + (s + 1) * 256])
                rmsnorm_tile(xt, 0, 256, C_NMIX, xn5[:, :, s * 256:(s + 1) * 256])

            def f_in(i0):
                bk = BK[i0 % 4]
                for c in range(8):
                    mm(bk[0:64, :], xn5[:, c, i0::8], win_s5[:, c, :], start=(c == 0), stop=(c == 7))
                ecp(i0, X8[0:64, :, i0, :], bk[0:64, :].rearrange("p (g c) -> p g c", c=16))

            def f_tr(gb):
                bk = BK[4 + gb]
                for gi in range(8):
                    g = gb * 8 + gi
                    tr(bk[:, gi * 64:(gi + 1) * 64], X8[0:64, g, :, :].rearrange("p a b -> p (a b)"), identb[0:64, 0:64])
                ecp(gb, U8_[:, gb * 8:(gb + 1) * 8, :], bk.rearrange("p (g b) -> p g b", b=64))

            def f_e8(hf):
                for g in range(hf * 16, hf * 16 + 16):
                    hs = HS[g % 2]; pg = g // 2
                    mm(pC[hs, (pg % 8) * 64:(pg % 8 + 1) * 64], WEre[:, g, :], U8_[:, g, :], True, True)
                    mm(pLP[hs, (pg % 8) * 64:(pg % 8 + 1) * 64], WEim[:, g, :], U8_[:, g, :], True, True)
                o = hf * 8
                cp(E8r_[:, o:o + 8, :], pC[:, :].rearrange("p (g b) -> p g b", b=64))
                acopy(E8i_[:, o:o + 8, :], pLP[:, :].rearrange("p (g b) -> p g b", b=64))

            E8rh, E8ih = vh(E8r_), vh(E8i_)

            def f_p1i():
                cp(a16r[:], E8rh[:, :, :, 0]); cp(a16i[:], E8ih[:, :, :, 0], eng="pool")

            def f_p1(j):
                if j < 4:
                    cstep(a16r[:], a16i[:], a16r[:], a16i[:], L8r, L8i, E8rh[:, :, :, j], E8ih[:, :, :, j], 16, scr=scr16)
                else:
                    lo_r = a16r[:].rearrange("p g (c h) -> p g c h", h=2)[:, :, :, 0]
                    lo_i = a16i[:].rearrange("p g (c h) -> p g c h", h=2)[:, :, :, 0]
                    hi_r = a16r[:].rearrange("p g (c h) -> p g c h", h=2)[:, :, :, 1]
                    hi_i = a16i[:].rearrange("p g (c h) -> p g c h", h=2)[:, :, :, 1]
                    cp(aLor[pp][:], lo_r); cp(aLoi[pp][:], lo_i, eng="pool")
                    cstep(accr[:], acci[:], lo_r, lo_i, L32r[:], L32i[:], hi_r, hi_i, 8, scr=scrF)

            hsb = [(accr, acci), (hsr, hsi)]

            def f_hs(k):
                d, pw_ = ((1, 0), (2, 1), (4, 3))[k]
                (Ar, Ai), (Br, Bi) = hsb[k % 2], hsb[(k + 1) % 2]
                cp(Br[:, :, 0:d], Ar[:, :, 0:d]); cp(Bi[:, :, 0:d], Ai[:, :, 0:d], eng="pool")
                cstep(Br[:, :, d:8], Bi[:, :, d:8], Ar[:, :, 0:8 - d], Ai[:, :, 0:8 - d], Lpr[:, :, pw_], Lpi[:, :, pw_],
                      Ar[:, :, d:8], Ai[:, :, d:8], 8 - d, scr=scrF)

            def f_carry():
                (Ar, Ai), (Br, Bi) = hsb[1], hsb[0]
                ca_, cb_, cc_, cd_ = scrF
                cbr = car[:].unsqueeze(2).to_broadcast([128, 16, 8]); cbi = cai[:].unsqueeze(2).to_broadcast([128, 16, 8])
                tt(ca_[:], cbr, Lpr[:], ALU.mult); tt(cb_[:], cbi, Lpi[:], ALU.mult)
                tt(cc_[:], cbr, Lpi[:], ALU.mult, eng="pool"); tt(cd_[:], cbi, Lpr[:], ALU.mult, eng="pool")
                tt(ca_[:], ca_[:], cb_[:], ALU.subtract); tt(cc_[:], cc_[:], cd_[:], ALU.add, eng="pool")
                tt(Br[:], ca_[:], Ar[:], ALU.add); tt(Bi[:], cc_[:], Ai[:], ALU.add, eng="pool")
                cr_, ci_ = cstrs[pp], cstis[pp]
                cp(cr_[:, :, 0], car[:]); cp(ci_[:, :, 0], cai[:], eng="pool")
                cp(cr_[:, :, 1:8], Br[:, :, 0:7]); cp(ci_[:, :, 1:8], Bi[:, :, 0:7], eng="pool")
                cp(car[:], Br[:, :, 7]); cp(cai[:], Bi[:, :, 7], eng="pool")

            st_ += [lambda s=s: f_rms(s) for s in range(2)]
            st_ += [lambda i0=i0: f_in(i0) for i0 in range(8)]
            st_ += [lambda gb=gb: f_tr(gb) for gb in range(4)]
            st_ += [lambda hf=hf: f_e8(hf) for hf in range(2)]
            st_ += [f_p1i]
            st_ += [lambda j=j: f_p1(j) for j in range(1, 5)]
            st_ += [lambda k=k: f_hs(k) for k in range(3)]
            st_ += [f_carry]
            return st_

        def back(ti):
            t0 = ti * 512
            o0 = t0 - (NT - OWN)
            pp = ti % 2
            U8_, E8r_, E8i_ = U8s[pp], E8rs[pp], E8is[pp]
            E8rv, E8iv = v8(E8r_), v8(E8i_)
            strv, stiv = v8(str_), v8(sti)
            st_ = []

            E8rh, E8ih = vh(E8r_), vh(E8i_)
            strh, stih = vh(str_), vh(sti)
            s5r = str_[:].rearrange("p g (c h j) -> p g c h j", h=2, j=4)
            s5i = sti[:].rearrange("p g (c h j) -> p g c h j", h=2, j=4)

            def b_init():
                cp(s5r[:, :, :, 0, 0], cstrs[pp][:]); cp(s5i[:, :, :, 0, 0], cstis[pp][:], eng="pool")
                cstep(s5r[:, :, :, 1, 0], s5i[:, :, :, 1, 0], cstrs[pp][:], cstis[pp][:], L32r[:], L32i[:],
                      aLor[pp][:], aLoi[pp][:], 8, scr=scrB)

            def b_p2(j):
                cstep(strh[:, :, :, j + 1], stih[:, :, :, j + 1], strh[:, :, :, j], stih[:, :, :, j], L8r, L8i,
                      E8rh[:, :, :, j], E8ih[:, :, :, j], 16, scr=scr16b)

            def b_stb():
                cp(stbr[:], str_[:]); cp(stbi[:], sti[:], eng="pool")

            def b_out(gb):
                bk = BK[gb]
                for gi in range(8):
                    g = gb * 8 + gi
                    hs = HS[g % 2]; pg = g // 2
                    mm(bk[:, gi * 64:(gi + 1) * 64], WCre[hs, pg, :], stbr[hs, pg, :], True, False)
                    mm(bk[:, gi * 64:(gi + 1) * 64], WCim[hs, pg, :], stbi[hs, pg, :], False, False)
                    mm(bk[:, gi * 64:(gi + 1) * 64], Toep[:, g, :], U8_[:, g, :], False, True)
                gs8 = slice(gb * 8, (gb + 1) * 8)
                tt(du[:], U8_[:, gs8, :], dcol[:, gs8].unsqueeze(2).to_broadcast([128, 8, 64]), ALU.mult, eng="pool")
                tt(Yim[:, gs8, :], du[:], bk.rearrange("p (g b) -> p g b", b=64), ALU.add)

            def b_a9(gb):
                bk = BK[gb % 4 + 4]
                for gi in range(4):
                    g = gb * 4 + gi
                    trT(bk[0:64, gi * 128:(gi + 1) * 128], Yim[:, g, :], ident[:])
                ecp(gb, Ytok8[0:64, :, gb * 64:(gb + 1) * 64].rearrange("p j (g c) -> p j g c", c=16),
                    bk[0:64, :].rearrange("p (g j c) -> p j g c", j=8, c=16))

            def b_a10(j0):
                bk = BK[j0 % 4]
                for q in range(4):
                    trT(bk[:, q * 64:(q + 1) * 64], Ytok8[0:64, j0, q * 128:(q + 1) * 128], ident[0:64, 0:64])
                ecp(j0, y5[:, :, j0::8], bk[:, 0:256].rearrange("p (q b) -> p q b", b=64))

            def b_gelu(q):
                tt(g1[:], y5[:, q, :], y5[:, q, :], ALU.mult)
                ts(g1[:], g1[:], 0.044715, 1.0, ALU.mult, ALU.add)
                tt(g1[:], g1[:], y5[:, q, :], ALU.mult)
                act(g1[:], g1[:], AF.Sigmoid, scale=1.5957691216057308)
                tt(y5[:, q, :], y5[:, q, :], g1[:], ALU.mult)
                cp(zb[:, q, :], zf[:, q, :], eng="pool")

            def b_glu(q):
                for kq in range(4):
                    mm(pX[:, :], glub[:, kq, q * 128:(q + 1) * 128], zb[:, kq, :], start=(kq == 0), stop=(kq == 3))
                act(g2t[:], pX[:, :], AF.Sigmoid, bias=col[:, C_GLUB + q:C_GLUB + q + 1])
                tt(concA[:, q, o0:o0 + 512], zf[:, q, :], g2t[:], ALU.mult)

            st_ += [b_init]
            st_ += [lambda j=j: b_p2(j) for j in range(3)]
            st_ += [b_stb]
            st_ += [lambda gb=gb: b_out(gb) for gb in range(4)]
            st_ += [lambda gb=gb: b_a9(gb) for gb in range(8)]
            st_ += [lambda j0=j0: b_a10(j0) for j0 in range(8)]
            st_ += [lambda q=q: b_gelu(q) for q in range(4)]
            st_ += [lambda q=q: b_glu(q) for q in range(4)]
            return st_

        NT5 = NT // 512
        NA = 14

        def zipped(l1, l2):
            for i_ in range(max(len(l1), len(l2))):
                if i_ < len(l1):
                    l1[i_]()
                if i_ < len(l2):
                    l2[i_]()

        NPRE = (NT - OWN) // 512
        fr = {0: front(0)}
        while a0q:
            a0q.pop(0)()
        for ti in range(NPRE):
            fr[ti + 1] = front(ti + 1)
            zipped(fr[ti][NA:], fr[ti + 1][:NA])
        zipped(fr[NPRE][NA:], [])
        for ti in range(NPRE, NT5):
            fs_ = front(ti + 1) if ti + 1 < NT5 else []
            zipped(back(ti), fs_)

        if PES[0] is not es:
            S.emit()
            PES[0].close()
        PES[0] = contextlib.ExitStack()
        h = es.enter_context(nc.sbuf_tensor("sb_h", [128, 8, OWN], F32))
        hn = es.enter_context(nc.sbuf_tensor("sb_hn", [128, 8, OWN], BF16))
        stage_ref[0] = [sb([128, 2048], F32, "stageC1a"), sb([128, 2048], F32, "stageC1b")]
        wo = sb([128, 8, 1024], BF16, "wo")
        load_w_bf16(wo, w_out, 1024)
        BK = [pA, pB, pC, pLP, pG, pDT, pW, pX]
        for tq_ in range(OWN // 512):
            dma(h[:, :, tq_ * 512:(tq_ + 1) * 512], xT_v[:, :, NT - OWN + tq_ * 512:NT - OWN + (tq_ + 1) * 512])
        for tq_ in range(OWN // 512):
            ts0 = slice(tq_ * 512, (tq_ + 1) * 512)
            for dc in range(8):
                bk = BK[dc % 4]
                for c in range(8):
                    mm(bk, wo[:, c, dc * 128:(dc + 1) * 128], (concA if c < 4 else concB)[:, c % 4, ts0], start=(c == 0), stop=(c == 7))
                tt(h[:, dc, ts0], h[:, dc, ts0], bk, ALU.add)

        def norm_own(gcol):
            for s in range(OWN // 256):
                sl = slice(s * 256, (s + 1) * 256)
                rmsnorm_tile(h[:, :, sl], 0, 256, gcol, hn[:, :, sl])

        norm_own(C_NFFN)
        S.emit()
        PES[0].close()
        PES[0] = contextlib.ExitStack()
        GS = 2
        NG = NFC // GS
        stage_ref[0] = [sb([128, 2048], F32, "stageC2a"), sb([128, 2048], F32, "stageC2b")]
        w1g = [sb([128, 8, GS * 128], BF16, "w1g%d" % i) for i in range(2)]
        w3g = [sb([128, 8, GS * 128], BF16, "w3g%d" % i) for i in range(2)]
        w2g = [sb([128, GS, 1024], BF16, "w2g%d" % i) for i in range(2)]
        gT = [sb([128, GS, 512], BF16, "gT%d" % i) for i in range(2)]
        s1 = [sb([128, 512], F32, "s1_%d" % i) for i in range(2)]

        stg3 = stage_ref[0] + [sb([128, 2048], F32, "stageC2c")]
        _lg = [0]

        def load_group(gi):
            f0 = gi * GS
            fs = slice(f0 * 128, (f0 + GS) * 128)
            v1 = ffn_w1.rearrange("(c p) n -> p c n", p=128)[:, :, fs]
            v3 = ffn_w3.rearrange("(c p) n -> p c n", p=128)[:, :, fs]
            v2 = ffn_w2[fs, :].rearrange("(f p) n -> p f n", p=128)
            for (dst, srcv, shp) in ((w1g[gi % 2], v1, (8, GS * 128)), (w3g[gi % 2], v3, (8, GS * 128)), (w2g[gi % 2], v2, (GS, 1024))):
                st = stg3[_lg[0] % 3]; _lg[0] += 1
                sv_ = st[:].rearrange("p (a b) -> p a b", a=shp[0])
                dma(sv_, srcv)
                act(dst[:], sv_, AF.Copy)

        wg = concB[:].rearrange("p a (b c) -> p (a b) c", c=1024)
        cAf = concA[:].rearrange("p a n -> p (a n)")
        wu = cAf[:, 0:2048].rearrange("p (c n) -> p c n", n=1024)
        pb = cAf[:, 2048:2048 + 2 * OWN].rearrange("p (c n) -> p c n", n=OWN)
        pT_v = pT.rearrange("(c p) t -> p c t", p=128)
        load_group(0)
        k_ = [0]

        def ffn_A(gi, tq_):
            W1, W3 = w1g[gi % 2], w3g[gi % 2]
            ts0 = slice(tq_ * 512, (tq_ + 1) * 512)
            G = gT[tq_ % 2]
            for f in range(GS):
                bA, bB = (pA, pB) if (k_[0] % 2 == 0) else (pG, pDT)
                for c in range(8):
                    mm(bA, W1[:, c, f * 128:(f + 1) * 128], hn[:, c, ts0], start=(c == 0), stop=(c == 7))
                for c in range(8):
                    mm(bB, W3[:, c, f * 128:(f + 1) * 128], hn[:, c, ts0], start=(c == 0), stop=(c == 7))
                act(s1[k_[0] % 2][:], bA, AF.Silu)
                tt(G[:, f, :], s1[k_[0] % 2][:], bB, ALU.mult)
                k_[0] += 1

        def ffn_B(gi, tq_):
            W2 = w2g[gi % 2]
            ts0 = slice(tq_ * 512, (tq_ + 1) * 512)
            G = gT[tq_ % 2]
            for dc in range(8):
                bk = [pC, pLP, pW, pX][dc % 4]
                for f in range(GS):
                    mm(bk, W2[:, f, dc * 128:(dc + 1) * 128], G[:, f, :], start=(f == 0), stop=(f == GS - 1))
                tt(h[:, dc, ts0], h[:, dc, ts0], bk, ALU.add)

        units = [(gi, tq_) for gi in range(NG) for tq_ in range(OWN // 512)]
        for ui, (gi, tq_) in enumerate(units):
            ffn_A(gi, tq_)
            if ui > 0:
                ffn_B(*units[ui - 1])
            if tq_ == 0 and gi + 1 < NG:
                load_group(gi + 1)
            if ui == 6:
                stage_ref[0] = stg3
                load_w_bf16(wg, gate_w, 1024)
                load_w_bf16(wu, up_w, 1024)
                for c in range(2):
                    load_cast(pb[:, c, :], pT_v[:, c, :], OWN)
        ffn_B(*units[-1])
        S.emit()
        PES[0].close()
        PES[0] = contextlib.ExitStack()
        norm_own(C_NPLE)
        s1 = [sb([128, 512], F32, "s1p_%d" % i) for i in range(2)]
        for tq_ in range(OWN // 512):
            ts0 = slice(tq_ * 512, (tq_ + 1) * 512)
            for dc in range(8):
                bA, bB = (pA, pB) if (dc % 2 == 0) else (pG, pDT)
                for c in range(8):
                    mm(bA, wg[:, c, dc * 128:(dc + 1) * 128], hn[:, c, ts0], start=(c == 0), stop=(c == 7))
                for c in range(2):
                    mm(bB, wu[:, c, dc * 128:(dc + 1) * 128], pb[:, c, ts0], start=(c == 0), stop=(c == 1))
                act(s1[dc % 2][:], bA, AF.Sigmoid)
                tt(s1[dc % 2][:], s1[dc % 2][:], bB, ALU.mult)
                tt(h[:, dc, ts0], h[:, dc, ts0], s1[dc % 2][:], ALU.add, eng="pool")
        outv = outT.rearrange("(c p) t -> p c t", p=128)
        obs = [sb([128, 8, 256], F32, "ob%d" % i) for i in range(2)]
        for s in range(OWN // 256):
            sl = slice(s * 256, (s + 1) * 256)
            rmsnorm_tile(h[:, :, sl], 0, 256, C_NFIN, obs[s % 2])
            dma(outv[:, :, sl], obs[s % 2][:])
        S.emit()
        PES[0].close()
    return nc


def _cols(a, n):
    return np.ascontiguousarray(np.asarray(a, np.float32).reshape(n, 128).T)


def kernel(**inp):
    f = lambda k: np.asarray(inp[k], np.float32)
    x = f("x"); p = f("p")[0]
    col = np.zeros((128, 80), np.float32)
    col[:, 0:8] = _cols(f("norm_mix")[0], 8); col[:, 8:16] = _cols(f("norm_ffn")[0], 8)
    col[:, 16:24] = _cols(f("norm_ple")[0], 8); col[:, 24:32] = _cols(f("final_norm"), 8)
    col[:, 32:46] = _cols(f("rw_shift_mu")[0], 14)
    col[:, 46:50] = _cols(f("rw_k_k")[0], 4); col[:, 50:54] = _cols(f("rw_k_a")[0], 4)
    col[:, 54:58] = _cols(f("rw_r_k")[0].reshape(512), 4); col[:, 58:62] = _cols(f("rw_ln_w")[0], 4)
    col[:, 62:66] = _cols(f("rw_ln_b")[0], 4); col[:, 66:70] = _cols(f("rw_a0")[0], 4)
    col[:, 70:74] = _cols(f("s5_glu_b")[0], 4)
    ident = np.eye(128, dtype=np.float32)
    cst = np.zeros((128, 1024), np.float32)
    cst[:, 0:128] = 1.0
    pi = np.arange(128)
    same = (pi[:, None] // 64) == (pi[None, :] // 64)
    cst[:, 128:256] = same
    cst[:, 256:384] = same & (pi[:, None] <= pi[None, :])
    cst[:, 384:512] = same & (pi[:, None] < pi[None, :])
    s = (pi % 64)[:, None]; t = np.arange(64)[None, :]
    lt = (s < t).astype(np.float32); le = (s <= t).astype(np.float32)
    cst[:, 512:576] = lt; cst[:, 576:640] = le; cst[:, 640:704] = lt; cst[:, 704:768] = le
    cst[:, 768:832] = (t < s).astype(np.float32)
    cst[:, 832:896] = (s == t).astype(np.float32)
    c2 = np.zeros((128, 1536), np.float32)
    nmask = (t < s).astype(np.float32)
    for c in range(4):
        c2[:, c * 128:c * 128 + 64] = lt; c2[:, c * 128 + 64:c * 128 + 128] = le
        c2[:, 512 + c * 64:512 + (c + 1) * 64] = nmask
        c2[:, 768 + c * 64:768 + (c + 1) * 64] = (s == t)
    w2aug = np.concatenate([f("rw_w2")[0], f("rw_w0")[0][None, :]], 0)
    a2 = np.zeros((128, 512), np.float32); a2[64:128] = f("rw_a2")[0]
    g2 = f("rw_g2")[0]
    def lb(a):
        return np.ascontiguousarray(a.reshape(16, 2, 64).transpose(1, 2, 0).reshape(128, 16))
    def lb3(a):
        return np.ascontiguousarray(a.reshape(16, 2, 64, 16).transpose(1, 2, 0, 3).reshape(128, 256))
    s5B = np.concatenate([
        lb(f("s5_lam_re")[0]), lb(f("s5_lam_im")[0]), lb(np.repeat(f("s5_log_step")[0][:, None], 64, 1)),
        lb3(f("s5_c_re")[0].transpose(0, 2, 1)), lb3(f("s5_c_im")[0].transpose(0, 2, 1)),
        lb3(f("s5_b_re")[0]), lb3(f("s5_b_im")[0])], 1).astype(np.float32)
    s5c = np.zeros((128, 177), np.float32)
    s5c[:, 0:17] = np.array(list(range(-7, 9)) + [64], np.float32)[None, :]
    s5c[:, 17:49] = np.tile(f("s5_d")[0].reshape(32, 16).T, (8, 1))
    s5c[:, 49:177] = ((pi[None, :] // 16) >= (pi[:, None] // 16)).astype(np.float32)
    common = {
        "w_in": f("w_in")[0], "w_out": f("w_out")[0], "ffn_w1": f("ffn_w1")[0], "ffn_w3": f("ffn_w3")[0],
        "ffn_w2": f("ffn_w2")[0], "gate_w": f("ple_gate_w")[0], "up_w": f("ple_up_w")[0], "glu_w": f("s5_glu_w")[0],
        "cols": col, "ident": ident, "consts": cst, "w2aug": w2aug, "a2": a2, "g2": g2, "c2": c2, "s5B": s5B, "s5c": s5c,
    }
    in_maps = []
    for core in range(8):
        b, hf = core // 2, core % 2
        xw = np.zeros((D, NT), np.float32)
        if hf == 0:
            xw[:, OWN:] = x[b, 0:OWN].T
        else:
            xw[:, :] = x[b].T
        m = dict(common)
        m["xT"] = xw
        m["pT"] = np.ascontiguousarray(p[b, hf * OWN:(hf + 1) * OWN].T)
        in_maps.append(m)
    nc = build()
    res = run_bass_kernel_spmd(nc, in_maps, core_ids=list(range(8)))
    out = np.zeros((4, 4096, D), np.float32)
    for core in range(8):
        b, hf = core // 2, core % 2
        out[b, hf * OWN:(hf + 1) * OWN] = res.results[core]["outT"].T
    return out
```

```python
import contextlib
import numpy as np
import concourse.bass as bass
import concourse.mybir as mybir
from concourse.bass_utils import run_bass_kernel_spmd

F32 = mybir.dt.float32
BF16 = mybir.dt.bfloat16
I32 = mybir.dt.int32
ALU = mybir.AluOpType
AF = mybir.ActivationFunctionType

D = 1024
NT = 4096
OWN = 2048
FF = 2816
NFC = 22
EPS = 1e-6
TWO_PI = 6.283185307179586
PI = 3.141592653589793


class Sched:
    ENG = ("pe", "dve", "act", "pool", "sp")

    def __init__(self, nc, es):
        self.nc = nc
        self.es = es
        self.ops = []
        self.lw = {}
        self.rd = {}
        self.count = {e: 0 for e in self.ENG}
        self.sem = {e: es.enter_context(nc.semaphore("s_" + e)) for e in ("pe", "dve", "act", "pool")}
        self.NSLOT = 8
        self.dsem = [es.enter_context(nc.semaphore("s_dma%d" % i)) for i in range(self.NSLOT)]
        self.ndma = 0

    region_fn = staticmethod(lambda col: col // 512)

    def key(self, ap):
        n = ap.name
        n = n() if callable(n) else n
        if n == "pALL":
            return ("psum", self.region_fn(ap.offset % 4096))
        return n

    def op(self, eng, fn, outs, ins, keys_out=None, keys_in=None):
        ko = [self.key(a) for a in outs] if keys_out is None else keys_out
        ki = [self.key(a) for a in ins] if keys_in is None else keys_in
        ko = list(ko) + [k for k in ki if isinstance(k, tuple) and k[0] == "psum" and k not in ko]
        ki = [k for k in ki if not (isinstance(k, tuple) and k[0] == "psum")]
        deps = set()
        for k in ki:
            if k in self.lw:
                deps.add(self.lw[k])
        for k in ko:
            if k in self.lw:
                deps.add(self.lw[k])
            deps.update(self.rd.get(k, ()))
        idx = len(self.ops)
        if eng == "sp":
            slot = self.ndma % self.NSLOT
            val = 16 * (self.ndma // self.NSLOT + 1)
            self.ndma += 1
            sig = (self.dsem[slot], val, True)
        else:
            self.count[eng] += 1
            sig = (self.sem[eng], self.count[eng], False)
        self.ops.append((eng, fn, deps, sig))
        for k in ki:
            self.rd.setdefault(k, []).append(idx)
        for k in ko:
            self.lw[k] = idx
            self.rd[k] = []
        return idx

    def emit(self):
        nc = self.nc
        start = getattr(self, "emitted", 0)
        prev_tot = getattr(self, "prev_tot", {})
        self.emitted = len(self.ops)
        tot_now = {}
        for (eng, fn, deps, sig) in self.ops:
            tot_now[id(sig[0])] = (sig[0], max(tot_now.get(id(sig[0]), (None, 0))[1], sig[1]))
        self.prev_tot = tot_now
        with nc.Block() as block:
            def run(ename):
                def body(e):
                    waited = {}
                    last = None
                    for kk, (s, v) in prev_tot.items():
                        e.wait_ge(s, v)
                        waited[kk] = v
                    for (eng, fn, deps, sig) in self.ops[start:]:
                        if eng != ename:
                            continue
                        need = {}
                        for d in deps:
                            deng, _, _, dsig = self.ops[d]
                            if deng == "pe" and ename == "pe":
                                continue
                            s, v, isd = dsig
                            kk = id(s)
                            if waited.get(kk, 0) >= v:
                                continue
                            if kk not in need or need[kk][1] < v:
                                need[kk] = (s, v)
                        if sig[2]:
                            s, v, _ = sig
                            if v > 16 and waited.get(id(s), 0) < v - 16:
                                if id(s) not in need or need[id(s)][1] < v - 16:
                                    need[id(s)] = (s, v - 16)
                        for kk, (s, v) in need.items():
                            e.wait_ge(s, v)
                            waited[kk] = v
                        ins = fn(e)
                        ins.then_inc(sig[0], 16 if sig[2] else 1)
                        last = sig
                    if ename == "sp":
                        tot = {}
                        for (eng, fn, deps, sig) in self.ops:
                            if eng == "sp":
                                tot[id(sig[0])] = (sig[0], max(tot.get(id(sig[0]), (None, 0))[1], sig[1]))
                        for kk, (s, v) in tot.items():
                            e.wait_ge(s, v)
                return body
            block.tensor(run("pe"))
            block.vector(run("dve"))
            block.scalar(run("act"))
            block.gpsimd(run("pool"))
            block.sync(run("sp"))


def build():
    nc = bass.Bass("TRN2", target_bir_lowering=False)
    es = contextlib.ExitStack()
    with es:
        S = Sched(nc, es)

        def din(name, shape, dt=F32):
            return nc.dram_tensor(name, list(shape), dt, kind="ExternalInput").ap()

        xT = din("xT", [D, NT])
        pT = din("pT", [256, OWN])
        outT = nc.dram_tensor("outT", [D, OWN], F32, kind="ExternalOutput").ap()
        w_in = din("w_in", [D, 2304])
        w_out = din("w_out", [D, D])
        ffn_w1 = din("ffn_w1", [D, FF])
        ffn_w3 = din("ffn_w3", [D, FF])
        ffn_w2 = din("ffn_w2", [FF, D])
        gate_w = din("gate_w", [D, D])
        up_w = din("up_w", [256, D])
        glu_w = din("glu_w", [512, 512])
        cols = din("cols", [128, 80])
        ident_d = din("ident", [128, 128])
        consts_d = din("consts", [128, 1024])
        w2aug_d = din("w2aug", [65, 512])
        a2_d = din("a2", [128, 512])
        g2_d = din("g2", [128, 512])
        c2_d = din("c2", [128, 1536])
        s5B_d = din("s5B", [128, 48 + 4 * 256])
        s5c_d = din("s5c", [128, 17 + 32 + 128])

        _n = [0]
        PES = [es]

        def sb(shape, dt=F32, name=None):
            _n[0] += 1
            return PES[0].enter_context(nc.sbuf_tensor(("sb_" + name) if name else ("anon%d" % _n[0]), list(shape), dt))

        def psum(shape, name=None):
            _n[0] += 1
            return es.enter_context(nc.psum_tensor(name or ("p%d" % _n[0]), list(shape), F32))

        def dma(out, in_):
            S.op("sp", lambda e: e.dma_start(out=out, in_=in_), [out], [in_])

        def mm(out, lhsT, rhs, start=True, stop=True):
            S.op("pe", lambda e: e.matmul(out, lhsT, rhs, start=start, stop=stop), [out], [lhsT, rhs])

        def tr(out, in_, ident):
            S.op("pe", lambda e: e.matmul(out, in_, ident, start=True, stop=True), [out], [in_, ident])

        def trT(out, in_, ident_):
            S.op("pe", lambda e: e.transpose(out, in_, ident_), [out], [in_, ident_])

        def act(out, in_, func, bias=None, scale=1.0, eng="act"):
            ins = [in_] + ([bias] if (bias is not None and not isinstance(bias, float)) else []) + \
                ([scale] if not isinstance(scale, float) else [])
            kw = {}
            if bias is not None:
                kw["bias"] = bias
            S.op("act", lambda e: e.activation(out, in_, func, scale=scale, **kw), [out], ins)

        def tt(out, a, b, op, eng="dve"):
            S.op(eng, lambda e: e.tensor_tensor(out, a, b, op), [out], [a, b])

        def ts(out, a, s1, s2, op0, op1=None, eng="dve"):
            ins = [a] + [s for s in (s1, s2) if s is not None and not isinstance(s, (float, int))]
            if op1 is None:
                S.op(eng, lambda e: e.tensor_scalar(out, a, s1, None, op0), [out], ins)
            else:
                S.op(eng, lambda e: e.tensor_scalar(out, a, s1, s2, op0, op1), [out], ins)

        def stt(out, a, s, b, op0, op1, eng="dve"):
            ins = [a, b] + ([s] if not isinstance(s, (float, int)) else [])
            S.op(eng, lambda e: e.scalar_tensor_tensor(out, a, s, b, op0, op1), [out], ins)

        def cp(out, in_, eng="dve"):
            S.op(eng, lambda e: e.tensor_copy(out, in_), [out], [in_])

        def ecp(i, out, in_):
            if i % 2:
                act(out, in_, AF.Copy)
            else:
                cp(out, in_)

        def acopy(out, in_):
            act(out, in_, AF.Copy)

        def memset(ap, v, eng="dve"):
            S.op(eng, lambda e: e.memset(ap, v), [ap], [])

        def recip(out, in_):
            S.op("dve", lambda e: e.reciprocal(out, in_), [out], [in_])

        ident = sb([128, 128])
        dma(ident[:], ident_d)
        cst = sb([128, 1024])
        dma(cst[:], consts_d)
        ones = cst[:, 0:128]
        blk1 = cst[:, 128:256]
        tri_i = cst[:, 256:384]
        tri_e = cst[:, 384:512]
        mask_g = cst[:, 512:832]
        col = sb([128, 80])
        dma(col[:], cols)
        C_NMIX, C_NFFN, C_NPLE, C_NFIN = 0, 8, 16, 24
        C_MU = 32
        C_KK, C_KA, C_RK, C_LNW, C_LNB, C_A0, C_GLUB = 46, 50, 54, 58, 62, 66, 70


        pALL = es.enter_context(nc.psum_tensor("pALL", [128, 4096], F32))
        pA, pB, pC, pLP, pG, pDT, pW, pX = [pALL[:, i * 512:(i + 1) * 512] for i in range(8)]

        concB = sb([128, 4, OWN], BF16, "concB")
        xt = sb([128, 8, 256], F32, "xt")
        sqb = [sb([128, 256], F32, "sq%d" % i) for i in range(2)]
        rstd = sb([128, 256], F32, "rstd")
        xT_v = xT.rearrange("(c p) t -> p c t", p=128)

        def rmsnorm_tile(src_tile, t0, n, gcol, dst):
            pss = pALL[:, 2560:2560 + n]
            for c in range(8):
                act(sqb[c % 2][:, 0:n], src_tile[:, c, 0:n], AF.Square)
                mm(pss, ones, sqb[c % 2][:, 0:n], start=(c == 0), stop=(c == 7))
            ts(rstd[:, 0:n], pss, 1.0 / D, EPS, ALU.mult, ALU.add)
            act(rstd[:, 0:n], rstd[:, 0:n], AF.Sqrt)
            recip(rstd[:, 0:n], rstd[:, 0:n])
            for c in range(8):
                stt(dst[:, c, 0:n], src_tile[:, c, 0:n], col[:, gcol + c:gcol + c + 1], rstd[:, 0:n], ALU.mult, ALU.mult)

        stage_ref = [None]

        _ld = [0]

        def load_cast(dst, src_ap, ncols):
            st = stage_ref[0]
            step = (st[0] if isinstance(st, list) else st).shape[1]
            for n0 in range(0, ncols, step):
                n1 = min(ncols, n0 + step)
                k = _ld[0]; _ld[0] += 1
                stage = st[k % len(st)] if isinstance(st, list) else st
                dma(stage[:, 0:n1 - n0], src_ap[:, n0:n1])
                if isinstance(st, list) and k % 2 == 1:
                    act(dst[:, n0:n1], stage[:, 0:n1 - n0], AF.Copy)
                else:
                    cp(dst[:, n0:n1], stage[:, 0:n1 - n0], eng="pool")

        def load_w_bf16(dst, src_ap, ncols):
            kc = src_ap.shape[0] // 128
            v = src_ap.rearrange("(c p) n -> p c n", p=128)
            for c in range(kc):
                load_cast(dst[:, c, :], v[:, c, :], ncols)

        if PES[0] is not es:
            S.emit()
            PES[0].close()
        PES[0] = contextlib.ExitStack()
        CW = 0.6065306597126334
        win_rw = sb([128, 8, 1792], BF16, "win_rw")
        _phB = PES[0]
        PES[0] = contextlib.ExitStack()
        stage_ref[0] = [sb([128, 2048], F32, "stageBa"), sb([128, 2048], F32, "stageBb")]
        load_w_bf16(win_rw, w_in[:, 512:2304], 1792)
        S.emit()
        PES[0].close()
        PES[0] = _phB
        w2aug = sb([65, 512], F32, "w2aug"); dma(w2aug[:], w2aug_d)
        a2sb = sb([128, 512], F32, "a2sb"); dma(a2sb[:], a2_d)
        g2sb = sb([128, 512], F32, "g2sb"); dma(g2sb[:], g2_d)
        c2 = sb([128, 1024], F32, "c2masks"); dma(c2[:], c2_d[:, 0:1024])
        maskLE4 = c2[:, 0:512].rearrange("p (c n) -> p c n", n=128)
        maskN4 = c2[:, 512:768].rearrange("p (c n) -> p c n", n=64)
        istk4 = c2[:, 768:1024].rearrange("p (c n) -> p c n", n=64)
        xn = sb([128, 8, 256], BF16, "xn")
        z = [sb([128, 257], F32, "z%d" % m) for m in range(14)]
        tmpb = [sb([128, 256], F32, "tmpb%d" % i) for i in range(2)]
        tw = sb([65, 256], F32, "tw"); memset(tw[64:65, :], 1.0)
        sgt = [sb([128, 512], F32, "sgt%d" % i) for i in range(2)]
        sgl = sb([128, 256], F32, "sgl")
        t1s = sb([128, 256], F32, "t1s")
        R4 = range(4)
        P_ = [sb([128, 256], F32, "P_%d" % i) for i in R4]
        g_ = [sb([128, 256], F32, "g_%d" % i) for i in R4]
        ar = [sb([128, 4, 128], F32, "ar%d" % i) for i in R4]
        bT = [sb([128, 4, 64], F32, "bT%d" % i) for i in R4]
        kT = [sb([128, 4, 64], F32, "kT%d" % i) for i in R4]
        bonus = [sb([128, 256], F32, "bonus%d" % i) for i in R4]
        tok = [sb([128, 4, 192], F32, "tok%d" % i) for i in R4]
        gmA = [sb([128, 4, 128], F32, "gmA%d" % i) for i in R4]
        gmN = [sb([128, 4, 64], F32, "gmN%d" % i) for i in R4]
        gmB = [sb([128, 4, 128], F32, "gmB%d" % i) for i in R4]
        nzb = [[sb([128, 4, 128], F32, "nzb%d_%d" % (i, j)) for j in range(2)] for i in R4]
        inj = [sb([128, 4, 64], F32, "inj%d" % i) for i in R4]
        xtb = [[sb([128, 4, 64], F32, "xtb%d_%d" % (i, j)) for j in range(2)] for i in R4]
        Wsb = [sb([128, 64], F32, "Wsb%d" % i) for i in R4]
        Usb = [sb([128, 64], F32, "Usb%d" % i) for i in R4]
        yT = [sb([128, 256], F32, "yT%d" % i) for i in R4]
        ST = [sb([128, 64], F32, "ST%d" % h) for h in R4]
        for h in R4:
            memset(ST[h][:], 0.0)
        yc = sb([128, 256], F32, "yc")
        HS = (slice(0, 64), slice(64, 128))
        p_in = pALL[:, 2048:2304]
        p_sgt = pALL[:, 2560:3072]
        p_lp = pALL[:, 3072:3584]
        p_x0 = pALL[:, 3584:3840]; p_x1 = pALL[:, 3840:4096]

        def pset(hp):
            return pALL[:, hp * 512:(hp + 1) * 512].rearrange("p (c n) -> p c n", n=128)

        def acopy(out, in_):
            act(out, in_, AF.Copy)

        zc = sb([128, 14], F32, "zc"); memset(zc[:], 0.0)

        def rms_in(tj_):
            dma(xt[:], xT_v[:, :, tj_ * 256:tj_ * 256 + 256])
            rmsnorm_tile(xt, 0, 256, C_NMIX, xn)

        def inproj_chunk(m):
            cp(z[m][:, 0:1], zc[:, m:m + 1], eng="pool")
            p_in = pALL[:, 2048 + (m % 2) * 512:2048 + (m % 2) * 512 + 256]
            for c in range(8):
                mm(p_in, win_rw[:, c, m * 128:(m + 1) * 128], xn[:, c, :], start=(c == 0), stop=(c == 7))
            acopy(z[m][:, 1:257], p_in)
            tb_ = tmpb[m % 2]
            tt(tb_[:], z[m][:, 0:256], z[m][:, 1:257], ALU.subtract, eng="pool")
            cp(zc[:, m:m + 1], z[m][:, 256:257], eng="pool")
            stt(z[m][:, 1:257], tb_[:], col[:, C_MU + m:C_MU + m + 1], z[m][:, 1:257], ALU.mult, ALU.add)

        NTILE = NT // 256
        for tj in range(NTILE):
            t0 = tj * 256
            own = t0 >= NT - OWN
            o0 = t0 - (NT - OWN)
            if tj == 0:
                rms_in(0)
                for m in range(14):
                    inproj_chunk(m)
            zs = [zz[:, 1:257] for zz in z]
            act(tw[0:64, :], zs[12][0:64, :], AF.Tanh)
            for tb in range(2):
                mm(p_sgt, tw[:, tb * 128:(tb + 1) * 128], w2aug[:], True, True)
                act(sgt[tb][:], p_sgt, AF.Sigmoid)
            act(sgl[:], zs[13], AF.Sigmoid)
            fl = lambda t_: t_[:].rearrange("p c n -> p (c n)")
            Pe = [fl(gmA[hp])[:, 0:256] for hp in R4]; Pi = [fl(gmA[hp])[:, 256:512] for hp in R4]
            a_ = [fl(gmB[hp])[:, 0:256] for hp in R4]; kk = [fl(gmB[hp])[:, 256:512] for hp in R4]
            rn = [fl(nzb[hp][0])[:, 0:256] for hp in R4]; kp = [fl(nzb[hp][0])[:, 256:512] for hp in R4]
            t1 = [fl(tok[hp])[:, 0:256] for hp in R4]
            HC = [slice(hp * 128, (hp + 1) * 128) for hp in R4]
            plp = [pALL[:, hp * 512:(hp + 1) * 512] for hp in R4]
            px0 = [pALL[:, 2048 + hp * 512:2048 + hp * 512 + 256] for hp in R4]
            px1 = [pALL[:, 2048 + hp * 512 + 256:2048 + hp * 512 + 512] for hp in R4]
            v4 = lambda a: a.rearrange("p (c t) -> p c t", t=64)
            for hp in R4:
                for tb in range(2):
                    mm(plp[hp][:, tb * 128:(tb + 1) * 128], sgt[tb][:, HC[hp]], tri_i, True, True)
                    mm(plp[hp][:, 256 + tb * 128:256 + (tb + 1) * 128], sgt[tb][:, HC[hp]], tri_e, True, True)
            for hp in R4:
                mm(px0[hp], a2sb[64:128, HC[hp]], zs[12][64:128, :], True, True)
                mm(px1[hp], g2sb[:, HC[hp]], sgl[:], True, True)
            for hp in R4:
                ts(kk[hp], zs[4 + hp], col[:, C_KK + hp:C_KK + hp + 1], None, ALU.mult)
                tt(rn[hp], kk[hp], kk[hp], ALU.mult, eng="pool")
            for hp in R4:
                act(P_[hp][:], plp[hp][:, 0:256], AF.Exp, scale=-CW)
                act(Pi[hp], plp[hp][:, 0:256], AF.Exp, scale=CW)
                act(Pe[hp], plp[hp][:, 256:512], AF.Exp, scale=-CW)
            for hp in R4:
                act(a_[hp], px0[hp], AF.Sigmoid, bias=col[:, C_A0 + hp:C_A0 + hp + 1])
            for hp in R4:
                acopy(g_[hp][:], px1[hp])
            for hp in R4:
                mm(px0[hp], blk1, rn[hp], True, True)
            for hp in R4:
                ts(rn[hp], px0[hp], 1e-24, None, ALU.max)
            for hp in R4:
                act(rn[hp], rn[hp], AF.Sqrt)
            for hp in R4:
                recip(rn[hp], rn[hp])
                tt(kk[hp], kk[hp], rn[hp], ALU.mult)
                ts(t1[hp], a_[hp], -1.0, col[:, C_KA + hp:C_KA + hp + 1], ALU.add, ALU.mult)
                stt(kp[hp], t1[hp], 1.0, zs[4 + hp], ALU.add, ALU.mult)
            for hp in R4:
                tt(ar[hp][:, :, 64:128], v4(zs[hp]), v4(P_[hp][:]), ALU.mult, eng="pool")
            for hp in R4:
                stt(ar[hp][:, :, 0:64], v4(kk[hp]), -1.0, v4(Pe[hp]), ALU.mult, ALU.mult)
                tt(t1[hp], kk[hp], a_[hp], ALU.mult)
                tt(bT[hp][:].rearrange("p c t -> p (c t)"), t1[hp], Pi[hp], ALU.mult)
                tt(kT[hp][:].rearrange("p c t -> p (c t)"), kp[hp], Pi[hp], ALU.mult)
            if own:
                for hp in R4:
                    stt(t1[hp], zs[hp], col[:, C_RK + hp:C_RK + hp + 1], kp[hp], ALU.mult, ALU.mult)
                for hp in R4:
                    mm(px0[hp], blk1, t1[hp], True, True)
                for hp in R4:
                    tt(bonus[hp][:], px0[hp], zs[8 + hp], ALU.mult)

            def st_T1(hp):
                ps = pset(hp); v_ = zs[8 + hp]
                for c in range(4):
                    cs = slice(c * 64, (c + 1) * 64)
                    for h in range(2):
                        hs = HS[h]
                        tr(ps[hs, c, 0:64], v_[hs, cs], ident[hs, hs])
                        tr(ps[hs, c, 64:128], bT[hp][hs, c, :], ident[hs, hs])
                acopy(tok[hp][:, :, 0:128], ps)

            def st_T2(hp):
                ps = pset(hp)
                for c in range(4):
                    for h in range(2):
                        hs = HS[h]
                        tr(ps[hs, c, 0:64], kT[hp][hs, c, :], ident[hs, hs])
                        mm(ps[hs, c, 64:128], ar[hp][hs, c, 0:64], bT[hp][hs, c, :], True, True)
                acopy(tok[hp][:, :, 128:192], ps[:, :, 0:64])
                tt(gmN[hp][:], ps[:, :, 64:128], maskN4, ALU.mult)

            def st_G1(hp):
                ps = pset(hp)
                for c in range(4):
                    for h in range(2):
                        hs = HS[h]
                        mm(ps[hs, c, 0:128], bT[hp][hs, c, :], ar[hp][hs, c, :], True, True)
                tt(gmA[hp][:], ps, maskLE4, ALU.mult)
                tt(xtb[hp][0][:], gmA[hp][:, :, 0:64], istk4, ALU.add, eng="pool")

            def st_G2(hp):
                ps = pset(hp)
                for c in range(4):
                    for h in range(2):
                        hs = HS[h]
                        mm(ps[hs, c, 0:128], kT[hp][hs, c, :], ar[hp][hs, c, :], True, True)
                tt(gmB[hp][:], ps, maskLE4, ALU.mult)

            def mk_level(j):
                def sa(hp):
                    ps = pset(hp)
                    if j == 1:
                        Zc, Nc = gmA[hp][:, :, 0:64], gmN[hp][:]
                    else:
                        Nc, Zc = nzb[hp][j % 2][:, :, 0:64], nzb[hp][j % 2][:, :, 64:128]
                    for c in range(4):
                        for h in range(2):
                            hs = HS[h]
                            mm(ps[hs, c, 0:64], Zc[hs, c, :], Nc[hs, c, :], True, True)
                            if j < 5:
                                mm(ps[hs, c, 64:128], Nc[hs, c, :], Zc[hs, c, :], True, True)
                    tt(inj[hp][:], ps[:, :, 0:64], istk4, ALU.add)
                    if j < 5:
                        acopy(nzb[hp][(j + 1) % 2][:], ps)

                def sb_(hp):
                    ps = pset(hp)
                    XTc = xtb[hp][(j - 1) % 2]
                    for c in range(4):
                        for h in range(2):
                            hs = HS[h]
                            mm(ps[hs, c, 0:64], inj[hp][hs, c, :], XTc[hs, c, :], True, True)
                    acopy(xtb[hp][j % 2][:], ps[:, :, 0:64])
                return sa, sb_

            stages = [st_T1, st_T2, st_G1, st_G2]
            for j in range(1, 6):
                stages.extend(mk_level(j))
            for si, stg in enumerate(stages):
                for hp in R4:
                    stg(hp)
                if tj + 1 < NTILE:
                    if si == 0:
                        rms_in(tj + 1)
                    else:
                        inproj_chunk(si - 1)
            if tj + 1 < NTILE:
                inproj_chunk(13)
            for c in range(4):
                cs = slice(c * 64, (c + 1) * 64)
                for hp in R4:
                    pw = pALL[:, 2048 + hp * 512:2048 + hp * 512 + 64]
                    for h in range(2):
                        hs = HS[h]
                        mm(pw[hs], ar[hp][hs, c, 0:64], ST[hp][hs, :], True, False)
                        mm(pw[hs], gmB[hp][hs, c, 0:64], tok[hp][hs, c, 0:64], False, True)
                    cp(Wsb[hp][:], pw)
                for hp in R4:
                    pu = pALL[:, 2048 + hp * 512 + 64:2048 + hp * 512 + 128]
                    for h in range(2):
                        hs = HS[h]
                        mm(pu[hs], xtb[hp][1][hs, c, :], Wsb[hp][hs, :], True, True)
                    acopy(Usb[hp][:], pu)
                for hp in R4:
                    py = pALL[:, 2048 + hp * 512 + 128:2048 + hp * 512 + 192]
                    pst = pALL[:, 2048 + hp * 512 + 192:2048 + hp * 512 + 256]
                    Vt, Bt, Kt = tok[hp][:, c, 0:64], tok[hp][:, c, 64:128], tok[hp][:, c, 128:192]
                    for h in range(2):
                        hs = HS[h]
                        mm(pst[hs], ident[hs, hs], ST[hp][hs, :], True, False)
                        mm(pst[hs], Bt[hs], Usb[hp][hs, :], False, False)
                        mm(pst[hs], Kt[hs], Vt[hs], False, True)
                    if own:
                        for h in range(2):
                            hs = HS[h]
                            mm(py[hs], ST[hp][hs, :], ar[hp][hs, c, 64:128], True, False)
                            mm(py[hs], Usb[hp][hs, :], gmA[hp][hs, c, 64:128], False, False)
                            mm(py[hs], Vt[hs], gmB[hp][hs, c, 64:128], False, True)
                        acopy(yT[hp][:, cs], py)
                    ts(ST[hp][:], pst, P_[hp][:, c * 64 + 63:c * 64 + 64], None, ALU.mult)
            if own:
                for hp in R4:
                    mm(p_x0, blk1, yT[hp][:], True, True)
                    stt(yc[:], p_x0, -1.0 / 64, yT[hp][:], ALU.mult, ALU.add)
                    tt(t1s[:], yc[:], yc[:], ALU.mult, eng="pool")
                    mm(p_x1, blk1, t1s[:], True, True)
                    ts(t1s[:], p_x1, 1.0 / 64, 64e-5, ALU.mult, ALU.add)
                    act(t1s[:], t1s[:], AF.Sqrt)
                    recip(t1s[:], t1s[:])
                    tt(yc[:], yc[:], t1s[:], ALU.mult)
                    ts(yc[:], yc[:], col[:, C_LNW + hp:C_LNW + hp + 1], col[:, C_LNB + hp:C_LNB + hp + 1], ALU.mult, ALU.add)
                    tt(yc[:], yc[:], bonus[hp][:], ALU.add, eng="pool")
                    tt(concB[:, hp, o0:o0 + 256], yc[:], g_[hp][:], ALU.mult, eng="pool")

        if PES[0] is not es:
            S.emit()
            PES[0].close()
        PES[0] = contextlib.ExitStack()
        concA = es.enter_context(nc.sbuf_tensor("sb_concA", [128, 4, OWN], BF16))
        NPW = 17
        s5c = sb([128, 177], F32, "s5c"); dma(s5c[:], s5c_d)
        pr = sb([128, 16, NPW], F32, "pr"); pi_ = sb([128, 16, NPW], F32, "pi_")
        WCre = sb([128, 16, 128], BF16, "WCre"); WCim = sb([128, 16, 128], BF16, "WCim")
        WEre = sb([128, 32, 64], BF16, "WEre"); WEim = sb([128, 32, 64], BF16, "WEim"); Toep = sb([128, 32, 128], BF16, "Toep")
        identb = sb([128, 128], BF16, "identb"); cp(identb[:], ident[:])
        win_s5 = sb([128, 8, 512], BF16, "win_s5")
        glub = sb([128, 4, 512], BF16, "glub")
        _phaseA = PES[0]
        PES[0] = contextlib.ExitStack()
        stage_ref[0] = [sb([128, 1024], F32, "stageAa"), sb([128, 1024], F32, "stageAb")]
        load_w_bf16(win_s5, w_in[:, 0:512], 512)
        load_w_bf16(glub, glu_w, 512)
        s5B = sb([128, 1072], F32, "s5B"); dma(s5B[:], s5B_d)
        lre, lim, lst = s5B[:, 0:16], s5B[:, 16:32], s5B[:, 32:48]
        cre = s5B[:, 48:304].rearrange("p (g c) -> p g c", c=16)
        cim = s5B[:, 304:560].rearrange("p (g c) -> p g c", c=16)
        bre = s5B[:, 560:816].rearrange("p (g c) -> p g c", c=16)
        bim = s5B[:, 816:1072].rearrange("p (g c) -> p g c", c=16)
        mtab = s5c[:, 0:17]; dcol = s5c[:, 17:49]; maskT = s5c[:, 49:177]
        dtt = sb([128, 16], F32, "dtt"); are = sb([128, 16], F32, "are"); aim = sb([128, 16], F32, "aim")
        act(dtt[:], lst, AF.Exp)
        tt(are[:], lre, dtt[:], ALU.mult)
        tt(aim[:], lim, dtt[:], ALU.mult)
        IDX0 = lambda m: (m + 7) if m != 64 else 16
        m1 = sb([128, 16], F32, "m1"); rr = sb([128, 16], F32, "rr"); acc_ = sb([128, 16], F32, "acc_")
        sn = sb([128, 16], F32, "sn"); cs_ = sb([128, 16], F32, "cs_"); mg = sb([128, 16], F32, "mg")
        act(mg[:], are[:], AF.Exp)

        def sin_small(dst, shift):
            ts(rr[:], aim[:], shift, None, ALU.add)
            memset(acc_[:], 0.0)
            for k in range(1, 7):
                ts(m1[:], rr[:], (2 * k - 1) * PI, -TWO_PI, ALU.is_gt, ALU.mult)
                tt(acc_[:], acc_[:], m1[:], ALU.add)
            tt(rr[:], rr[:], acc_[:], ALU.add)
            act(dst, rr[:], AF.Sin)

        sin_small(sn[:], 0.0)
        sin_small(cs_[:], PI / 2)
        P1r, P1i = pr[:, :, IDX0(1)], pi_[:, :, IDX0(1)]
        tt(P1r, cs_[:], mg[:], ALU.mult)
        tt(P1i, sn[:], mg[:], ALU.mult)
        memset(pr[:, :, IDX0(0)], 1.0); memset(pi_[:, :, IDX0(0)], 0.0)
        cm1 = sb([128, 16], F32, "cm1"); cm2 = sb([128, 16], F32, "cm2")

        def cmul_s(dr, di, ar_, ai_, br_, bi_):
            tt(cm1[:], ar_, br_, ALU.mult); tt(cm2[:], ai_, bi_, ALU.mult); tt(dr, cm1[:], cm2[:], ALU.subtract)
            tt(cm1[:], ar_, bi_, ALU.mult); tt(cm2[:], ai_, br_, ALU.mult); tt(di, cm1[:], cm2[:], ALU.add)

        for m in range(2, 9):
            cmul_s(pr[:, :, IDX0(m)], pi_[:, :, IDX0(m)], pr[:, :, IDX0(m - 1)], pi_[:, :, IDX0(m - 1)], P1r, P1i)
        ivr = sb([128, 16], F32, "ivr"); ivi = sb([128, 16], F32, "ivi")
        tt(cm1[:], mg[:], mg[:], ALU.mult)
        recip(cm1[:], cm1[:])
        tt(ivr[:], P1r, cm1[:], ALU.mult)
        stt(ivi[:], P1i, -1.0, cm1[:], ALU.mult, ALU.mult)
        cp(pr[:, :, IDX0(-1)], ivr[:]); cp(pi_[:, :, IDX0(-1)], ivi[:])
        for m in range(2, 8):
            cmul_s(pr[:, :, IDX0(-m)], pi_[:, :, IDX0(-m)], pr[:, :, IDX0(-(m - 1))], pi_[:, :, IDX0(-(m - 1))], ivr[:], ivi[:])
        s16r = sb([128, 16], F32, "s16r"); s16i = sb([128, 16], F32, "s16i")
        s32r = sb([128, 16], F32, "s32r"); s32i = sb([128, 16], F32, "s32i")
        cmul_s(s16r[:], s16i[:], pr[:, :, IDX0(8)], pi_[:, :, IDX0(8)], pr[:, :, IDX0(8)], pi_[:, :, IDX0(8)])
        cmul_s(s32r[:], s32i[:], s16r[:], s16i[:], s16r[:], s16i[:])
        cmul_s(pr[:, :, IDX0(64)], pi_[:, :, IDX0(64)], s32r[:], s32i[:], s32r[:], s32i[:])
        IDX = lambda m: (m + 7) if m != 64 else 16
        qn = sb([128, 16], F32, "qn"); qre = sb([128, 16], F32, "qre"); qim = sb([128, 16], F32, "qim")
        den = sb([128, 16], F32, "den"); lb1 = sb([128, 16], F32, "lb1"); q2 = sb([128, 16], F32, "q2")
        ts(lb1[:], pr[:, :, IDX(1)], -1.0, None, ALU.add)
        tt(den[:], lre, lre, ALU.mult); tt(q2[:], lim, lim, ALU.mult); tt(den[:], den[:], q2[:], ALU.add)
        recip(den[:], den[:])
        tt(qre[:], lb1[:], lre, ALU.mult); tt(q2[:], pi_[:, :, IDX(1)], lim, ALU.mult); tt(qre[:], qre[:], q2[:], ALU.add)
        tt(qre[:], qre[:], den[:], ALU.mult)
        tt(qim[:], pi_[:, :, IDX(1)], lre, ALU.mult); tt(q2[:], lb1[:], lim, ALU.mult); tt(qim[:], qim[:], q2[:], ALU.subtract)
        tt(qim[:], qim[:], den[:], ALU.mult)
        bbr = sb([128, 16, 16], F32, "bbr"); bbi = sb([128, 16, 16], F32, "bbi"); tq = sb([128, 16, 16], F32, "tq")
        bq = lambda a: a.unsqueeze(2).to_broadcast([128, 16, 16])
        tt(bbr[:], bre, bq(qre[:]), ALU.mult); tt(tq[:], bim, bq(qim[:]), ALU.mult); tt(bbr[:], bbr[:], tq[:], ALU.subtract)
        tt(bbi[:], bim, bq(qre[:]), ALU.mult); tt(tq[:], bre, bq(qim[:]), ALU.mult); tt(bbi[:], bbi[:], tq[:], ALU.add)
        QC9r = sb([128, 16, 144], F32, "QC9r"); QC9i = sb([128, 16, 144], F32, "QC9i")
        ctmp9 = sb([128, 16, 144], F32, "ctmp9")
        PBre = sb([128, 16, 128], F32, "PBre"); PBimn = sb([128, 16, 128], F32, "PBimn")
        P7re = sb([128, 16, 128], F32, "P7re"); P7im = sb([128, 16, 128], F32, "P7im")

        prN = sb([128, 16, 8], F32, "prN"); piN = sb([128, 16, 8], F32, "piN")
        pr7 = sb([128, 16, 8], F32, "pr7"); pi7 = sb([128, 16, 8], F32, "pi7")
        for j in range(8):
            cp(prN[:, :, j], pr[:, :, IDX(-j)], eng="pool"); cp(piN[:, :, j], pi_[:, :, IDX(-j)], eng="pool")
            cp(pr7[:, :, j], pr[:, :, IDX(7 - j)], eng="pool"); cp(pi7[:, :, j], pi_[:, :, IDX(7 - j)], eng="pool")
        ctmp = sb([128, 16, 128], F32, "ctmp")

        def cmulv(Tre, Tim, sre, sim, lr8, li8, neg_im=False):
            v = lambda a: a.rearrange("p g (j c) -> p g j c", c=16)
            bs = lambda a: a.unsqueeze(2).to_broadcast([128, 16, 8, 16])
            bl = lambda a: a.unsqueeze(3).to_broadcast([128, 16, 8, 16])
            tt(v(Tre[:]), bs(sre), bl(lr8), ALU.mult)
            tt(v(ctmp[:]), bs(sim), bl(li8), ALU.mult)
            tt(Tre[:], Tre[:], ctmp[:], ALU.subtract)
            tt(v(Tim[:]), bs(sim), bl(lr8), ALU.mult)
            tt(v(ctmp[:]), bs(sre), bl(li8), ALU.mult)
            tt(Tim[:], Tim[:], ctmp[:], ALU.add)
            if neg_im:
                ts(Tim[:], Tim[:], -1.0, None, ALU.mult)

        v9 = lambda a: a.rearrange("p g (j c) -> p g j c", c=16)
        bs9 = lambda a: a.unsqueeze(2).to_broadcast([128, 16, 9, 16])
        bl9 = lambda a: a.unsqueeze(3).to_broadcast([128, 16, 9, 16])
        l9r = pr[:, :, IDX(0):IDX(8) + 1]; l9i = pi_[:, :, IDX(0):IDX(8) + 1]
        tt(v9(QC9r[:]), bs9(cre), bl9(l9r), ALU.mult)
        tt(v9(ctmp9[:]), bs9(cim), bl9(l9i), ALU.mult)
        tt(QC9r[:], QC9r[:], ctmp9[:], ALU.subtract)
        tt(v9(QC9i[:]), bs9(cim), bl9(l9r), ALU.mult)
        tt(v9(ctmp9[:]), bs9(cre), bl9(l9i), ALU.mult)
        tt(QC9i[:], QC9i[:], ctmp9[:], ALU.add)
        acopy(WCre[:], QC9r[:, :, 16:144])
        act(WCim[:], QC9i[:, :, 16:144], AF.Copy, scale=-1.0)
        QCre = QC9r[:, :, 0:128]; QCim = QC9i[:, :, 0:128]
        cmulv(PBre, PBimn, bbr[:], bbi[:], prN[:], piN[:], neg_im=True)
        cmulv(P7re, P7im, bbr[:], bbi[:], pr7[:], pi7[:])
        BKp = [pA, pB, pC, pLP, pG, pDT, pW, pX]
        for gb in range(8):
            bkE = BKp[gb % 4]; bkT = BKp[4 + gb % 4]
            for gi in range(4):
                g = gb * 4 + gi
                hs = HS[g % 2]; pg = g // 2
                tr(bkE[:, gi * 64:(gi + 1) * 64], P7re[hs, pg, :], ident[hs, hs])
                tr(bkE[:, 256 + gi * 64:256 + (gi + 1) * 64], P7im[hs, pg, :], ident[hs, hs])
                mm(bkT[:, gi * 128:(gi + 1) * 128], PBre[hs, pg, :], QCre[hs, pg, :], True, False)
                mm(bkT[:, gi * 128:(gi + 1) * 128], PBimn[hs, pg, :], QCim[hs, pg, :], False, True)
            gs4 = slice(gb * 4, gb * 4 + 4)
            acopy(WEre[:, gs4, :], bkE[:, 0:256].rearrange("p (g n) -> p g n", n=64))
            acopy(WEim[:, gs4, :], bkE[:, 256:512].rearrange("p (g n) -> p g n", n=64))
            tt(Toep[:, gs4, :], bkT.rearrange("p (g n) -> p g n", n=128), maskT.unsqueeze(1).to_broadcast([128, 4, 128]), ALU.mult)
        S.emit()
        PES[0].close()
        PES[0] = _phaseA
        L8r, L8i = pr[:, :, IDX(8)], pi_[:, :, IDX(8)]
        L64r, L64i = pr[:, :, IDX(64)], pi_[:, :, IDX(64)]
        L16r = sb([128, 16], F32, "L16r"); L16i = sb([128, 16], F32, "L16i")
        L32r = sb([128, 16], F32, "L32r"); L32i = sb([128, 16], F32, "L32i")
        lsq = sb([128, 16], F32, "lsq"); lsq2 = sb([128, 16], F32, "lsq2")
        for (sr_, si_, dr_, di_) in ((L8r, L8i, L16r[:], L16i[:]), (L16r[:], L16i[:], L32r[:], L32i[:])):
            tt(lsq[:], sr_, sr_, ALU.mult); tt(lsq2[:], si_, si_, ALU.mult); tt(dr_, lsq[:], lsq2[:], ALU.subtract)
            tt(lsq[:], sr_, si_, ALU.mult); ts(di_, lsq[:], 2.0, None, ALU.mult)
        Lpr = sb([128, 16, 8], F32, "Lpr"); Lpi = sb([128, 16, 8], F32, "Lpi")
        lq1 = sb([128, 16], F32, "lq1"); lq2 = sb([128, 16], F32, "lq2")
        cp(Lpr[:, :, 0], L64r); cp(Lpi[:, :, 0], L64i)
        for c in range(1, 8):
            tt(lq1[:], Lpr[:, :, c - 1], L64r, ALU.mult); tt(lq2[:], Lpi[:, :, c - 1], L64i, ALU.mult)
            tt(Lpr[:, :, c], lq1[:], lq2[:], ALU.subtract)
            tt(lq1[:], Lpr[:, :, c - 1], L64i, ALU.mult); tt(lq2[:], Lpi[:, :, c - 1], L64r, ALU.mult)
            tt(Lpi[:, :, c], lq1[:], lq2[:], ALU.add)

        xn5 = sb([128, 8, 512], BF16, "xn5")
        X8 = sb([128, 32, 8, 16], BF16, "X8")
        Ytok8 = sb([128, 8, 512], F32, "Ytok8")[:]
        U8 = sb([128, 32, 64], BF16, "U8")
        stbr = sb([128, 16, 64], BF16, "stbr"); stbi = sb([128, 16, 64], BF16, "stbi")
        du = sb([128, 8, 64], F32, "du")
        E8r = sb([128, 16, 64], F32, "E8r"); E8i = sb([128, 16, 64], F32, "E8i")
        str_ = sb([128, 16, 64], F32, "str_"); sti = sb([128, 16, 64], F32, "sti")
        accr = sb([128, 16, 8], F32, "accr"); acci = sb([128, 16, 8], F32, "acci")
        ca = sb([128, 16, 8], F32, "ca"); cb = sb([128, 16, 8], F32, "cb")
        cc = sb([128, 16, 8], F32, "cc"); cd = sb([128, 16, 8], F32, "cd")
        hsr = sb([128, 16, 8], F32, "hsr"); hsi = sb([128, 16, 8], F32, "hsi")
        cstr = sb([128, 16, 8], F32, "cstr"); csti = sb([128, 16, 8], F32, "csti")
        car = sb([128, 16], F32, "car"); cai = sb([128, 16], F32, "cai"); memset(car[:], 0.0); memset(cai[:], 0.0)
        c1 = sb([128, 16], F32, "c1"); c2 = sb([128, 16], F32, "c2")
        Yim = sb([128, 32, 64], F32, "Yim")
        y5 = Yim[:].rearrange("p g b -> p (g b)").rearrange("p (q t) -> p q t", t=512)
        g1 = sb([128, 512], F32, "g1"); g2t = sb([128, 512], F32, "g2t")
        zb = sb([128, 4, 512], BF16, "zb")
        zf = y5

        def bc(ap16, n):
            return ap16.unsqueeze(2).to_broadcast([128, 16, n])

        def cstep(dr, di, sr, si, lr, li, er, ei, n, scr=None):
            ca, cb, cc, cd = scr
            tt(ca[:, :, 0:n], sr, bc(lr, n), ALU.mult)
            tt(cb[:, :, 0:n], si, bc(li, n), ALU.mult)
            tt(cc[:, :, 0:n], sr, bc(li, n), ALU.mult)
            tt(cd[:, :, 0:n], si, bc(lr, n), ALU.mult)
            tt(ca[:, :, 0:n], ca[:, :, 0:n], cb[:, :, 0:n], ALU.subtract)
            tt(cc[:, :, 0:n], cc[:, :, 0:n], cd[:, :, 0:n], ALU.add)
            tt(dr, ca[:, :, 0:n], er, ALU.add)
            tt(di, cc[:, :, 0:n], ei, ALU.add)

        BK = [pA, pB, pC, pLP, pG, pDT, pW, pX]
        U8s = [U8, sb([128, 32, 64], BF16, "U8b")]
        E8rs = [E8r, sb([128, 16, 64], F32, "E8rb")]; E8is = [E8i, sb([128, 16, 64], F32, "E8ib")]
        cstrs = [cstr, sb([128, 16, 8], F32, "cstrb")]; cstis = [csti, sb([128, 16, 8], F32, "cstib")]
        scrF = (ca, cb, cc, cd)
        scrB = tuple(sb([128, 16, 8], F32, "scrB%d" % i) for i in range(4))
        v8 = lambda t_: t_[:].rearrange("p g (c j) -> p g c j", j=8)
        vh = lambda t_: t_[:].rearrange("p g (ch j) -> p g ch j", j=4)
        a16r = sb([128, 16, 16], F32, "a16r"); a16i = sb([128, 16, 16], F32, "a16i")
        scr16 = tuple(sb([128, 16, 16], F32, "scr16_%d" % i) for i in range(4))
        scr16b = tuple(sb([128, 16, 16], F32, "scr16b_%d" % i) for i in range(4))
        aLor = [sb([128, 16, 8], F32, "aLor%d" % i) for i in range(2)]
        aLoi = [sb([128, 16, 8], F32, "aLoi%d" % i) for i in range(2)]

        def front(ti):
            t0 = ti * 512
            pp = ti % 2
            U8_, E8r_, E8i_ = U8s[pp], E8rs[pp], E8is[pp]
            E8rv, E8iv = v8(E8r_), v8(E8i_)
            st_ = []

            def f_rms(s):
                dma(xt[:], xT_v[:, :, t0 + s * 256:t0 + (s + 1) * 256])
                rmsnorm_tile(xt, 0, 256, C_NMIX, xn5[:, :, s * 256:(s + 1) * 256])

            def f_in(i0):
                bk = BK[i0 % 4]
                for c in range(8):
                    mm(bk[0:64, :], xn5[:, c, i0::8], win_s5[:, c, :], start=(c == 0), stop=(c == 7))
                ecp(i0, X8[0:64, :, i0, :], bk[0:64, :].rearrange("p (g c) -> p g c", c=16))

            def f_tr(gb):
                bk = BK[4 + gb]
                for gi in range(8):
                    g = gb * 8 + gi
                    tr(bk[:, gi * 64:(gi + 1) * 64], X8[0:64, g, :, :].rearrange("p a b -> p (a b)"), identb[0:64, 0:64])
                ecp(gb, U8_[:, gb * 8:(gb + 1) * 8, :], bk.rearrange("p (g b) -> p g b", b=64))

            def f_e8(hf):
                for g in range(hf * 16, hf * 16 + 16):
                    hs = HS[g % 2]; pg = g // 2
                    mm(pC[hs, (pg % 8) * 64:(pg % 8 + 1) * 64], WEre[:, g, :], U8_[:, g, :], True, True)
                    mm(pLP[hs, (pg % 8) * 64:(pg % 8 + 1) * 64], WEim[:, g, :], U8_[:, g, :], True, True)
                o = hf * 8
                cp(E8r_[:, o:o + 8, :], pC[:, :].rearrange("p (g b) -> p g b", b=64))
                acopy(E8i_[:, o:o + 8, :], pLP[:, :].rearrange("p (g b) -> p g b", b=64))

            E8rh, E8ih = vh(E8r_), vh(E8i_)

            def f_p1i():
                cp(a16r[:], E8rh[:, :, :, 0]); cp(a16i[:], E8ih[:, :, :, 0], eng="pool")

            def f_p1(j):
                if j < 4:
                    cstep(a16r[:], a16i[:], a16r[:], a16i[:], L8r, L8i, E8rh[:, :, :, j], E8ih[:, :, :, j], 16, scr=scr16)
                else:
                    lo_r = a16r[:].rearrange("p g (c h) -> p g c h", h=2)[:, :, :, 0]
                    lo_i = a16i[:].rearrange("p g (c h) -> p g c h", h=2)[:, :, :, 0]
                    hi_r = a16r[:].rearrange("p g (c h) -> p g c h", h=2)[:, :, :, 1]
                    hi_i = a16i[:].rearrange("p g (c h) -> p g c h", h=2)[:, :, :, 1]
                    cp(aLor[pp][:], lo_r); cp(aLoi[pp][:], lo_i, eng="pool")
                    cstep(accr[:], acci[:], lo_r, lo_i, L32r[:], L32i[:], hi_r, hi_i, 8, scr=scrF)

            hsb = [(accr, acci), (hsr, hsi)]

            def f_hs(k):
                d, pw_ = ((1, 0), (2, 1), (4, 3))[k]
                (Ar, Ai), (Br, Bi) = hsb[k % 2], hsb[(k + 1) % 2]
                cp(Br[:, :, 0:d], Ar[:, :, 0:d]); cp(Bi[:, :, 0:d], Ai[:, :, 0:d], eng="pool")
                cstep(Br[:, :, d:8], Bi[:, :, d:8], Ar[:, :, 0:8 - d], Ai[:, :, 0:8 - d], Lpr[:, :, pw_], Lpi[:, :, pw_],
                      Ar[:, :, d:8], Ai[:, :, d:8], 8 - d, scr=scrF)

            def f_carry():
                (Ar, Ai), (Br, Bi) = hsb[1], hsb[0]
                ca_, cb_, cc_, cd_ = scrF
                cbr = car[:].unsqueeze(2).to_broadcast([128, 16, 8]); cbi = cai[:].unsqueeze(2).to_broadcast([128, 16, 8])
                tt(ca_[:], cbr, Lpr[:], ALU.mult); tt(cb_[:], cbi, Lpi[:], ALU.mult)
                tt(cc_[:], cbr, Lpi[:], ALU.mult, eng="pool"); tt(cd_[:], cbi, Lpr[:], ALU.mult, eng="pool")
                tt(ca_[:], ca_[:], cb_[:], ALU.subtract); tt(cc_[:], cc_[:], cd_[:], ALU.add, eng="pool")
                tt(Br[:], ca_[:], Ar[:], ALU.add); tt(Bi[:], cc_[:], Ai[:], ALU.add, eng="pool")
                cr_, ci_ = cstrs[pp], cstis[pp]
                cp(cr_[:, :, 0], car[:]); cp(ci_[:, :, 0], cai[:], eng="pool")
                cp(cr_[:, :, 1:8], Br[:, :, 0:7]); cp(ci_[:, :, 1:8], Bi[:, :, 0:7], eng="pool")
                cp(car[:], Br[:, :, 7]); cp(cai[:], Bi[:, :, 7], eng="pool")

            st_ += [lambda s=s: f_rms(s) for s in range(2)]
            st_ += [lambda i0=i0: f_in(i0) for i0 in range(8)]
            st_ += [lambda gb=gb: f_tr(gb) for gb in range(4)]
            st_ += [lambda hf=hf: f_e8(hf) for hf in range(2)]
            st_ += [f_p1i]
            st_ += [lambda j=j: f_p1(j) for j in range(1, 5)]
            st_ += [lambda k=k: f_hs(k) for k in range(3)]
            st_ += [f_carry]
            return st_

        def back(ti):
            t0 = ti * 512
            o0 = t0 - (NT - OWN)
            pp = ti % 2
            U8_, E8r_, E8i_ = U8s[pp], E8rs[pp], E8is[pp]
            E8rv, E8iv = v8(E8r_), v8(E8i_)
            strv, stiv = v8(str_), v8(sti)
            st_ = []

            E8rh, E8ih = vh(E8r_), vh(E8i_)
            strh, stih = vh(str_), vh(sti)
            s5r = str_[:].rearrange("p g (c h j) -> p g c h j", h=2, j=4)
            s5i = sti[:].rearrange("p g (c h j) -> p g c h j", h=2, j=4)

            def b_init():
                cp(s5r[:, :, :, 0, 0], cstrs[pp][:]); cp(s5i[:, :, :, 0, 0], cstis[pp][:], eng="pool")
                cstep(s5r[:, :, :, 1, 0], s5i[:, :, :, 1, 0], cstrs[pp][:], cstis[pp][:], L32r[:], L32i[:],
                      aLor[pp][:], aLoi[pp][:], 8, scr=scrB)

            def b_p2(j):
                cstep(strh[:, :, :, j + 1], stih[:, :, :, j + 1], strh[:, :, :, j], stih[:, :, :, j], L8r, L8i,
                      E8rh[:, :, :, j], E8ih[:, :, :, j], 16, scr=scr16b)

            def b_stb():
                cp(stbr[:], str_[:]); cp(stbi[:], sti[:], eng="pool")

            def b_out(gb):
                bk = BK[gb]
                for gi in range(8):
                    g = gb * 8 + gi
                    hs = HS[g % 2]; pg = g // 2
                    mm(bk[:, gi * 64:(gi + 1) * 64], WCre[hs, pg, :], stbr[hs, pg, :], True, False)
                    mm(bk[:, gi * 64:(gi + 1) * 64], WCim[hs, pg, :], stbi[hs, pg, :], False, False)
                    mm(bk[:, gi * 64:(gi + 1) * 64], Toep[:, g, :], U8_[:, g, :], False, True)
                gs8 = slice(gb * 8, (gb + 1) * 8)
                tt(du[:], U8_[:, gs8, :], dcol[:, gs8].unsqueeze(2).to_broadcast([128, 8, 64]), ALU.mult, eng="pool")
                tt(Yim[:, gs8, :], du[:], bk.rearrange("p (g b) -> p g b", b=64), ALU.add)

            def b_a9(gb):
                bk = BK[gb % 4 + 4]
                for gi in range(4):
                    g = gb * 4 + gi
                    trT(bk[0:64, gi * 128:(gi + 1) * 128], Yim[:, g, :], ident[:])
                ecp(gb, Ytok8[0:64, :, gb * 64:(gb + 1) * 64].rearrange("p j (g c) -> p j g c", c=16),
                    bk[0:64, :].rearrange("p (g j c) -> p j g c", j=8, c=16))

            def b_a10(j0):
                bk = BK[j0 % 4]
                for q in range(4):
                    trT(bk[:, q * 64:(q + 1) * 64], Ytok8[0:64, j0, q * 128:(q + 1) * 128], ident[0:64, 0:64])
                ecp(j0, y5[:, :, j0::8], bk[:, 0:256].rearrange("p (q b) -> p q b", b=64))

            def b_gelu(q):
                tt(g1[:], y5[:, q, :], y5[:, q, :], ALU.mult)
                ts(g1[:], g1[:], 0.044715, 1.0, ALU.mult, ALU.add)
                tt(g1[:], g1[:], y5[:, q, :], ALU.mult)
                act(g1[:], g1[:], AF.Sigmoid, scale=1.5957691216057308)
                tt(y5[:, q, :], y5[:, q, :], g1[:], ALU.mult)
                cp(zb[:, q, :], zf[:, q, :], eng="pool")

            def b_glu(q):
                for kq in range(4):
                    mm(pX[:, :], glub[:, kq, q * 128:(q + 1) * 128], zb[:, kq, :], start=(kq == 0), stop=(kq == 3))
                act(g2t[:], pX[:, :], AF.Sigmoid, bias=col[:, C_GLUB + q:C_GLUB + q + 1])
                tt(concA[:, q, o0:o0 + 512], zf[:, q, :], g2t[:], ALU.mult)

            st_ += [b_init]
            st_ += [lambda j=j: b_p2(j) for j in range(3)]
            st_ += [b_stb]
            st_ += [lambda gb=gb: b_out(gb) for gb in range(4)]
            st_ += [lambda gb=gb: b_a9(gb) for gb in range(8)]
            st_ += [lambda j0=j0: b_a10(j0) for j0 in range(8)]
            st_ += [lambda q=q: b_gelu(q) for q in range(4)]
            st_ += [lambda q=q: b_glu(q) for q in range(4)]
            return st_

        NT5 = NT // 512
        NA = 14

        def zipped(l1, l2):
            for i_ in range(max(len(l1), len(l2))):
                if i_ < len(l1):
                    l1[i_]()
                if i_ < len(l2):
                    l2[i_]()

        NPRE = (NT - OWN) // 512
        fr = {0: front(0)}
        zipped(fr[0][:NA], [])
        for ti in range(NPRE):
            fr[ti + 1] = front(ti + 1)
            zipped(fr[ti][NA:], fr[ti + 1][:NA])
        zipped(fr[NPRE][NA:], [])
        for ti in range(NPRE, NT5):
            fs_ = front(ti + 1) if ti + 1 < NT5 else []
            zipped(back(ti), fs_)

        if PES[0] is not es:
            S.emit()
            PES[0].close()
        PES[0] = contextlib.ExitStack()
        h = es.enter_context(nc.sbuf_tensor("sb_h", [128, 8, OWN], F32))
        hn = es.enter_context(nc.sbuf_tensor("sb_hn", [128, 8, OWN], BF16))
        stage_ref[0] = [sb([128, 2048], F32, "stageC1a"), sb([128, 2048], F32, "stageC1b")]
        wo = sb([128, 8, 1024], BF16, "wo")
        load_w_bf16(wo, w_out, 1024)
        BK = [pA, pB, pC, pLP, pG, pDT, pW, pX]
        for tq_ in range(OWN // 512):
            dma(h[:, :, tq_ * 512:(tq_ + 1) * 512], xT_v[:, :, NT - OWN + tq_ * 512:NT - OWN + (tq_ + 1) * 512])
        for tq_ in range(OWN // 512):
            ts0 = slice(tq_ * 512, (tq_ + 1) * 512)
            for dc in range(8):
                bk = BK[dc % 4]
                for c in range(8):
                    mm(bk, wo[:, c, dc * 128:(dc + 1) * 128], (concA if c < 4 else concB)[:, c % 4, ts0], start=(c == 0), stop=(c == 7))
                tt(h[:, dc, ts0], h[:, dc, ts0], bk, ALU.add)

        def norm_own(gcol):
            for s in range(OWN // 256):
                sl = slice(s * 256, (s + 1) * 256)
                rmsnorm_tile(h[:, :, sl], 0, 256, gcol, hn[:, :, sl])

        norm_own(C_NFFN)
        S.emit()
        PES[0].close()
        PES[0] = contextlib.ExitStack()
        GS = 2
        NG = NFC // GS
        stage_ref[0] = [sb([128, 2048], F32, "stageC2a"), sb([128, 2048], F32, "stageC2b")]
        w1g = [sb([128, 8, GS * 128], BF16, "w1g%d" % i) for i in range(2)]
        w3g = [sb([128, 8, GS * 128], BF16, "w3g%d" % i) for i in range(2)]
        w2g = [sb([128, GS, 1024], BF16, "w2g%d" % i) for i in range(2)]
        gT = [sb([128, GS, 512], BF16, "gT%d" % i) for i in range(2)]
        s1 = [sb([128, 512], F32, "s1_%d" % i) for i in range(2)]

        stg3 = stage_ref[0] + [sb([128, 2048], F32, "stageC2c")]
        _lg = [0]

        def load_group(gi):
            f0 = gi * GS
            fs = slice(f0 * 128, (f0 + GS) * 128)
            v1 = ffn_w1.rearrange("(c p) n -> p c n", p=128)[:, :, fs]
            v3 = ffn_w3.rearrange("(c p) n -> p c n", p=128)[:, :, fs]
            v2 = ffn_w2[fs, :].rearrange("(f p) n -> p f n", p=128)
            for (dst, srcv, shp) in ((w1g[gi % 2], v1, (8, GS * 128)), (w3g[gi % 2], v3, (8, GS * 128)), (w2g[gi % 2], v2, (GS, 1024))):
                st = stg3[_lg[0] % 3]; _lg[0] += 1
                sv_ = st[:].rearrange("p (a b) -> p a b", a=shp[0])
                dma(sv_, srcv)
                act(dst[:], sv_, AF.Copy)

        wg = concB[:].rearrange("p a (b c) -> p (a b) c", c=1024)
        cAf = concA[:].rearrange("p a n -> p (a n)")
        wu = cAf[:, 0:2048].rearrange("p (c n) -> p c n", n=1024)
        pb = cAf[:, 2048:2048 + 2 * OWN].rearrange("p (c n) -> p c n", n=OWN)
        pT_v = pT.rearrange("(c p) t -> p c t", p=128)
        load_group(0)
        k_ = [0]

        def ffn_A(gi, tq_):
            W1, W3 = w1g[gi % 2], w3g[gi % 2]
            ts0 = slice(tq_ * 512, (tq_ + 1) * 512)
            G = gT[tq_ % 2]
            for f in range(GS):
                bA, bB = (pA, pB) if (k_[0] % 2 == 0) else (pG, pDT)
                for c in range(8):
                    mm(bA, W1[:, c, f * 128:(f + 1) * 128], hn[:, c, ts0], start=(c == 0), stop=(c == 7))
                for c in range(8):
                    mm(bB, W3[:, c, f * 128:(f + 1) * 128], hn[:, c, ts0], start=(c == 0), stop=(c == 7))
                act(s1[k_[0] % 2][:], bA, AF.Silu)
                tt(G[:, f, :], s1[k_[0] % 2][:], bB, ALU.mult)
                k_[0] += 1

        def ffn_B(gi, tq_):
            W2 = w2g[gi % 2]
            ts0 = slice(tq_ * 512, (tq_ + 1) * 512)
            G = gT[tq_ % 2]
            for dc in range(8):
                bk = [pC, pLP, pW, pX][dc % 4]
                for f in range(GS):
                    mm(bk, W2[:, f, dc * 128:(dc + 1) * 128], G[:, f, :], start=(f == 0), stop=(f == GS - 1))
                tt(h[:, dc, ts0], h[:, dc, ts0], bk, ALU.add)

        units = [(gi, tq_) for gi in range(NG) for tq_ in range(OWN // 512)]
        for ui, (gi, tq_) in enumerate(units):
            ffn_A(gi, tq_)
            if ui > 0:
                ffn_B(*units[ui - 1])
            if tq_ == 0 and gi + 1 < NG:
                load_group(gi + 1)
            if ui == 6:
                stage_ref[0] = stg3
                load_w_bf16(wg, gate_w, 1024)
                load_w_bf16(wu, up_w, 1024)
                for c in range(2):
                    load_cast(pb[:, c, :], pT_v[:, c, :], OWN)
        ffn_B(*units[-1])
        S.emit()
        PES[0].close()
        PES[0] = contextlib.ExitStack()
        norm_own(C_NPLE)
        s1 = [sb([128, 512], F32, "s1p_%d" % i) for i in range(2)]
        for tq_ in range(OWN // 512):
            ts0 = slice(tq_ * 512, (tq_ + 1) * 512)
            for dc in range(8):
                bA, bB = (pA, pB) if (dc % 2 == 0) else (pG, pDT)
                for c in range(8):
                    mm(bA, wg[:, c, dc * 128:(dc + 1) * 128], hn[:, c, ts0], start=(c == 0), stop=(c == 7))
                for c in range(2):
                    mm(bB, wu[:, c, dc * 128:(dc + 1) * 128], pb[:, c, ts0], start=(c == 0), stop=(c == 1))
                act(s1[dc % 2][:], bA, AF.Sigmoid)
                tt(s1[dc % 2][:], s1[dc % 2][:], bB, ALU.mult)
                tt(h[:, dc, ts0], h[:, dc, ts0], s1[dc % 2][:], ALU.add, eng="pool")
        outv = outT.rearrange("(c p) t -> p c t", p=128)
        obs = [sb([128, 8, 256], F32, "ob%d" % i) for i in range(2)]
        for s in range(OWN // 256):
            sl = slice(s * 256, (s + 1) * 256)
            rmsnorm_tile(h[:, :, sl], 0, 256, C_NFIN, obs[s % 2])
            dma(outv[:, :, sl], obs[s % 2][:])
        S.emit()
        PES[0].close()
    return nc


def _cols(a, n):
    return np.ascontiguousarray(np.asarray(a, np.float32).reshape(n, 128).T)


def kernel(**inp):
    f = lambda k: np.asarray(inp[k], np.float32)
    x = f("x"); p = f("p")[0]
    col = np.zeros((128, 80), np.float32)
    col[:, 0:8] = _cols(f("norm_mix")[0], 8); col[:, 8:16] = _cols(f("norm_ffn")[0], 8)
    col[:, 16:24] = _cols(f("norm_ple")[0], 8); col[:, 24:32] = _cols(f("final_norm"), 8)
    col[:, 32:46] = _cols(f("rw_shift_mu")[0], 14)
    col[:, 46:50] = _cols(f("rw_k_k")[0], 4); col[:, 50:54] = _cols(f("rw_k_a")[0], 4)
    col[:, 54:58] = _cols(f("rw_r_k")[0].reshape(512), 4); col[:, 58:62] = _cols(f("rw_ln_w")[0], 4)
    col[:, 62:66] = _cols(f("rw_ln_b")[0], 4); col[:, 66:70] = _cols(f("rw_a0")[0], 4)
    col[:, 70:74] = _cols(f("s5_glu_b")[0], 4)
    ident = np.eye(128, dtype=np.float32)
    cst = np.zeros((128, 1024), np.float32)
    cst[:, 0:128] = 1.0
    pi = np.arange(128)
    same = (pi[:, None] // 64) == (pi[None, :] // 64)
    cst[:, 128:256] = same
    cst[:, 256:384] = same & (pi[:, None] <= pi[None, :])
    cst[:, 384:512] = same & (pi[:, None] < pi[None, :])
    s = (pi % 64)[:, None]; t = np.arange(64)[None, :]
    lt = (s < t).astype(np.float32); le = (s <= t).astype(np.float32)
    cst[:, 512:576] = lt; cst[:, 576:640] = le; cst[:, 640:704] = lt; cst[:, 704:768] = le
    cst[:, 768:832] = (t < s).astype(np.float32)
    cst[:, 832:896] = (s == t).astype(np.float32)
    c2 = np.zeros((128, 1536), np.float32)
    nmask = (t < s).astype(np.float32)
    for c in range(4):
        c2[:, c * 128:c * 128 + 64] = lt; c2[:, c * 128 + 64:c * 128 + 128] = le
        c2[:, 512 + c * 64:512 + (c + 1) * 64] = nmask
        c2[:, 768 + c * 64:768 + (c + 1) * 64] = (s == t)
    w2aug = np.concatenate([f("rw_w2")[0], f("rw_w0")[0][None, :]], 0)
    a2 = np.zeros((128, 512), np.float32); a2[64:128] = f("rw_a2")[0]
    g2 = f("rw_g2")[0]
    def lb(a):
        return np.ascontiguousarray(a.reshape(16, 2, 64).transpose(1, 2, 0).reshape(128, 16))
    def lb3(a):
        return np.ascontiguousarray(a.reshape(16, 2, 64, 16).transpose(1, 2, 0, 3).reshape(128, 256))
    s5B = np.concatenate([
        lb(f("s5_lam_re")[0]), lb(f("s5_lam_im")[0]), lb(np.repeat(f("s5_log_step")[0][:, None], 64, 1)),
        lb3(f("s5_c_re")[0].transpose(0, 2, 1)), lb3(f("s5_c_im")[0].transpose(0, 2, 1)),
        lb3(f("s5_b_re")[0]), lb3(f("s5_b_im")[0])], 1).astype(np.float32)
    s5c = np.zeros((128, 177), np.float32)
    s5c[:, 0:17] = np.array(list(range(-7, 9)) + [64], np.float32)[None, :]
    s5c[:, 17:49] = np.tile(f("s5_d")[0].reshape(32, 16).T, (8, 1))
    s5c[:, 49:177] = ((pi[None, :] // 16) >= (pi[:, None] // 16)).astype(np.float32)
    common = {
        "w_in": f("w_in")[0], "w_out": f("w_out")[0], "ffn_w1": f("ffn_w1")[0], "ffn_w3": f("ffn_w3")[0],
        "ffn_w2": f("ffn_w2")[0], "gate_w": f("ple_gate_w")[0], "up_w": f("ple_up_w")[0], "glu_w": f("s5_glu_w")[0],
        "cols": col, "ident": ident, "consts": cst, "w2aug": w2aug, "a2": a2, "g2": g2, "c2": c2, "s5B": s5B, "s5c": s5c,
    }
    in_maps = []
    for core in range(8):
        b, hf = core // 2, core % 2
        xw = np.zeros((D, NT), np.float32)
        if hf == 0:
            xw[:, OWN:] = x[b, 0:OWN].T
        else:
            xw[:, :] = x[b].T
        m = dict(common)
        m["xT"] = xw
        m["pT"] = np.ascontiguousarray(p[b, hf * OWN:(hf + 1) * OWN].T)
        in_maps.append(m)
    nc = build()
    res = run_bass_kernel_spmd(nc, in_maps, core_ids=list(range(8)))
    out = np.zeros((4, 4096, D), np.float32)
    for core in range(8):
        b, hf = core // 2, core % 2
        out[b, hf * OWN:(hf + 1) * OWN] = res.results[core]["outT"].T
    return out
```
